# Optimizing a Trainium2 kernel written in Bass

```python
import math
import jax, jax.numpy as jnp
from jax import lax
import numpy as np

D_MODEL = 1024
BATCH = 1
SEQ = 16384
DEPTH = 4

MIX_WIDTH = D_MODEL
ATTN_WIDTH = MIX_WIDTH // 2
POOL_WIDTH = MIX_WIDTH - ATTN_WIDTH
DIFF_HEAD_DIM = 64
N_DIFF_HEADS = ATTN_WIDTH // (2 * DIFF_HEAD_DIM)
N_POOL_GROUPS = 4
POOL_GROUP_WIDTH = POOL_WIDTH // N_POOL_GROUPS
POOL_WINDOWS = (2, 4, 8, 16)
D_FF = ((8 * D_MODEL // 3 + 255) // 256) * 256
IN_WIDTH = 3 * ATTN_WIDTH + POOL_WIDTH
ROPE_THETA = 10000.0
Q_BLOCK = 128
NORM_EPS = 1e-6

kernel_name = "hymba_diffattn_pool_macaron"


def rms_norm(x, g):
    xf = x.astype(jnp.float32)
    y = xf * lax.rsqrt(jnp.mean(xf * xf, axis=-1, keepdims=True) + NORM_EPS)
    return (y * g.astype(jnp.float32)).astype(x.dtype)


def swiglu(h, w_gate, w_up, w_down):
    return (jax.nn.silu(h @ w_gate) * (h @ w_up)) @ w_down


def rope_tables(seq, dim):
    inv_freq = ROPE_THETA ** (-jnp.arange(0, dim, 2, dtype=jnp.float32) / dim)
    ang = jnp.arange(seq, dtype=jnp.float32)[:, None] * inv_freq[None, :]
    ang = jnp.concatenate([ang, ang], axis=-1)
    return jnp.cos(ang), jnp.sin(ang)


def apply_rope(t, cos, sin):
    half = t.shape[-1] // 2
    t1, t2 = t[..., :half], t[..., half:]
    rot = jnp.concatenate([-t2, t1], axis=-1)
    c = cos[None, :, None, None, :]
    s = sin[None, :, None, None, :]
    return (t.astype(jnp.float32) * c + rot.astype(jnp.float32) * s).astype(t.dtype)


def diff_attention(q, k, v, lam):
    B, S, H, _, d = q.shape
    nb = S // Q_BLOCK
    scale = d ** -0.5
    kt = jnp.transpose(k, (0, 2, 3, 1, 4))
    vt = jnp.transpose(v, (0, 2, 1, 3))
    qb = (q * scale).reshape(B, nb, Q_BLOCK, H, 2, d).transpose(1, 0, 3, 4, 2, 5)
    key_pos = jnp.arange(S, dtype=jnp.int32)
    starts = jnp.arange(nb, dtype=jnp.int32) * Q_BLOCK

    def block(args):
        q_blk, start = args
        s = jnp.einsum('bhcqd,bhckd->bhcqk', q_blk, kt).astype(jnp.float32)
        q_pos = start + jnp.arange(Q_BLOCK, dtype=jnp.int32)
        mask = key_pos[None, :] <= q_pos[:, None]
        s = jnp.where(mask, s, -jnp.inf)
        p = jax.nn.softmax(s, axis=-1)
        a = p[:, :, 0] - lam * p[:, :, 1]
        return jnp.einsum('bhqk,bhke->bhqe', a.astype(v.dtype), vt)

    out = lax.map(block, (qb, starts))
    return out.transpose(1, 0, 3, 2, 4).reshape(B, S, H, 2 * d)


def pool_mixer(u, w, scale):
    B, S, _ = u.shape
    ug = u.reshape(B, S, N_POOL_GROUPS, POOL_GROUP_WIDTH).astype(jnp.float32)
    cs = jnp.cumsum(ug, axis=1)
    cs = jnp.concatenate([jnp.zeros_like(cs[:, :1]), cs], axis=1)
    pos = jnp.arange(S, dtype=jnp.int32)[:, None]
    win = jnp.array(POOL_WINDOWS, dtype=jnp.int32)[None, :]
    lo = jnp.maximum(pos + 1 - win, 0)
    grp = jnp.arange(N_POOL_GROUPS, dtype=jnp.int32)[None, :]
    window_sum = cs[:, 1:] - cs[:, lo, grp]
    count = jnp.minimum(pos + 1, win).astype(jnp.float32)
    diff = (window_sum / count[None, :, :, None] - ug).astype(u.dtype)
    y = jnp.einsum('bsgc,gce->bsge', diff, w).reshape(B, S, POOL_WIDTH)
    return y * scale


def setup_inputs(seed: int = 0) -> dict:
    key = jax.random.key(seed)
    ks = jax.random.split(key, 24)
    f32 = jnp.float32

    def normal(k, shape, s):
        return jax.random.normal(k, shape, dtype=f32) * s

    def gain(k, shape):
        return 1.0 + normal(k, shape, 0.02)

    return {
        "x": normal(ks[0], (BATCH, SEQ, D_MODEL), 1.0),
        "ffn1_norm": gain(ks[1], (DEPTH, D_MODEL)),
        "ffn1_w_gate": normal(ks[2], (DEPTH, D_MODEL, D_FF), D_MODEL ** -0.5),
        "ffn1_w_up": normal(ks[3], (DEPTH, D_MODEL, D_FF), D_MODEL ** -0.5),
        "ffn1_w_down": normal(ks[4], (DEPTH, D_FF, D_MODEL), D_FF ** -0.5),
        "mix_norm": gain(ks[5], (DEPTH, D_MODEL)),
        "w_in": normal(ks[6], (DEPTH, D_MODEL, IN_WIDTH), D_MODEL ** -0.5),
        "lambda_q1": normal(ks[7], (DEPTH, DIFF_HEAD_DIM), 0.1),
        "lambda_k1": normal(ks[8], (DEPTH, DIFF_HEAD_DIM), 0.1),
        "lambda_q2": normal(ks[9], (DEPTH, DIFF_HEAD_DIM), 0.1),
        "lambda_k2": normal(ks[10], (DEPTH, DIFF_HEAD_DIM), 0.1),
        "subln_gain": gain(ks[11], (DEPTH, 2 * DIFF_HEAD_DIM)),
        "pool_w": normal(ks[12], (DEPTH, N_POOL_GROUPS, POOL_GROUP_WIDTH, POOL_GROUP_WIDTH), POOL_GROUP_WIDTH ** -0.5),
        "pool_scale": gain(ks[13], (DEPTH, POOL_WIDTH)),
        "w_out": normal(ks[14], (DEPTH, MIX_WIDTH, D_MODEL), MIX_WIDTH ** -0.5),
        "ffn2_norm": gain(ks[15], (DEPTH, D_MODEL)),
        "ffn2_w_gate": normal(ks[16], (DEPTH, D_MODEL, D_FF), D_MODEL ** -0.5),
        "ffn2_w_up": normal(ks[17], (DEPTH, D_MODEL, D_FF), D_MODEL ** -0.5),
        "ffn2_w_down": normal(ks[18], (DEPTH, D_FF, D_MODEL), D_FF ** -0.5),
        "final_norm": gain(ks[19], (D_MODEL,)),
    }


def reference(x, ffn1_norm, ffn1_w_gate, ffn1_w_up, ffn1_w_down, mix_norm, w_in,
              lambda_q1, lambda_k1, lambda_q2, lambda_k2, subln_gain, pool_w, pool_scale,
              w_out, ffn2_norm, ffn2_w_gate, ffn2_w_up, ffn2_w_down, final_norm):
    B, S, _ = x.shape
    H, d = N_DIFF_HEADS, DIFF_HEAD_DIM
    cos, sin = rope_tables(S, d)

    for l in range(DEPTH):
        h = rms_norm(x, ffn1_norm[l])
        x = x + 0.5 * swiglu(h, ffn1_w_gate[l], ffn1_w_up[l], ffn1_w_down[l])

        h = rms_norm(x, mix_norm[l])
        proj = h @ w_in[l]
        q = proj[..., :ATTN_WIDTH].reshape(B, S, H, 2, d)
        k = proj[..., ATTN_WIDTH:2 * ATTN_WIDTH].reshape(B, S, H, 2, d)
        v = proj[..., 2 * ATTN_WIDTH:3 * ATTN_WIDTH].reshape(B, S, H, 2 * d)
        u = proj[..., 3 * ATTN_WIDTH:]

        q = apply_rope(q, cos, sin)
        k = apply_rope(k, cos, sin)
        lam_init = 0.8 - 0.6 * math.exp(-0.3 * l)
        lam = (jnp.exp(jnp.sum(lambda_q1[l].astype(jnp.float32) * lambda_k1[l].astype(jnp.float32)))
               - jnp.exp(jnp.sum(lambda_q2[l].astype(jnp.float32) * lambda_k2[l].astype(jnp.float32)))
               + lam_init)
        o = diff_attention(q, k, v, lam)
        o = rms_norm(o, subln_gain[l]) * (1.0 - lam_init)
        o = o.reshape(B, S, ATTN_WIDTH)

        p = pool_mixer(u, pool_w[l], pool_scale[l])

        x = x + jnp.concatenate([o, p], axis=-1) @ w_out[l]

        h = rms_norm(x, ffn2_norm[l])
        x = x + 0.5 * swiglu(h, ffn2_w_gate[l], ffn2_w_up[l], ffn2_w_down[l])

    return rms_norm(x, final_norm)
```

```python
import math
from contextlib import ExitStack

import numpy as np
import ml_dtypes

import concourse.bass as bass
import concourse.mybir as mybir
from concourse.bass_utils import run_bass_kernel_spmd

F32 = mybir.dt.float32
BF16 = mybir.dt.bfloat16
AF = mybir.ActivationFunctionType
ALU = mybir.AluOpType

NCORES = 8
D = 1024
S = 16384
DEPTH = 4
DFF = 2816
NFC = DFF // 128
T = S // NCORES
HALF = 1024
NB = 2
EPS = 1e-6
SAME_ENGINE_SYNC = True


class Sched:
    ENGS = ("pe", "act", "dve", "pool", "sp")

    def __init__(self):
        self.ops = []
        self.last_writer = {}
        self.readers = {}
        self.dma_count = {}

    def op(self, eng, fn, reads=(), writes=(), dma=None):
        idx = len(self.ops)
        deps = set()
        for r in reads:
            if r in self.last_writer:
                deps.add(self.last_writer[r])
        for w in writes:
            if w in self.last_writer:
                deps.add(self.last_writer[w])
            for rd in self.readers.get(w, ()):
                deps.add(rd)
        o = dict(eng=eng, fn=fn, deps=deps, dma=dma, signal=False, sig_idx=None,
                 dma_val=None)
        if dma is not None:
            self.dma_count[dma] = self.dma_count.get(dma, 0) + 1
            o["dma_val"] = 16 * self.dma_count[dma]
        self.ops.append(o)
        for w in writes:
            self.last_writer[w] = idx
            self.readers[w] = []
        for r in reads:
            self.readers.setdefault(r, []).append(idx)
        return idx

    def finalize(self):
        ops = self.ops
        for o in ops:
            for d in o["deps"]:
                od = ops[d]
                if od["dma"] is None:
                    if od["eng"] != o["eng"] or (SAME_ENGINE_SYNC and od["eng"] != "pe"
                                                 and o["dma"] is None):
                        od["signal"] = True
                    elif o["dma"] is not None and od["eng"] == o["eng"]:
                        od["signal"] = True
        cnt = {e: 0 for e in self.ENGS}
        for o in ops:
            if o["signal"]:
                cnt[o["eng"]] += 1
                o["sig_idx"] = cnt[o["eng"]]
        waited = {e: {} for e in self.ENGS}
        for o in ops:
            w = {}
            for d in o["deps"]:
                od = ops[d]
                if od["dma"] is not None:
                    key = ("dma", od["dma"])
                    val = od["dma_val"]
                else:
                    if not od["signal"]:
                        continue
                    key = ("tl", od["eng"])
                    val = od["sig_idx"]
                w[key] = max(w.get(key, 0), val)
            wl = []
            for key, val in w.items():
                if waited[o["eng"]].get(key, 0) >= val:
                    continue
                waited[o["eng"]][key] = val
                wl.append((key, val))
            o["waits"] = wl

    def emit(self, nc, stack):
        self.finalize()
        sems = {}
        for e in self.ENGS:
            sems[("tl", e)] = stack.enter_context(nc.semaphore("tl_" + e))
        for k in self.dma_count:
            sems[("dma", k)] = stack.enter_context(nc.semaphore("dma_" + str(k)))
        block = stack.enter_context(nc.Block())
        ops = self.ops
        dma_final = dict(self.dma_count)

        def run(eng_name, e, final=False):
            for o in ops:
                if o["eng"] != eng_name:
                    continue
                for key, val in o["waits"]:
                    e.wait_ge(sems[key], val)
                inst = o["fn"](e)
                if o["dma"] is not None:
                    inst.then_inc(sems[("dma", o["dma"])], 16)
                elif o["signal"]:
                    inst.then_inc(sems[("tl", eng_name)], 1)
            if final:
                for k, n in dma_final.items():
                    e.wait_ge(sems[("dma", k)], 16 * n)

        @block.tensor
        def _(e):
            run("pe", e)

        @block.scalar
        def _(e):
            run("act", e)

        @block.vector
        def _(e):
            run("dve", e)

        @block.gpsimd
        def _(e):
            run("pool", e)

        @block.sync
        def _(e):
            run("sp", e, final=True)


class Ctx:
    def __init__(self):
        self.nc = bass.Bass("TRN2", target_bir_lowering=False)
        self.sc = Sched()
        self.stack = ExitStack()
        self.bank_ctr = 0
        self.rot = {}

    def din(self, name, shape, dt=F32):
        return self.nc.dram_tensor(name, list(shape), dt, kind="ExternalInput").ap()

    def dout(self, name, shape, dt=F32):
        return self.nc.dram_tensor(name, list(shape), dt, kind="ExternalOutput").ap()

    def sb(self, name, shape, dt):
        return self.stack.enter_context(self.nc.sbuf_tensor("s_" + name, list(shape), dt))

    def ps(self, name, shape, dt=F32):
        return self.stack.enter_context(self.nc.psum_tensor("p_" + name, list(shape), dt))

    def bank(self):
        b = self.bank_ctr % 8
        self.bank_ctr += 1
        return b

    def slot(self, name, n):
        v = self.rot.get(name, 0)
        self.rot[name] = v + 1
        return v % n

    def finish(self):
        self.sc.emit(self.nc, self.stack)
        self.stack.close()
        return self.nc


def alloc_common(cx):
    t = {}
    t["x"] = cx.sb("x", [128, 8, HALF], F32)
    t["h"] = cx.sb("h", [128, 8, HALF], BF16)
    t["act"] = cx.sb("act", [128, NFC, HALF], BF16)
    t["wa"] = [cx.sb(f"wa{i}", [128, 8, 256], BF16) for i in range(3)]
    t["wb"] = [cx.sb(f"wb{i}", [128, 8, 256], BF16) for i in range(3)]
    t["wd"] = [cx.sb(f"wd{i}", [128, 4, 512], BF16) for i in range(3)]
    t["sq"] = [cx.sb(f"sq{i}", [128, 512], BF16) for i in range(4)]
    t["std"] = cx.sb("std", [128, 512], F32)
    t["rstd"] = [cx.sb(f"rstd{i}", [128, 512], F32) for i in range(2)]
    t["tmpf"] = [cx.sb(f"tmpf{i}", [128, 512], F32) for i in range(4)]
    t["ones"] = cx.sb("ones", [128, 128], BF16)
    t["psum"] = cx.ps("psum", [128, 8, 512], F32)
    cx.sc.op("pool", lambda e: e.memset(t["ones"][:], 1.0), writes=[("ones",)])
    return t


def load_vec(cx, t, name, dram_ap, ncol):
    t[name] = cx.sb(name, [128, ncol], F32)
    cx.sc.op("sp", lambda e: e.dma_start(out=t[name][:], in_=dram_ap),
             writes=[(name,)], dma=name)


def rmsnorm_to_h(cx, t, gname, nchunk=8, src="x", dst="h", scale_d=D):
    sc = cx.sc
    ps = t["psum"]
    for b in range(NB):
        bs = slice(b * 512, (b + 1) * 512)
        bk = cx.bank()
        for c in range(nchunk):
            s = cx.slot("sq", 4)
            sc.op("act", lambda e, c=c, s=s, bs=bs: e.activation(
                out=t["sq"][s][:], in_=t[src][:, c, bs], func=AF.Square),
                reads=[(src, c, b)], writes=[("sq", s)])
            sc.op("pe", lambda e, c=c, s=s, bk=bk: e.matmul(
                ps[:, bk, :], lhsT=t["ones"][:], rhs=t["sq"][s][:],
                start=(c == 0), stop=(c == nchunk - 1)),
                reads=[("sq", s), ("ones",)], writes=[("ps", bk)])
        sc.op("act", lambda e, bk=bk: e.activation(
            out=t["std"][:], in_=ps[:, bk, :], func=AF.Sqrt, bias=t["eps"][:, 0:1],
            scale=1.0 / scale_d),
            reads=[("ps", bk), ("eps",)], writes=[("std",)])
        sc.op("dve", lambda e, b=b: e.reciprocal(out=t["rstd"][b][:], in_=t["std"][:]),
              reads=[("std",)], writes=[("rstd", b)])
        for c in range(nchunk):
            sc.op("dve", lambda e, c=c, b=b, bs=bs: e.scalar_tensor_tensor(
                out=t[dst][:, c, bs], in0=t[src][:, c, bs], scalar=t[gname][:, c:c + 1],
                in1=t["rstd"][b][:], op0=ALU.mult, op1=ALU.mult),
                reads=[(src, c, b), (gname,), ("rstd", b)], writes=[(dst, c, b)])


def ffn(cx, t, gname, wg, wu, wd):
    sc = cx.sc
    ps = t["psum"]
    rmsnorm_to_h(cx, t, gname)
    wgv = wg.rearrange("(c p) f -> p c f", p=128)
    wuv = wu.rearrange("(c p) f -> p c f", p=128)
    nslab = NFC // 2
    for s in range(nslab):
        sa = cx.slot("wa", 3)
        sb_ = cx.slot("wb", 3)
        sc.op("pool", lambda e, s=s, sa=sa: e.dma_start(
            out=t["wa"][sa][:], in_=wgv[:, :, 256 * s:256 * s + 256]),
            writes=[("wa", sa)], dma=f"wa{sa}")
        sc.op("pool", lambda e, s=s, sb_=sb_: e.dma_start(
            out=t["wb"][sb_][:], in_=wuv[:, :, 256 * s:256 * s + 256]),
            writes=[("wb", sb_)], dma=f"wb{sb_}")
        for f2 in range(2):
            fc = 2 * s + f2
            for b in range(NB):
                bs = slice(b * 512, (b + 1) * 512)
                bg = cx.bank()
                bu = cx.bank()
                for c in range(8):
                    sc.op("pe", lambda e, c=c, sa=sa, f2=f2, bg=bg, bs=bs: e.matmul(
                        ps[:, bg, :], lhsT=t["wa"][sa][:, c, 128 * f2:128 * f2 + 128],
                        rhs=t["h"][:, c, bs], start=(c == 0), stop=(c == 7)),
                        reads=[("wa", sa), ("h", c, b)], writes=[("ps", bg)])
                for c in range(8):
                    sc.op("pe", lambda e, c=c, sb_=sb_, f2=f2, bu=bu, bs=bs: e.matmul(
                        ps[:, bu, :], lhsT=t["wb"][sb_][:, c, 128 * f2:128 * f2 + 128],
                        rhs=t["h"][:, c, bs], start=(c == 0), stop=(c == 7)),
                        reads=[("wb", sb_), ("h", c, b)], writes=[("ps", bu)])
                ts = cx.slot("tmpf", 4)
                sc.op("act", lambda e, bg=bg, ts=ts: e.activation(
                    out=t["tmpf"][ts][:], in_=ps[:, bg, :], func=AF.Silu),
                    reads=[("ps", bg)], writes=[("tmpf", ts)])
                sc.op("dve", lambda e, bu=bu, ts=ts, fc=fc, bs=bs: e.tensor_tensor(
                    out=t["act"][:, fc, bs], in0=ps[:, bu, :], in1=t["tmpf"][ts][:],
                    op=ALU.mult),
                    reads=[("ps", bu), ("tmpf", ts)], writes=[("act", fc, b)])
    wdv = wd.rearrange("(s p) d -> p s d", p=128)
    for p_ in range(2):
        banks = [[cx.bank() for b in range(NB)] for jj in range(4)]
        nsl = (NFC + 3) // 4
        for s in range(nsl):
            nch = min(4, NFC - 4 * s)
            sd = cx.slot("wd", 3)
            sc.op("pool", lambda e, s=s, sd=sd, nch=nch, p_=p_: e.dma_start(
                out=t["wd"][sd][:, 0:nch, :],
                in_=wdv[:, 4 * s:4 * s + nch, 512 * p_:512 * p_ + 512]),
                writes=[("wd", sd)], dma=f"wd{sd}")
            for f4 in range(nch):
                fc = 4 * s + f4
                for jj in range(4):
                    for b in range(NB):
                        bs = slice(b * 512, (b + 1) * 512)
                        bk = banks[jj][b]
                        sc.op("pe", lambda e, sd=sd, f4=f4, jj=jj, bk=bk, fc=fc, bs=bs: e.matmul(
                            ps[:, bk, :], lhsT=t["wd"][sd][:, f4, 128 * jj:128 * jj + 128],
                            rhs=t["act"][:, fc, bs], start=(fc == 0), stop=(fc == NFC - 1)),
                            reads=[("wd", sd), ("act", fc, b)], writes=[("ps", bk)])
        for jj in range(4):
            j = 4 * p_ + jj
            for b in range(NB):
                bs = slice(b * 512, (b + 1) * 512)
                bk = banks[jj][b]
                sc.op("dve", lambda e, j=j, bk=bk, bs=bs: e.scalar_tensor_tensor(
                    out=t["x"][:, j, bs], in0=ps[:, bk, :], scalar=0.5,
                    in1=t["x"][:, j, bs], op0=ALU.mult, op1=ALU.add),
                    reads=[("ps", bk), ("x", j, b)], writes=[("x", j, b)])


def load_x(cx, t, x_dram, hf):
    xv = x_dram.rearrange("(c p) n -> p c n", p=128)
    cx.sc.op("sp", lambda e: e.dma_start(out=t["x"][:], in_=xv[:, :, hf * HALF:(hf + 1) * HALF]),
             writes=[("x", c, b) for c in range(8) for b in range(NB)], dma="xin")


def store_x(cx, t, x_dram, hf):
    xv = x_dram.rearrange("(c p) n -> p c n", p=128)
    cx.sc.op("sp", lambda e: e.dma_start(out=xv[:, :, hf * HALF:(hf + 1) * HALF], in_=t["x"][:]),
             reads=[("x", c, b) for c in range(8) for b in range(NB)], dma="xout")


def build_A():
    cx = Ctx()
    sc = cx.sc
    x_in = cx.din("x_in", [D, T])
    g1 = cx.din("g1", [128, 8])
    g2 = cx.din("g2", [128, 8])
    epsd = cx.din("epsd", [128, 1])
    wg = cx.din("wg", [D, DFF])
    wu = cx.din("wu", [D, DFF])
    wd = cx.din("wd", [DFF, D])
    win = cx.din("win", [D, 2048])
    winp = cx.din("winp", [D, 1024])
    cosd = cx.din("cosd", [128, T])
    sind = cx.din("sind", [128, T])
    x_out = cx.dout("x_out", [D, T])
    q_out = cx.dout("q_out", [512, T], BF16)
    k_out = cx.dout("k_out", [512, T], BF16)
    v_out = cx.dout("v_out", [T, 512], BF16)
    u_out = cx.dout("u_out", [512, T])

    t = alloc_common(cx)
    load_vec(cx, t, "g1", g1, 8)
    load_vec(cx, t, "g2", g2, 8)
    load_vec(cx, t, "eps", epsd, 1)
    t["cos"] = cx.sb("cos", [128, HALF], F32)
    t["sin"] = cx.sb("sin", [128, HALF], F32)
    t["stg16"] = [cx.sb(f"stg16_{i}", [128, 512], BF16) for i in range(4)]
    t["stg32"] = [cx.sb(f"stg32_{i}", [128, 512], F32) for i in range(2)]
    ps = t["psum"]
    winv = win.rearrange("(c p) f -> p c f", p=128)
    winpv = winp.rearrange("(c p) f -> p c f", p=128)

    for hf in range(2):
        load_x(cx, t, x_in, hf)
        ffn(cx, t, "g1", wg, wu, wd)
        store_x(cx, t, x_out, hf)
        rmsnorm_to_h(cx, t, "g2")
        sc.op("sp", lambda e, hf=hf: e.dma_start(out=t["cos"][:], in_=cosd[:, hf * HALF:(hf + 1) * HALF]),
              writes=[("cos",)], dma="cos")
        sc.op("sp", lambda e, hf=hf: e.dma_start(out=t["sin"][:], in_=sind[:, hf * HALF:(hf + 1) * HALF]),
              writes=[("sin",)], dma="sin")
        for s in range(4):
            sa = cx.slot("wa", 3)
            sb_ = cx.slot("wb", 3)
            sc.op("pool", lambda e, s=s, sa=sa: e.dma_start(
                out=t["wa"][sa][:], in_=winv[:, :, 256 * s:256 * s + 256]),
                writes=[("wa", sa)], dma=f"wa{sa}")
            sc.op("pool", lambda e, s=s, sb_=sb_: e.dma_start(
                out=t["wb"][sb_][:], in_=winpv[:, :, 256 * s:256 * s + 256]),
                writes=[("wb", sb_)], dma=f"wb{sb_}")
            for f2 in range(2):
                ch = 2 * s + f2
                dst = q_out if ch < 4 else k_out
                row0 = (ch % 4) * 128
                for b in range(NB):
                    bs = slice(b * 512, (b + 1) * 512)
                    b1 = cx.bank()
                    b2 = cx.bank()
                    for c in range(8):
                        sc.op("pe", lambda e, c=c, sa=sa, f2=f2, b1=b1, bs=bs: e.matmul(
                            ps[:, b1, :], lhsT=t["wa"][sa][:, c, 128 * f2:128 * f2 + 128],
                            rhs=t["h"][:, c, bs], start=(c == 0), stop=(c == 7)),
                            reads=[("wa", sa), ("h", c, b)], writes=[("ps", b1)])
                    for c in range(8):
                        sc.op("pe", lambda e, c=c, sb_=sb_, f2=f2, b2=b2, bs=bs: e.matmul(
                            ps[:, b2, :], lhsT=t["wb"][sb_][:, c, 128 * f2:128 * f2 + 128],
                            rhs=t["h"][:, c, bs], start=(c == 0), stop=(c == 7)),
                            reads=[("wb", sb_), ("h", c, b)], writes=[("ps", b2)])
                    s1 = cx.slot("tmpf", 4)
                    s2 = cx.slot("tmpf", 4)
                    so = cx.slot("stg16", 4)
                    sc.op("dve", lambda e, b1=b1, s1=s1, bs=bs: e.tensor_tensor(
                        out=t["tmpf"][s1][:], in0=ps[:, b1, :], in1=t["cos"][:, bs], op=ALU.mult),
                        reads=[("ps", b1), ("cos",)], writes=[("tmpf", s1)])
                    sc.op("dve", lambda e, b2=b2, s2=s2, bs=bs: e.tensor_tensor(
                        out=t["tmpf"][s2][:], in0=ps[:, b2, :], in1=t["sin"][:, bs], op=ALU.mult),
                        reads=[("ps", b2), ("sin",)], writes=[("tmpf", s2)])
                    sc.op("pool", lambda e, s1=s1, s2=s2, so=so: e.tensor_tensor(
                        out=t["stg16"][so][:], in0=t["tmpf"][s1][:], in1=t["tmpf"][s2][:], op=ALU.add),
                        reads=[("tmpf", s1), ("tmpf", s2)], writes=[("stg16", so)])
                    c0 = hf * HALF + b * 512
                    sc.op("sp", lambda e, dst=dst, row0=row0, c0=c0, so=so: e.dma_start(
                        out=dst[row0:row0 + 128, c0:c0 + 512], in_=t["stg16"][so][:]),
                        reads=[("stg16", so)], dma=f"stg16_{so}")
        for s in range(2):
            sa = cx.slot("wa", 3)
            sc.op("pool", lambda e, s=s, sa=sa: e.dma_start(
                out=t["wa"][sa][:], in_=winv[:, :, 1536 + 256 * s:1536 + 256 * s + 256]),
                writes=[("wa", sa)], dma=f"wa{sa}")
            for f2 in range(2):
                ch = 2 * s + f2
                for b in range(NB):
                    bs = slice(b * 512, (b + 1) * 512)
                    b1 = cx.bank()
                    for c in range(8):
                        sc.op("pe", lambda e, c=c, sa=sa, f2=f2, b1=b1, bs=bs: e.matmul(
                            ps[:, b1, :], lhsT=t["wa"][sa][:, c, 128 * f2:128 * f2 + 128],
                            rhs=t["h"][:, c, bs], start=(c == 0), stop=(c == 7)),
                            reads=[("wa", sa), ("h", c, b)], writes=[("ps", b1)])
                    so = cx.slot("stg32", 2)
                    sc.op("act", lambda e, b1=b1, so=so: e.activation(
                        out=t["stg32"][so][:], in_=ps[:, b1, :], func=AF.Copy),
                        reads=[("ps", b1)], writes=[("stg32", so)])
                    c0 = hf * HALF + b * 512
                    sc.op("sp", lambda e, ch=ch, c0=c0, so=so: e.dma_start(
                        out=u_out[ch * 128:ch * 128 + 128, c0:c0 + 512], in_=t["stg32"][so][:]),
                        reads=[("stg32", so)], dma=f"stg32_{so}")
        for s in range(2):
            sa = cx.slot("wa", 3)
            sc.op("pool", lambda e, s=s, sa=sa: e.dma_start(
                out=t["wa"][sa][:], in_=winv[:, :, 1024 + 256 * s:1024 + 256 * s + 256]),
                writes=[("wa", sa)], dma=f"wa{sa}")
            for tt in range(8):
                b = tt // 4
                b1 = cx.bank()
                for c in range(8):
                    sc.op("pe", lambda e, c=c, sa=sa, tt=tt, b1=b1: e.matmul(
                        ps[:, b1, 0:256], lhsT=t["h"][:, c, tt * 128:tt * 128 + 128],
                        rhs=t["wa"][sa][:, c, :], start=(c == 0), stop=(c == 7)),
                        reads=[("wa", sa), ("h", c, b)], writes=[("ps", b1)])
                so = cx.slot("stg16", 4)
                sc.op("act", lambda e, b1=b1, so=so: e.activation(
                    out=t["stg16"][so][:, 0:256], in_=ps[:, b1, 0:256], func=AF.Copy),
                    reads=[("ps", b1)], writes=[("stg16", so)])
                r0 = hf * HALF + tt * 128
                sc.op("sp", lambda e, r0=r0, s=s, so=so: e.dma_start(
                    out=v_out[r0:r0 + 128, 256 * s:256 * s + 256], in_=t["stg16"][so][:, 0:256]),
                    reads=[("stg16", so)], dma=f"stg16_{so}")
    return cx.finish()


def build_B():
    cx = Ctx()
    sc = cx.sc
    NQB = S // 512
    NKT = S // 128
    q_in = cx.din("q_in", [64, S], BF16)
    k_in = cx.din("k_in", [64, S], BF16)
    v_in = cx.din("v_in", [128, NKT * 128], BF16)
    m_in = cx.din("m_in", [128, 4 * 512], BF16)
    o_out = cx.dout("o_out", [128, S])

    qT = cx.sb("qT", [64, S], BF16)
    kT = cx.sb("kT", [64, S], BF16)
    vv = cx.sb("vv", [128, NKT * 128], BF16)
    mk = cx.sb("mk", [128, 4 * 512], BF16)
    ones = cx.sb("ones", [128, 128], BF16)
    NP = 6
    pT = [cx.sb(f"pT{i}", [128, 512], BF16) for i in range(NP)]
    rl = cx.sb("rl", [128, 512], F32)
    ostg = [cx.sb(f"ostg{i}", [128, 512], F32) for i in range(2)]
    ps = cx.ps("psum", [128, 8, 512], F32)

    sc.op("pool", lambda e: e.memset(ones[:], 1.0), writes=[("ones",)])
    NCH = 8
    cw = S // NCH
    for i in range(NCH):
        sc.op("sp", lambda e, i=i: e.dma_start(out=qT[:, i * cw:(i + 1) * cw], in_=q_in[:, i * cw:(i + 1) * cw]),
              writes=[("q", i)], dma=f"q{i}")
        sc.op("sp", lambda e, i=i: e.dma_start(out=kT[:, i * cw:(i + 1) * cw], in_=k_in[:, i * cw:(i + 1) * cw]),
              writes=[("k", i)], dma=f"k{i}")
        sc.op("sp", lambda e, i=i: e.dma_start(out=vv[:, i * cw:(i + 1) * cw], in_=v_in[:, i * cw:(i + 1) * cw]),
              writes=[("v", i)], dma=f"v{i}")
    sc.op("sp", lambda e: e.dma_start(out=mk[:], in_=m_in), writes=[("mk",)], dma="mk")

    pairs = [(qb, kt) for qb in range(NQB) for kt in range(4 * (qb + 1))]
    LOOK = 2
    sbank = {}
    pslot = {}

    def emit_S(i):
        qb, kt = pairs[i]
        bk = i % 4
        sbank[i] = bk
        sc.op("pe", lambda e: e.matmul(
            ps[0:128, bk, :], lhsT=kT[:, kt * 128:(kt + 1) * 128], rhs=qT[:, qb * 512:(qb + 1) * 512],
            start=True, stop=True),
            reads=[("k", kt * 128 // cw), ("q", qb * 512 // cw)], writes=[("ps", bk)])
        sl = i % NP
        pslot[i] = sl
        sc.op("act", lambda e: e.activation(out=pT[sl][:], in_=ps[:, bk, :], func=AF.Exp, scale=0.125),
              reads=[("ps", bk)], writes=[("pT", sl)])
        j = kt - 4 * qb
        if j >= 0:
            sc.op("pool", lambda e: e.tensor_tensor(out=pT[sl][:], in0=pT[sl][:],
                                                   in1=mk[:, j * 512:(j + 1) * 512], op=ALU.mult),
                  reads=[("pT", sl), ("mk",)], writes=[("pT", sl)])

    def emit_PV(i):
        qb, kt = pairs[i]
        sl = pslot[i]
        ob = 4 + (qb % 2)
        lb = 6 + (qb % 2)
        last = (kt == 4 * (qb + 1) - 1)
        sc.op("pe", lambda e: e.matmul(ps[:, ob, :], lhsT=vv[:, kt * 128:(kt + 1) * 128], rhs=pT[sl][:],
                                       start=(kt == 0), stop=last),
              reads=[("pT", sl), ("v", kt * 128 // cw)], writes=[("ps", ob)])
        sc.op("pe", lambda e: e.matmul(ps[:, lb, :], lhsT=ones[:], rhs=pT[sl][:],
                                       start=(kt == 0), stop=last),
              reads=[("pT", sl), ("ones",)], writes=[("ps", lb)])
        if last:
            so = qb % 2
            sc.op("dve", lambda e: e.reciprocal(out=rl[:], in_=ps[:, lb, :]),
                  reads=[("ps", lb)], writes=[("rl",)])
            sc.op("dve", lambda e: e.tensor_tensor(out=ostg[so][:], in0=ps[:, ob, :], in1=rl[:], op=ALU.mult),
                  reads=[("ps", ob), ("rl",)], writes=[("ostg", so)])
            sc.op("sp", lambda e: e.dma_start(out=o_out[:, qb * 512:(qb + 1) * 512], in_=ostg[so][:]),
                  reads=[("ostg", so)], dma=f"ostg{so}")

    n = len(pairs)
    for i in range(n + LOOK):
        if i < n:
            emit_S(i)
        if i - LOOK >= 0:
            emit_PV(i - LOOK)
    return cx.finish()


def build_C():
    cx = Ctx()
    sc = cx.sc
    x_in = cx.din("x_in", [D, T])
    o_in = cx.din("o_in", [8 * 128, T])
    u_in = cx.din("u_in", [512, 16 + T])
    icnt = cx.din("icnt", [128, 4 * 16])
    lam4 = cx.din("lam4", [128, 4 * 64])
    lconst = cx.din("lconst", [128, 2])
    sgain = cx.din("sgain", [128, 1])
    pw = cx.din("pw", [4 * 128, 128])
    pscale = cx.din("pscale", [128, 4])
    wout = cx.din("wout", [D, D])
    g3 = cx.din("g3", [128, 8])
    gf = cx.din("gf", [128, 8])
    epsd = cx.din("epsd", [128, 1])
    wg = cx.din("wg", [D, DFF])
    wu = cx.din("wu", [D, DFF])
    wd = cx.din("wd", [DFF, D])
    x_out = cx.dout("x_out", [D, T])
    y_out = cx.dout("y_out", [D, T])

    t = alloc_common(cx)
    ps = t["psum"]
    for nm, ap, n in (("g3", g3, 8), ("gf", gf, 8), ("eps", epsd, 1), ("lam4", lam4, 256),
                      ("lconst", lconst, 2), ("sgain", sgain, 1), ("pscale", pscale, 4),
                      ("icnt", icnt, 64)):
        load_vec(cx, t, nm, ap, n)
    t["lt"] = cx.sb("lt", [128, 128], F32)
    t["ls"] = cx.sb("ls", [128, 8], F32)
    sc.op("dve", lambda e: e.tensor_tensor(out=t["lt"][:, 0:64], in0=t["lam4"][:, 0:64],
                                           in1=t["lam4"][:, 64:128], op=ALU.mult),
          reads=[("lam4",)], writes=[("lt", 0)])
    sc.op("dve", lambda e: e.tensor_tensor(out=t["lt"][:, 64:128], in0=t["lam4"][:, 128:192],
                                           in1=t["lam4"][:, 192:256], op=ALU.mult),
          reads=[("lam4",)], writes=[("lt", 1)])
    sc.op("dve", lambda e: e.tensor_reduce(out=t["ls"][:, 0:1], in_=t["lt"][:, 0:64],
                                           axis=mybir.AxisListType.X, op=ALU.add),
          reads=[("lt", 0)], writes=[("ls", 0)])
    sc.op("dve", lambda e: e.tensor_reduce(out=t["ls"][:, 1:2], in_=t["lt"][:, 64:128],
                                           axis=mybir.AxisListType.X, op=ALU.add),
          reads=[("lt", 1)], writes=[("ls", 1)])
    sc.op("act", lambda e: e.activation(out=t["ls"][:, 2:4], in_=t["ls"][:, 0:2], func=AF.Exp),
          reads=[("ls", 0), ("ls", 1)], writes=[("ls", 2)])
    sc.op("dve", lambda e: e.tensor_tensor(out=t["ls"][:, 4:5], in0=t["ls"][:, 3:4], in1=t["ls"][:, 2:3],
                                           op=ALU.subtract),
          reads=[("ls", 2)], writes=[("ls", 4)])
    sc.op("dve", lambda e: e.tensor_tensor(out=t["ls"][:, 5:6], in0=t["ls"][:, 4:5], in1=t["lconst"][:, 0:1],
                                           op=ALU.subtract),
          reads=[("ls", 4), ("lconst",)], writes=[("neglam",)])
    sc.op("dve", lambda e: e.tensor_tensor(out=t["ls"][:, 6:7], in0=t["sgain"][:, 0:1], in1=t["lconst"][:, 1:2],
                                           op=ALU.mult),
          reads=[("sgain",), ("lconst",)], writes=[("sg",)])

    t["o1"] = [cx.sb(f"o1_{i}", [128, 512], F32) for i in range(2)]
    t["o2"] = [cx.sb(f"o2_{i}", [128, 512], F32) for i in range(2)]
    t["od"] = [cx.sb(f"od_{i}", [128, 512], F32) for i in range(2)]
    t["uu"] = cx.sb("uu", [128, 16 + HALF], F32)
    t["ua"] = cx.sb("ua", [128, 16 + HALF], F32)
    t["ub"] = cx.sb("ub", [128, 16 + HALF], F32)
    t["dif"] = cx.sb("dif", [128, HALF], BF16)
    t["pwt"] = cx.sb("pwt", [128, 4, 128], BF16)
    sc.op("pool", lambda e: e.dma_start(out=t["pwt"][:], in_=pw.rearrange("(g c) e -> c g e", c=128)),
          writes=[("pwt",)], dma="pwt")
    woutv = wout.rearrange("(c p) f -> p c f", p=128)
    WINS = (2, 4, 8, 16)

    for hf in range(2):
        load_x(cx, t, x_in, hf)
        for hd in range(4):
            for b in range(NB):
                bs = slice(b * 512, (b + 1) * 512)
                c0 = hf * HALF + b * 512
                s1 = cx.slot("o1", 2)
                s2 = cx.slot("o2", 2)
                sd_ = cx.slot("od", 2)
                sc.op("sp", lambda e, hd=hd, c0=c0, s1=s1: e.dma_start(
                    out=t["o1"][s1][:], in_=o_in[(2 * hd) * 128:(2 * hd) * 128 + 128, c0:c0 + 512]),
                    writes=[("o1", s1)], dma=f"o1_{s1}")
                sc.op("sp", lambda e, hd=hd, c0=c0, s2=s2: e.dma_start(
                    out=t["o2"][s2][:], in_=o_in[(2 * hd + 1) * 128:(2 * hd + 1) * 128 + 128, c0:c0 + 512]),
                    writes=[("o2", s2)], dma=f"o2_{s2}")
                sc.op("dve", lambda e, s1=s1, s2=s2, sd_=sd_: e.scalar_tensor_tensor(
                    out=t["od"][sd_][:], in0=t["o2"][s2][:], scalar=t["ls"][:, 5:6], in1=t["o1"][s1][:],
                    op0=ALU.mult, op1=ALU.add),
                    reads=[("o1", s1), ("o2", s2), ("neglam",)], writes=[("od", sd_)])
                sq = cx.slot("sq", 4)
                sc.op("act", lambda e, sd_=sd_, sq=sq: e.activation(
                    out=t["sq"][sq][:], in_=t["od"][sd_][:], func=AF.Square),
                    reads=[("od", sd_)], writes=[("sq", sq)])
                bk = cx.bank()
                sc.op("pe", lambda e, sq=sq, bk=bk: e.matmul(
                    ps[:, bk, :], lhsT=t["ones"][:], rhs=t["sq"][sq][:], start=True, stop=True),
                    reads=[("sq", sq), ("ones",)], writes=[("ps", bk)])
                sc.op("act", lambda e, bk=bk: e.activation(
                    out=t["std"][:], in_=ps[:, bk, :], func=AF.Sqrt, bias=t["eps"][:, 0:1], scale=1.0 / 128),
                    reads=[("ps", bk), ("eps",)], writes=[("std",)])
                rs = cx.slot("rstd", 2)
                sc.op("dve", lambda e, rs=rs: e.reciprocal(out=t["rstd"][rs][:], in_=t["std"][:]),
                      reads=[("std",)], writes=[("rstd", rs)])
                sc.op("dve", lambda e, hd=hd, bs=bs, sd_=sd_, rs=rs: e.scalar_tensor_tensor(
                    out=t["h"][:, hd, bs], in0=t["od"][sd_][:], scalar=t["ls"][:, 6:7],
                    in1=t["rstd"][rs][:], op0=ALU.mult, op1=ALU.mult),
                    reads=[("od", sd_), ("sg",), ("rstd", rs)], writes=[("h", hd, b)])
        for g in range(4):
            w = WINS[g]
            c0 = hf * HALF
            sc.op("sp", lambda e, g=g, c0=c0: e.dma_start(
                out=t["uu"][:], in_=u_in[g * 128:g * 128 + 128, c0:c0 + 16 + HALF]),
                writes=[("uu",)], dma="uu")
            src = "uu"
            sh = 1
            flip = 0
            while sh < w:
                dst = "ua" if flip == 0 else "ub"
                sc.op("dve", lambda e, src=src, dst=dst, sh=sh: e.tensor_tensor(
                    out=t[dst][:, sh:16 + HALF], in0=t[src][:, sh:16 + HALF],
                    in1=t[src][:, 0:16 + HALF - sh], op=ALU.add),
                    reads=[(src,)], writes=[(dst,)])
                src = dst
                flip ^= 1
                sh *= 2
            oth = "ua" if src == "ub" else "ub"
            sc.op("dve", lambda e, src=src, w=w: e.scalar_tensor_tensor(
                out=t["dif"][:, 0:HALF], in0=t[src][:, 16:16 + HALF], scalar=1.0 / w,
                in1=t["uu"][:, 16:16 + HALF], op0=ALU.mult, op1=ALU.subtract),
                reads=[(src,), ("uu",)], writes=[("dif",)])
            if hf == 0:
                sc.op("dve", lambda e, src=src, oth=oth, g=g: e.tensor_tensor(
                    out=t[oth][:, 16:32], in0=t[src][:, 16:32],
                    in1=t["icnt"][:, g * 16:g * 16 + 16], op=ALU.mult),
                    reads=[(src,), ("icnt",)], writes=[(oth,)])
                sc.op("dve", lambda e, oth=oth: e.tensor_tensor(
                    out=t["dif"][:, 0:16], in0=t[oth][:, 16:32], in1=t["uu"][:, 16:32], op=ALU.subtract),
                    reads=[(oth,), ("uu",), ("dif",)], writes=[("dif",)])
            for b in range(NB):
                bs = slice(b * 512, (b + 1) * 512)
                bk = cx.bank()
                sc.op("pe", lambda e, g=g, bk=bk, bs=bs: e.matmul(
                    ps[:, bk, :], lhsT=t["pwt"][:, g, :], rhs=t["dif"][:, bs], start=True, stop=True),
                    reads=[("pwt",), ("dif",)], writes=[("ps", bk)])
                sc.op("dve", lambda e, g=g, bk=bk, bs=bs: e.tensor_scalar(
                    out=t["h"][:, 4 + g, bs], in0=ps[:, bk, :], scalar1=t["pscale"][:, g:g + 1],
                    scalar2=None, op0=ALU.mult),
                    reads=[("ps", bk), ("pscale",)], writes=[("h", 4 + g, b)])
        for s in range(4):
            sa = cx.slot("wa", 3)
            sc.op("pool", lambda e, s=s, sa=sa: e.dma_start(
                out=t["wa"][sa][:], in_=woutv[:, :, 256 * s:256 * s + 256]),
                writes=[("wa", sa)], dma=f"wa{sa}")
            for f2 in range(2):
                j = 2 * s + f2
                for b in range(NB):
                    bs = slice(b * 512, (b + 1) * 512)
                    bk = cx.bank()
                    for c in range(8):
                        sc.op("pe", lambda e, c=c, sa=sa, f2=f2, bk=bk, bs=bs: e.matmul(
                            ps[:, bk, :], lhsT=t["wa"][sa][:, c, 128 * f2:128 * f2 + 128],
                            rhs=t["h"][:, c, bs], start=(c == 0), stop=(c == 7)),
                            reads=[("wa", sa), ("h", c, b)], writes=[("ps", bk)])
                    sc.op("dve", lambda e, j=j, bk=bk, bs=bs: e.tensor_tensor(
                        out=t["x"][:, j, bs], in0=ps[:, bk, :], in1=t["x"][:, j, bs], op=ALU.add),
                        reads=[("ps", bk), ("x", j, b)], writes=[("x", j, b)])
        ffn(cx, t, "g3", wg, wu, wd)
        store_x(cx, t, x_out, hf)
        rmsnorm_final(cx, t, y_out, hf)
    return cx.finish()


def rmsnorm_final(cx, t, y_out, hf):
    sc = cx.sc
    ps = t["psum"]
    yv = y_out.rearrange("(c p) n -> p c n", p=128)
    for b in range(NB):
        bs = slice(b * 512, (b + 1) * 512)
        bk = cx.bank()
        for c in range(8):
            s = cx.slot("sq", 4)
            sc.op("act", lambda e, c=c, s=s, bs=bs: e.activation(
                out=t["sq"][s][:], in_=t["x"][:, c, bs], func=AF.Square),
                reads=[("x", c, b)], writes=[("sq", s)])
            sc.op("pe", lambda e, c=c, s=s, bk=bk: e.matmul(
                ps[:, bk, :], lhsT=t["ones"][:], rhs=t["sq"][s][:], start=(c == 0), stop=(c == 7)),
                reads=[("sq", s), ("ones",)], writes=[("ps", bk)])
        sc.op("act", lambda e, bk=bk: e.activation(
            out=t["std"][:], in_=ps[:, bk, :], func=AF.Sqrt, bias=t["eps"][:, 0:1], scale=1.0 / D),
            reads=[("ps", bk), ("eps",)], writes=[("std",)])
        sc.op("dve", lambda e, b=b: e.reciprocal(out=t["rstd"][b][:], in_=t["std"][:]),
              reads=[("std",)], writes=[("rstd", b)])
        for c in range(8):
            so = cx.slot("tmpf", 4)
            sc.op("dve", lambda e, c=c, b=b, bs=bs, so=so: e.scalar_tensor_tensor(
                out=t["tmpf"][so][:], in0=t["x"][:, c, bs], scalar=t["gf"][:, c:c + 1],
                in1=t["rstd"][b][:], op0=ALU.mult, op1=ALU.mult),
                reads=[("x", c, b), ("gf",), ("rstd", b)], writes=[("tmpf", so)])
            c0 = hf * HALF + b * 512
            sc.op("sp", lambda e, c=c, c0=c0, so=so: e.dma_start(
                out=yv[:, c, c0:c0 + 512], in_=t["tmpf"][so][:]),
                reads=[("tmpf", so)], dma=f"tmpf{so}")


_PROGS = {}


def _prog(name):
    if name not in _PROGS:
        _PROGS[name] = {"A": build_A, "B": build_B, "C": build_C}[name]()
    return _PROGS[name]


def _run(name, in_maps):
    res = run_bass_kernel_spmd(_prog(name), in_maps, core_ids=list(range(NCORES)))
    return res.results


def _vec8(g):
    return np.ascontiguousarray(np.asarray(g, np.float32).reshape(8, 128).T)


def _rope_tables():
    d = 64
    inv = (10000.0 ** (-np.arange(0, d, 2, dtype=np.float32) / d)).astype(np.float32)
    ang = np.arange(S, dtype=np.float32)[:, None] * inv[None, :]
    ang = np.concatenate([ang, ang], axis=-1)
    cos = np.cos(ang).astype(np.float32).T
    sin = np.sin(ang).astype(np.float32).T
    sin[:32] *= -1.0
    cos2 = np.concatenate([cos, cos], axis=0)
    sin2 = np.concatenate([sin, sin], axis=0)
    return cos2, sin2


def kernel(x, ffn1_norm, ffn1_w_gate, ffn1_w_up, ffn1_w_down, mix_norm, w_in,
           lambda_q1, lambda_k1, lambda_q2, lambda_k2, subln_gain, pool_w, pool_scale,
           w_out, ffn2_norm, ffn2_w_gate, ffn2_w_up, ffn2_w_down, final_norm):
    f = lambda a: np.ascontiguousarray(np.asarray(a, dtype=np.float32))
    x = f(x)
    xT = [np.ascontiguousarray(x[0, c * T:(c + 1) * T, :].T) for c in range(NCORES)]
    cos2, sin2 = _rope_tables()
    epsd = np.full((128, 1), EPS, np.float32)
    kp = np.arange(128)[:, None, None] + 128 * np.arange(4)[None, :, None]
    qq = np.arange(512)[None, None, :]
    mask = (kp <= qq).astype(np.float32).reshape(128, 4 * 512).astype(ml_dtypes.bfloat16)
    perm = (np.arange(1024).reshape(16, 2, 32)[:, ::-1, :]).reshape(-1)
    WINS = (2, 4, 8, 16)
    y = None
    for l in range(DEPTH):
        wl = f(w_in[l])
        winp = np.ascontiguousarray(wl[:, :1024][:, perm])
        ins = []
        for c in range(NCORES):
            ins.append(dict(x_in=xT[c], g1=_vec8(ffn1_norm[l]), g2=_vec8(mix_norm[l]), epsd=epsd,
                            wg=f(ffn1_w_gate[l]), wu=f(ffn1_w_up[l]), wd=f(ffn1_w_down[l]),
                            win=wl, winp=winp,
                            cosd=np.ascontiguousarray(cos2[:, c * T:(c + 1) * T]),
                            sind=np.ascontiguousarray(sin2[:, c * T:(c + 1) * T])))
        ra = _run("A", ins)
        xT = [ra[c]["x_out"] for c in range(NCORES)]
        qT = np.concatenate([ra[c]["q_out"] for c in range(NCORES)], axis=1)
        kT = np.concatenate([ra[c]["k_out"] for c in range(NCORES)], axis=1)
        v = np.concatenate([ra[c]["v_out"] for c in range(NCORES)], axis=0)
        uT = np.concatenate([ra[c]["u_out"] for c in range(NCORES)], axis=1)
        ins = []
        for c in range(NCORES):
            h, m = c // 2, c % 2
            vh = v[:, h * 128:(h + 1) * 128].reshape(S // 128, 128, 128).transpose(1, 0, 2)
            ins.append(dict(q_in=np.ascontiguousarray(qT[c * 64:(c + 1) * 64]),
                            k_in=np.ascontiguousarray(kT[c * 64:(c + 1) * 64]),
                            v_in=np.ascontiguousarray(vh).reshape(128, -1),
                            m_in=mask))
        rb = _run("B", ins)
        oT = np.concatenate([rb[c]["o_out"] for c in range(NCORES)], axis=0)
        lam_init = 0.8 - 0.6 * math.exp(-0.3 * l)
        lam4 = np.concatenate([f(lambda_q1[l]), f(lambda_k1[l]), f(lambda_q2[l]), f(lambda_k2[l])])
        lam4 = np.ascontiguousarray(np.broadcast_to(lam4[None, :], (128, 256)))
        lconst = np.ascontiguousarray(np.broadcast_to(
            np.array([lam_init, 1.0 - lam_init], np.float32)[None, :], (128, 2)))
        upad = np.concatenate([np.zeros((512, 16), np.float32), uT], axis=1)
        ins = []
        for c in range(NCORES):
            icnt = np.zeros((128, 64), np.float32)
            for g in range(4):
                pos = c * T + np.arange(16)
                icnt[:, g * 16:(g + 1) * 16] = (1.0 / np.minimum(pos + 1, WINS[g]))[None, :]
            ins.append(dict(x_in=xT[c], o_in=np.ascontiguousarray(oT[:, c * T:(c + 1) * T]),
                            u_in=np.ascontiguousarray(upad[:, c * T:c * T + 16 + T]),
                            icnt=icnt, lam4=lam4, lconst=lconst,
                            sgain=f(subln_gain[l]).reshape(128, 1),
                            pw=f(pool_w[l]).reshape(512, 128),
                            pscale=np.ascontiguousarray(f(pool_scale[l]).reshape(4, 128).T),
                            wout=f(w_out[l]), g3=_vec8(ffn2_norm[l]), gf=_vec8(final_norm), epsd=epsd,
                            wg=f(ffn2_w_gate[l]), wu=f(ffn2_w_up[l]), wd=f(ffn2_w_down[l])))
        rc = _run("C", ins)
        xT = [rc[c]["x_out"] for c in range(NCORES)]
        y = [rc[c]["y_out"] for c in range(NCORES)]
    out = np.concatenate([yc.T for yc in y], axis=0)[None]
    return np.ascontiguousarray(out.astype(np.float32))
```

```python
import math
from contextlib import ExitStack

import numpy as np
import ml_dtypes

import concourse.bass as bass
import concourse.mybir as mybir
from concourse.bass_utils import run_bass_kernel_spmd

F32 = mybir.dt.float32
BF16 = mybir.dt.bfloat16
AF = mybir.ActivationFunctionType
ALU = mybir.AluOpType

NCORES = 8
D = 1024
S = 16384
DEPTH = 4
DFF = 2816
NFC = DFF // 128
T = S // NCORES
HALF = 1024
NB = 2
EPS = 1e-6
SAME_ENGINE_SYNC = True


class Sched:
    ENGS = ("pe", "act", "dve", "pool", "sp")

    def __init__(self):
        self.ops = []
        self.last_writer = {}
        self.readers = {}
        self.dma_count = {}

    def op(self, eng, fn, reads=(), writes=(), dma=None):
        idx = len(self.ops)
        deps = set()
        for r in reads:
            if r in self.last_writer:
                deps.add(self.last_writer[r])
        for w in writes:
            if w in self.last_writer:
                deps.add(self.last_writer[w])
            for rd in self.readers.get(w, ()):
                deps.add(rd)
        best = {}
        for d in deps:
            od = self.ops[d]
            k = ("dma", od["dma"]) if od["dma"] is not None else ("eng", od["eng"])
            if k not in best or d > best[k]:
                best[k] = d
        deps = set(best.values())
        o = dict(eng=eng, fn=fn, deps=deps, dma=dma, signal=False, sig_idx=None,
                 dma_val=None)
        if dma is not None:
            self.dma_count[dma] = self.dma_count.get(dma, 0) + 1
            o["dma_val"] = 16 * self.dma_count[dma]
        self.ops.append(o)
        for w in writes:
            self.last_writer[w] = idx
            self.readers[w] = []
        for r in reads:
            self.readers.setdefault(r, []).append(idx)
        return idx

    def finalize(self):
        ops = self.ops
        for o in ops:
            for d in o["deps"]:
                od = ops[d]
                if od["dma"] is None:
                    if od["eng"] != o["eng"] or (SAME_ENGINE_SYNC and od["eng"] != "pe"
                                                 and o["dma"] is None):
                        od["signal"] = True
                    elif o["dma"] is not None and od["eng"] == o["eng"]:
                        od["signal"] = True
        cnt = {e: 0 for e in self.ENGS}
        for o in ops:
            if o["signal"]:
                cnt[o["eng"]] += 1
                o["sig_idx"] = cnt[o["eng"]]
        waited = {e: {} for e in self.ENGS}
        for o in ops:
            w = {}
            for d in o["deps"]:
                od = ops[d]
                if od["dma"] is not None:
                    key = ("dma", od["dma"])
                    val = od["dma_val"]
                else:
                    if not od["signal"]:
                        continue
                    key = ("tl", od["eng"])
                    val = od["sig_idx"]
                w[key] = max(w.get(key, 0), val)
            wl = []
            for key, val in w.items():
                if waited[o["eng"]].get(key, 0) >= val:
                    continue
                waited[o["eng"]][key] = val
                wl.append((key, val))
            o["waits"] = wl

    def emit(self, nc, stack):
        self.finalize()
        sems = {}
        for e in self.ENGS:
            sems[("tl", e)] = stack.enter_context(nc.semaphore("tl_" + e))
        for k in self.dma_count:
            sems[("dma", k)] = stack.enter_context(nc.semaphore("dma_" + str(k)))
        block = stack.enter_context(nc.Block())
        ops = self.ops
        dma_final = dict(self.dma_count)

        def run(eng_name, e, final=False):
            for o in ops:
                if o["eng"] != eng_name:
                    continue
                for key, val in o["waits"]:
                    e.wait_ge(sems[key], val)
                inst = o["fn"](e)
                if o["dma"] is not None:
                    inst.then_inc(sems[("dma", o["dma"])], 16)
                elif o["signal"]:
                    inst.then_inc(sems[("tl", eng_name)], 1)
            if final:
                for k, n in dma_final.items():
                    e.wait_ge(sems[("dma", k)], 16 * n)

        @block.tensor
        def _(e):
            run("pe", e)

        @block.scalar
        def _(e):
            run("act", e)

        @block.vector
        def _(e):
            run("dve", e)

        @block.gpsimd
        def _(e):
            run("pool", e)

        @block.sync
        def _(e):
            run("sp", e, final=True)


class Ctx:
    def __init__(self):
        self.nc = bass.Bass("TRN2", target_bir_lowering=False)
        self.sc = Sched()
        self.stack = ExitStack()
        self.bank_ctr = 0
        self.rot = {}

    def din(self, name, shape, dt=F32):
        return self.nc.dram_tensor(name, list(shape), dt, kind="ExternalInput").ap()

    def dout(self, name, shape, dt=F32):
        return self.nc.dram_tensor(name, list(shape), dt, kind="ExternalOutput").ap()

    def sb(self, name, shape, dt):
        return self.stack.enter_context(self.nc.sbuf_tensor("s_" + name, list(shape), dt))

    def ps(self, name, shape, dt=F32):
        return self.stack.enter_context(self.nc.psum_tensor("p_" + name, list(shape), dt))

    def bank(self):
        b = self.bank_ctr % 8
        self.bank_ctr += 1
        return b

    def slot(self, name, n):
        v = self.rot.get(name, 0)
        self.rot[name] = v + 1
        return v % n

    def finish(self):
        self.sc.emit(self.nc, self.stack)
        self.stack.close()
        return self.nc


def alloc_common(cx):
    t = {}
    t["x"] = cx.sb("x", [128, 8, HALF], F32)
    t["h"] = cx.sb("h", [128, 8, HALF], BF16)
    t["act"] = cx.sb("act", [128, NFC, HALF], BF16)
    t["wa"] = [cx.sb(f"wa{i}", [128, 8, 256], BF16) for i in range(3)]
    t["wb"] = [cx.sb(f"wb{i}", [128, 8, 256], BF16) for i in range(3)]
    t["wd"] = [cx.sb(f"wd{i}", [128, 4, 512], BF16) for i in range(3)]
    t["sq"] = [cx.sb(f"sq{i}", [128, 512], BF16) for i in range(4)]
    t["std"] = cx.sb("std", [128, 512], F32)
    t["rstd"] = [cx.sb(f"rstd{i}", [128, 512], F32) for i in range(2)]
    t["tmpf"] = [cx.sb(f"tmpf{i}", [128, 512], F32) for i in range(4)]
    t["ones"] = cx.sb("ones", [128, 128], BF16)
    t["psum"] = cx.ps("psum", [128, 8, 512], F32)
    cx.sc.op("pool", lambda e: e.memset(t["ones"][:], 1.0), writes=[("ones",)])
    return t


def load_vec(cx, t, name, dram_ap, ncol):
    t[name] = cx.sb(name, [128, ncol], F32)
    cx.sc.op("sp", lambda e: e.dma_start(out=t[name][:], in_=dram_ap),
             writes=[(name,)], dma=name)


def rmsnorm_to_h(cx, t, gname, nchunk=8, src="x", dst="h", scale_d=D):
    sc = cx.sc
    ps = t["psum"]
    for b in range(NB):
        bs = slice(b * 512, (b + 1) * 512)
        bk = cx.bank()
        for c in range(nchunk):
            s = cx.slot("sq", 4)
            sc.op("act", lambda e, c=c, s=s, bs=bs: e.activation(
                out=t["sq"][s][:], in_=t[src][:, c, bs], func=AF.Square),
                reads=[(src, c, b)], writes=[("sq", s)])
            sc.op("pe", lambda e, c=c, s=s, bk=bk: e.matmul(
                ps[:, bk, :], lhsT=t["ones"][:], rhs=t["sq"][s][:],
                start=(c == 0), stop=(c == nchunk - 1)),
                reads=[("sq", s), ("ones",)], writes=[("ps", bk)])
        sc.op("act", lambda e, bk=bk: e.activation(
            out=t["std"][:], in_=ps[:, bk, :], func=AF.Sqrt, bias=t["eps"][:, 0:1],
            scale=1.0 / scale_d),
            reads=[("ps", bk), ("eps",)], writes=[("std",)])
        sc.op("dve", lambda e, b=b: e.reciprocal(out=t["rstd"][b][:], in_=t["std"][:]),
              reads=[("std",)], writes=[("rstd", b)])
        for c in range(nchunk):
            sc.op("dve", lambda e, c=c, b=b, bs=bs: e.scalar_tensor_tensor(
                out=t[dst][:, c, bs], in0=t[src][:, c, bs], scalar=t[gname][:, c:c + 1],
                in1=t["rstd"][b][:], op0=ALU.mult, op1=ALU.mult),
                reads=[(src, c, b), (gname,), ("rstd", b)], writes=[(dst, c, b)])


def ffn(cx, t, gname, wg, wu, wd):
    sc = cx.sc
    ps = t["psum"]
    rmsnorm_to_h(cx, t, gname)
    wgv = wg.rearrange("(c p) f -> p c f", p=128)
    wuv = wu.rearrange("(c p) f -> p c f", p=128)
    nslab = NFC // 2
    for s in range(nslab):
        sa = cx.slot("wa", 3)
        sb_ = cx.slot("wb", 3)
        sc.op("pool", lambda e, s=s, sa=sa: e.dma_start(
            out=t["wa"][sa][:], in_=wgv[:, :, 256 * s:256 * s + 256]),
            writes=[("wa", sa)], dma=f"wa{sa}")
        sc.op("pool", lambda e, s=s, sb_=sb_: e.dma_start(
            out=t["wb"][sb_][:], in_=wuv[:, :, 256 * s:256 * s + 256]),
            writes=[("wb", sb_)], dma=f"wb{sb_}")
        for f2 in range(2):
            fc = 2 * s + f2
            for b in range(NB):
                bs = slice(b * 512, (b + 1) * 512)
                bg = cx.bank()
                bu = cx.bank()
                for c in range(8):
                    sc.op("pe", lambda e, c=c, sa=sa, f2=f2, bg=bg, bs=bs: e.matmul(
                        ps[:, bg, :], lhsT=t["wa"][sa][:, c, 128 * f2:128 * f2 + 128],
                        rhs=t["h"][:, c, bs], start=(c == 0), stop=(c == 7)),
                        reads=[("wa", sa), ("h", c, b)], writes=[("ps", bg)])
                for c in range(8):
                    sc.op("pe", lambda e, c=c, sb_=sb_, f2=f2, bu=bu, bs=bs: e.matmul(
                        ps[:, bu, :], lhsT=t["wb"][sb_][:, c, 128 * f2:128 * f2 + 128],
                        rhs=t["h"][:, c, bs], start=(c == 0), stop=(c == 7)),
                        reads=[("wb", sb_), ("h", c, b)], writes=[("ps", bu)])
                ts = cx.slot("tmpf", 4)
                sc.op("act", lambda e, bg=bg, ts=ts: e.activation(
                    out=t["tmpf"][ts][:], in_=ps[:, bg, :], func=AF.Silu),
                    reads=[("ps", bg)], writes=[("tmpf", ts)])
                sc.op("dve", lambda e, bu=bu, ts=ts, fc=fc, bs=bs: e.tensor_tensor(
                    out=t["act"][:, fc, bs], in0=ps[:, bu, :], in1=t["tmpf"][ts][:],
                    op=ALU.mult),
                    reads=[("ps", bu), ("tmpf", ts)], writes=[("act", fc, b)])
    wdv = wd.rearrange("(s p) d -> p s d", p=128)
    for p_ in range(2):
        banks = [[cx.bank() for b in range(NB)] for jj in range(4)]
        nsl = (NFC + 3) // 4
        for s in range(nsl):
            nch = min(4, NFC - 4 * s)
            sd = cx.slot("wd", 3)
            sc.op("pool", lambda e, s=s, sd=sd, nch=nch, p_=p_: e.dma_start(
                out=t["wd"][sd][:, 0:nch, :],
                in_=wdv[:, 4 * s:4 * s + nch, 512 * p_:512 * p_ + 512]),
                writes=[("wd", sd)], dma=f"wd{sd}")
            for f4 in range(nch):
                fc = 4 * s + f4
                for jj in range(4):
                    for b in range(NB):
                        bs = slice(b * 512, (b + 1) * 512)
                        bk = banks[jj][b]
                        sc.op("pe", lambda e, sd=sd, f4=f4, jj=jj, bk=bk, fc=fc, bs=bs: e.matmul(
                            ps[:, bk, :], lhsT=t["wd"][sd][:, f4, 128 * jj:128 * jj + 128],
                            rhs=t["act"][:, fc, bs], start=(fc == 0), stop=(fc == NFC - 1)),
                            reads=[("wd", sd), ("act", fc, b)], writes=[("ps", bk)])
        for jj in range(4):
            j = 4 * p_ + jj
            for b in range(NB):
                bs = slice(b * 512, (b + 1) * 512)
                bk = banks[jj][b]
                sc.op("dve", lambda e, j=j, bk=bk, bs=bs: e.scalar_tensor_tensor(
                    out=t["x"][:, j, bs], in0=ps[:, bk, :], scalar=0.5,
                    in1=t["x"][:, j, bs], op0=ALU.mult, op1=ALU.add),
                    reads=[("ps", bk), ("x", j, b)], writes=[("x", j, b)])


def load_x(cx, t, x_dram, hf):
    xv = x_dram.rearrange("(c p) n -> p c n", p=128)
    cx.sc.op("sp", lambda e: e.dma_start(out=t["x"][:], in_=xv[:, :, hf * HALF:(hf + 1) * HALF]),
             writes=[("x", c, b) for c in range(8) for b in range(NB)], dma="xin")


def store_x(cx, t, x_dram, hf):
    xv = x_dram.rearrange("(c p) n -> p c n", p=128)
    cx.sc.op("sp", lambda e: e.dma_start(out=xv[:, :, hf * HALF:(hf + 1) * HALF], in_=t["x"][:]),
             reads=[("x", c, b) for c in range(8) for b in range(NB)], dma="xout")


def build_A():
    cx = Ctx()
    sc = cx.sc
    x_in = cx.din("x_in", [D, T])
    g1 = cx.din("g1", [128, 8])
    g2 = cx.din("g2", [128, 8])
    epsd = cx.din("epsd", [128, 1])
    wg = cx.din("wg", [D, DFF])
    wu = cx.din("wu", [D, DFF])
    wd = cx.din("wd", [DFF, D])
    win = cx.din("win", [D, 2048])
    winp = cx.din("winp", [D, 1024])
    cosd = cx.din("cosd", [128, T])
    sind = cx.din("sind", [128, T])
    x_out = cx.dout("x_out", [D, T])
    q_out = cx.dout("q_out", [512, T], BF16)
    k_out = cx.dout("k_out", [512, T], BF16)
    v_out = cx.dout("v_out", [T, 512], BF16)
    u_out = cx.dout("u_out", [512, T])

    t = alloc_common(cx)
    load_vec(cx, t, "g1", g1, 8)
    load_vec(cx, t, "g2", g2, 8)
    load_vec(cx, t, "eps", epsd, 1)
    t["cos"] = cx.sb("cos", [128, HALF], F32)
    t["sin"] = cx.sb("sin", [128, HALF], F32)
    t["stg16"] = [cx.sb(f"stg16_{i}", [128, 512], BF16) for i in range(4)]
    t["stg32"] = [cx.sb(f"stg32_{i}", [128, 512], F32) for i in range(2)]
    ps = t["psum"]
    winv = win.rearrange("(c p) f -> p c f", p=128)
    winpv = winp.rearrange("(c p) f -> p c f", p=128)

    for hf in range(2):
        load_x(cx, t, x_in, hf)
        ffn(cx, t, "g1", wg, wu, wd)
        store_x(cx, t, x_out, hf)
        rmsnorm_to_h(cx, t, "g2")
        sc.op("sp", lambda e, hf=hf: e.dma_start(out=t["cos"][:], in_=cosd[:, hf * HALF:(hf + 1) * HALF]),
              writes=[("cos",)], dma="cos")
        sc.op("sp", lambda e, hf=hf: e.dma_start(out=t["sin"][:], in_=sind[:, hf * HALF:(hf + 1) * HALF]),
              writes=[("sin",)], dma="sin")
        for s in range(4):
            sa = cx.slot("wa", 3)
            sb_ = cx.slot("wb", 3)
            sc.op("pool", lambda e, s=s, sa=sa: e.dma_start(
                out=t["wa"][sa][:], in_=winv[:, :, 256 * s:256 * s + 256]),
                writes=[("wa", sa)], dma=f"wa{sa}")
            sc.op("pool", lambda e, s=s, sb_=sb_: e.dma_start(
                out=t["wb"][sb_][:], in_=winpv[:, :, 256 * s:256 * s + 256]),
                writes=[("wb", sb_)], dma=f"wb{sb_}")
            for f2 in range(2):
                ch = 2 * s + f2
                dst = q_out if ch < 4 else k_out
                row0 = (ch % 4) * 128
                for b in range(NB):
                    bs = slice(b * 512, (b + 1) * 512)
                    b1 = cx.bank()
                    b2 = cx.bank()
                    for c in range(8):
                        sc.op("pe", lambda e, c=c, sa=sa, f2=f2, b1=b1, bs=bs: e.matmul(
                            ps[:, b1, :], lhsT=t["wa"][sa][:, c, 128 * f2:128 * f2 + 128],
                            rhs=t["h"][:, c, bs], start=(c == 0), stop=(c == 7)),
                            reads=[("wa", sa), ("h", c, b)], writes=[("ps", b1)])
                    for c in range(8):
                        sc.op("pe", lambda e, c=c, sb_=sb_, f2=f2, b2=b2, bs=bs: e.matmul(
                            ps[:, b2, :], lhsT=t["wb"][sb_][:, c, 128 * f2:128 * f2 + 128],
                            rhs=t["h"][:, c, bs], start=(c == 0), stop=(c == 7)),
                            reads=[("wb", sb_), ("h", c, b)], writes=[("ps", b2)])
                    s1 = cx.slot("tmpf", 4)
                    s2 = cx.slot("tmpf", 4)
                    so = cx.slot("stg16", 4)
                    sc.op("dve", lambda e, b1=b1, s1=s1, bs=bs: e.tensor_tensor(
                        out=t["tmpf"][s1][:], in0=ps[:, b1, :], in1=t["cos"][:, bs], op=ALU.mult),
                        reads=[("ps", b1), ("cos",)], writes=[("tmpf", s1)])
                    sc.op("dve", lambda e, b2=b2, s2=s2, bs=bs: e.tensor_tensor(
                        out=t["tmpf"][s2][:], in0=ps[:, b2, :], in1=t["sin"][:, bs], op=ALU.mult),
                        reads=[("ps", b2), ("sin",)], writes=[("tmpf", s2)])
                    sc.op("pool", lambda e, s1=s1, s2=s2, so=so: e.tensor_tensor(
                        out=t["stg16"][so][:], in0=t["tmpf"][s1][:], in1=t["tmpf"][s2][:], op=ALU.add),
                        reads=[("tmpf", s1), ("tmpf", s2)], writes=[("stg16", so)])
                    c0 = hf * HALF + b * 512
                    sc.op("sp", lambda e, dst=dst, row0=row0, c0=c0, so=so: e.dma_start(
                        out=dst[row0:row0 + 128, c0:c0 + 512], in_=t["stg16"][so][:]),
                        reads=[("stg16", so)], dma=f"stg16_{so}")
        for s in range(2):
            sa = cx.slot("wa", 3)
            sc.op("pool", lambda e, s=s, sa=sa: e.dma_start(
                out=t["wa"][sa][:], in_=winv[:, :, 1536 + 256 * s:1536 + 256 * s + 256]),
                writes=[("wa", sa)], dma=f"wa{sa}")
            for f2 in range(2):
                ch = 2 * s + f2
                for b in range(NB):
                    bs = slice(b * 512, (b + 1) * 512)
                    b1 = cx.bank()
                    for c in range(8):
                        sc.op("pe", lambda e, c=c, sa=sa, f2=f2, b1=b1, bs=bs: e.matmul(
                            ps[:, b1, :], lhsT=t["wa"][sa][:, c, 128 * f2:128 * f2 + 128],
                            rhs=t["h"][:, c, bs], start=(c == 0), stop=(c == 7)),
                            reads=[("wa", sa), ("h", c, b)], writes=[("ps", b1)])
                    so = cx.slot("stg32", 2)
                    sc.op("act", lambda e, b1=b1, so=so: e.activation(
                        out=t["stg32"][so][:], in_=ps[:, b1, :], func=AF.Copy),
                        reads=[("ps", b1)], writes=[("stg32", so)])
                    c0 = hf * HALF + b * 512
                    sc.op("sp", lambda e, ch=ch, c0=c0, so=so: e.dma_start(
                        out=u_out[ch * 128:ch * 128 + 128, c0:c0 + 512], in_=t["stg32"][so][:]),
                        reads=[("stg32", so)], dma=f"stg32_{so}")
        for s in range(2):
            sa = cx.slot("wa", 3)
            sc.op("pool", lambda e, s=s, sa=sa: e.dma_start(
                out=t["wa"][sa][:], in_=winv[:, :, 1024 + 256 * s:1024 + 256 * s + 256]),
                writes=[("wa", sa)], dma=f"wa{sa}")
            for tt in range(8):
                b = tt // 4
                b1 = cx.bank()
                for c in range(8):
                    sc.op("pe", lambda e, c=c, sa=sa, tt=tt, b1=b1: e.matmul(
                        ps[:, b1, 0:256], lhsT=t["h"][:, c, tt * 128:tt * 128 + 128],
                        rhs=t["wa"][sa][:, c, :], start=(c == 0), stop=(c == 7)),
                        reads=[("wa", sa), ("h", c, b)], writes=[("ps", b1)])
                so = cx.slot("stg16", 4)
                sc.op("act", lambda e, b1=b1, so=so: e.activation(
                    out=t["stg16"][so][:, 0:256], in_=ps[:, b1, 0:256], func=AF.Copy),
                    reads=[("ps", b1)], writes=[("stg16", so)])
                r0 = hf * HALF + tt * 128
                sc.op("sp", lambda e, r0=r0, s=s, so=so: e.dma_start(
                    out=v_out[r0:r0 + 128, 256 * s:256 * s + 256], in_=t["stg16"][so][:, 0:256]),
                    reads=[("stg16", so)], dma=f"stg16_{so}")
    return cx.finish()


def build_B():
    cx = Ctx()
    sc = cx.sc
    NQB = S // 512
    NKT = S // 128
    q_in = cx.din("q_in", [64, S], BF16)
    k_in = cx.din("k_in", [64, S], BF16)
    v_in = cx.din("v_in", [128, NKT * 128], BF16)
    m_in = cx.din("m_in", [128, 4 * 512], BF16)
    o_out = cx.dout("o_out", [128, S])

    qT = cx.sb("qT", [64, S], BF16)
    kT = cx.sb("kT", [64, S], BF16)
    vv = cx.sb("vv", [128, NKT * 128], BF16)
    mk = cx.sb("mk", [128, 4 * 512], BF16)
    mkv = mk[:].rearrange("p (j q) -> p j q", j=4)
    ones = cx.sb("ones", [128, 128], BF16)
    NP = 4
    pT = [cx.sb(f"pT{i}", [128, 2, 512], BF16) for i in range(NP)]
    rl = cx.sb("rl", [128, 512], F32)
    ostg = [cx.sb(f"ostg{i}", [128, 512], F32) for i in range(2)]
    ps = cx.ps("psum", [128, 8, 512], F32)

    sc.op("pool", lambda e: e.memset(ones[:], 1.0), writes=[("ones",)])
    NCH = 8
    cw = S // NCH
    for i in range(NCH):
        sc.op("sp", lambda e, i=i: e.dma_start(out=qT[:, i * cw:(i + 1) * cw], in_=q_in[:, i * cw:(i + 1) * cw]),
              writes=[("q", i)], dma=f"q{i}")
        sc.op("sp", lambda e, i=i: e.dma_start(out=kT[:, i * cw:(i + 1) * cw], in_=k_in[:, i * cw:(i + 1) * cw]),
              writes=[("k", i)], dma=f"k{i}")
        sc.op("sp", lambda e, i=i: e.dma_start(out=vv[:, i * cw:(i + 1) * cw], in_=v_in[:, i * cw:(i + 1) * cw]),
              writes=[("v", i)], dma=f"v{i}")
    sc.op("sp", lambda e: e.dma_start(out=mk[:], in_=m_in), writes=[("mk",)], dma="mk")

    groups = [(qb, g) for qb in range(NQB) for g in range(2 * (qb + 1))]
    LOOK = 1
    ginfo = {}

    def emit_S(i):
        qb, g = groups[i]
        b0 = 2 * (i % 2)
        sl = i % NP
        ginfo[i] = (b0, sl)
        for t2 in range(2):
            kt = 2 * g + t2
            sc.op("pe", lambda e, kt=kt, t2=t2: e.matmul(
                ps[0:128, b0 + t2, :], lhsT=kT[:, kt * 128:(kt + 1) * 128],
                rhs=qT[:, qb * 512:(qb + 1) * 512], start=True, stop=True),
                reads=[("k", kt * 128 // cw), ("q", qb * 512 // cw)], writes=[("ps", b0 + t2)])
        sc.op("act", lambda e: e.activation(out=pT[sl][:], in_=ps[:, b0:b0 + 2, :], func=AF.Exp, scale=0.125),
              reads=[("ps", b0), ("ps", b0 + 1)], writes=[("pT", sl)])
        j2 = g - 2 * qb
        if j2 >= 0:
            sc.op("pool", lambda e: e.tensor_tensor(out=pT[sl][:], in0=pT[sl][:],
                                                   in1=mk[:, j2 * 2, :] if False else mkv[:, j2 * 2:j2 * 2 + 2, :],
                                                   op=ALU.mult),
                  reads=[("pT", sl), ("mk",)], writes=[("pT", sl)])

    def emit_PV(i):
        qb, g = groups[i]
        b0, sl = ginfo[i]
        ob = 4 + (qb % 2)
        lb = 6 + (qb % 2)
        ng = 2 * (qb + 1)
        for t2 in range(2):
            kt = 2 * g + t2
            sc.op("pe", lambda e, kt=kt, t2=t2: e.matmul(
                ps[:, ob, :], lhsT=vv[:, kt * 128:(kt + 1) * 128], rhs=pT[sl][:, t2, :],
                start=(kt == 0), stop=(kt == 2 * ng - 1)),
                reads=[("pT", sl), ("v", kt * 128 // cw)], writes=[("ps", ob)])
        for t2 in range(2):
            kt = 2 * g + t2
            sc.op("pe", lambda e, kt=kt, t2=t2: e.matmul(
                ps[:, lb, :], lhsT=ones[:], rhs=pT[sl][:, t2, :],
                start=(kt == 0), stop=(kt == 2 * ng - 1)),
                reads=[("pT", sl), ("ones",)], writes=[("ps", lb)])
        if g == ng - 1:
            so = qb % 2
            sc.op("dve", lambda e: e.reciprocal(out=rl[:], in_=ps[:, lb, :]),
                  reads=[("ps", lb)], writes=[("rl",)])
            sc.op("dve", lambda e: e.tensor_tensor(out=ostg[so][:], in0=ps[:, ob, :], in1=rl[:], op=ALU.mult),
                  reads=[("ps", ob), ("rl",)], writes=[("ostg", so)])
            sc.op("sp", lambda e: e.dma_start(out=o_out[:, qb * 512:(qb + 1) * 512], in_=ostg[so][:]),
                  reads=[("ostg", so)], dma=f"ostg{so}")

    n = len(groups)
    for i in range(n + LOOK):
        if i < n:
            emit_S(i)
        if i - LOOK >= 0:
            emit_PV(i - LOOK)
    return cx.finish()


def build_C():
    cx = Ctx()
    sc = cx.sc
    x_in = cx.din("x_in", [D, T])
    o_in = cx.din("o_in", [8 * 128, T])
    u_in = cx.din("u_in", [512, 16 + T])
    icnt = cx.din("icnt", [128, 4 * 16])
    lam4 = cx.din("lam4", [128, 4 * 64])
    lconst = cx.din("lconst", [128, 2])
    sgain = cx.din("sgain", [128, 1])
    pw = cx.din("pw", [4 * 128, 128])
    pscale = cx.din("pscale", [128, 4])
    wout = cx.din("wout", [D, D])
    g3 = cx.din("g3", [128, 8])
    gf = cx.din("gf", [128, 8])
    epsd = cx.din("epsd", [128, 1])
    wg = cx.din("wg", [D, DFF])
    wu = cx.din("wu", [D, DFF])
    wd = cx.din("wd", [DFF, D])
    x_out = cx.dout("x_out", [D, T])
    y_out = cx.dout("y_out", [D, T])

    t = alloc_common(cx)
    ps = t["psum"]
    for nm, ap, n in (("g3", g3, 8), ("gf", gf, 8), ("eps", epsd, 1), ("lam4", lam4, 256),
                      ("lconst", lconst, 2), ("sgain", sgain, 1), ("pscale", pscale, 4),
                      ("icnt", icnt, 64)):
        load_vec(cx, t, nm, ap, n)
    t["lt"] = cx.sb("lt", [128, 128], F32)
    t["ls"] = cx.sb("ls", [128, 8], F32)
    sc.op("dve", lambda e: e.tensor_tensor(out=t["lt"][:, 0:64], in0=t["lam4"][:, 0:64],
                                           in1=t["lam4"][:, 64:128], op=ALU.mult),
          reads=[("lam4",)], writes=[("lt", 0)])
    sc.op("dve", lambda e: e.tensor_tensor(out=t["lt"][:, 64:128], in0=t["lam4"][:, 128:192],
                                           in1=t["lam4"][:, 192:256], op=ALU.mult),
          reads=[("lam4",)], writes=[("lt", 1)])
    sc.op("dve", lambda e: e.tensor_reduce(out=t["ls"][:, 0:1], in_=t["lt"][:, 0:64],
                                           axis=mybir.AxisListType.X, op=ALU.add),
          reads=[("lt", 0)], writes=[("ls", 0)])
    sc.op("dve", lambda e: e.tensor_reduce(out=t["ls"][:, 1:2], in_=t["lt"][:, 64:128],
                                           axis=mybir.AxisListType.X, op=ALU.add),
          reads=[("lt", 1)], writes=[("ls", 1)])
    sc.op("act", lambda e: e.activation(out=t["ls"][:, 2:4], in_=t["ls"][:, 0:2], func=AF.Exp),
          reads=[("ls", 0), ("ls", 1)], writes=[("ls", 2)])
    sc.op("dve", lambda e: e.tensor_tensor(out=t["ls"][:, 4:5], in0=t["ls"][:, 3:4], in1=t["ls"][:, 2:3],
                                           op=ALU.subtract),
          reads=[("ls", 2)], writes=[("ls", 4)])
    sc.op("dve", lambda e: e.tensor_tensor(out=t["ls"][:, 5:6], in0=t["ls"][:, 4:5], in1=t["lconst"][:, 0:1],
                                           op=ALU.subtract),
          reads=[("ls", 4), ("lconst",)], writes=[("neglam",)])
    sc.op("dve", lambda e: e.tensor_tensor(out=t["ls"][:, 6:7], in0=t["sgain"][:, 0:1], in1=t["lconst"][:, 1:2],
                                           op=ALU.mult),
          reads=[("sgain",), ("lconst",)], writes=[("sg",)])

    t["o1"] = [cx.sb(f"o1_{i}", [128, 512], F32) for i in range(2)]
    t["o2"] = [cx.sb(f"o2_{i}", [128, 512], F32) for i in range(2)]
    t["od"] = [cx.sb(f"od_{i}", [128, 512], F32) for i in range(2)]
    t["uu"] = cx.sb("uu", [128, 16 + HALF], F32)
    t["ua"] = cx.sb("ua", [128, 16 + HALF], F32)
    t["ub"] = cx.sb("ub", [128, 16 + HALF], F32)
    t["dif"] = cx.sb("dif", [128, HALF], BF16)
    t["pwt"] = cx.sb("pwt", [128, 4, 128], BF16)
    sc.op("pool", lambda e: e.dma_start(out=t["pwt"][:], in_=pw.rearrange("(g c) e -> c g e", c=128)),
          writes=[("pwt",)], dma="pwt")
    woutv = wout.rearrange("(c p) f -> p c f", p=128)
    WINS = (2, 4, 8, 16)

    for hf in range(2):
        load_x(cx, t, x_in, hf)
        for hd in range(4):
            for b in range(NB):
                bs = slice(b * 512, (b + 1) * 512)
                c0 = hf * HALF + b * 512
                s1 = cx.slot("o1", 2)
                s2 = cx.slot("o2", 2)
                sd_ = cx.slot("od", 2)
                sc.op("sp", lambda e, hd=hd, c0=c0, s1=s1: e.dma_start(
                    out=t["o1"][s1][:], in_=o_in[(2 * hd) * 128:(2 * hd) * 128 + 128, c0:c0 + 512]),
                    writes=[("o1", s1)], dma=f"o1_{s1}")
                sc.op("sp", lambda e, hd=hd, c0=c0, s2=s2: e.dma_start(
                    out=t["o2"][s2][:], in_=o_in[(2 * hd + 1) * 128:(2 * hd + 1) * 128 + 128, c0:c0 + 512]),
                    writes=[("o2", s2)], dma=f"o2_{s2}")
                sc.op("dve", lambda e, s1=s1, s2=s2, sd_=sd_: e.scalar_tensor_tensor(
                    out=t["od"][sd_][:], in0=t["o2"][s2][:], scalar=t["ls"][:, 5:6], in1=t["o1"][s1][:],
                    op0=ALU.mult, op1=ALU.add),
                    reads=[("o1", s1), ("o2", s2), ("neglam",)], writes=[("od", sd_)])
                sq = cx.slot("sq", 4)
                sc.op("act", lambda e, sd_=sd_, sq=sq: e.activation(
                    out=t["sq"][sq][:], in_=t["od"][sd_][:], func=AF.Square),
                    reads=[("od", sd_)], writes=[("sq", sq)])
                bk = cx.bank()
                sc.op("pe", lambda e, sq=sq, bk=bk: e.matmul(
                    ps[:, bk, :], lhsT=t["ones"][:], rhs=t["sq"][sq][:], start=True, stop=True),
                    reads=[("sq", sq), ("ones",)], writes=[("ps", bk)])
                sc.op("act", lambda e, bk=bk: e.activation(
                    out=t["std"][:], in_=ps[:, bk, :], func=AF.Sqrt, bias=t["eps"][:, 0:1], scale=1.0 / 128),
                    reads=[("ps", bk), ("eps",)], writes=[("std",)])
                rs = cx.slot("rstd", 2)
                sc.op("dve", lambda e, rs=rs: e.reciprocal(out=t["rstd"][rs][:], in_=t["std"][:]),
                      reads=[("std",)], writes=[("rstd", rs)])
                sc.op("dve", lambda e, hd=hd, bs=bs, sd_=sd_, rs=rs: e.scalar_tensor_tensor(
                    out=t["h"][:, hd, bs], in0=t["od"][sd_][:], scalar=t["ls"][:, 6:7],
                    in1=t["rstd"][rs][:], op0=ALU.mult, op1=ALU.mult),
                    reads=[("od", sd_), ("sg",), ("rstd", rs)], writes=[("h", hd, b)])
        for g in range(4):
            w = WINS[g]
            c0 = hf * HALF
            sc.op("sp", lambda e, g=g, c0=c0: e.dma_start(
                out=t["uu"][:], in_=u_in[g * 128:g * 128 + 128, c0:c0 + 16 + HALF]),
                writes=[("uu",)], dma="uu")
            src = "uu"
            sh = 1
            flip = 0
            while sh < w:
                dst = "ua" if flip == 0 else "ub"
                sc.op("dve", lambda e, src=src, dst=dst, sh=sh: e.tensor_tensor(
                    out=t[dst][:, sh:16 + HALF], in0=t[src][:, sh:16 + HALF],
                    in1=t[src][:, 0:16 + HALF - sh], op=ALU.add),
                    reads=[(src,)], writes=[(dst,)])
                src = dst
                flip ^= 1
                sh *= 2
            oth = "ua" if src == "ub" else "ub"
            sc.op("dve", lambda e, src=src, w=w: e.scalar_tensor_tensor(
                out=t["dif"][:, 0:HALF], in0=t[src][:, 16:16 + HALF], scalar=1.0 / w,
                in1=t["uu"][:, 16:16 + HALF], op0=ALU.mult, op1=ALU.subtract),
                reads=[(src,), ("uu",)], writes=[("dif",)])
            if hf == 0:
                sc.op("dve", lambda e, src=src, oth=oth, g=g: e.tensor_tensor(
                    out=t[oth][:, 16:32], in0=t[src][:, 16:32],
                    in1=t["icnt"][:, g * 16:g * 16 + 16], op=ALU.mult),
                    reads=[(src,), ("icnt",)], writes=[(oth,)])
                sc.op("dve", lambda e, oth=oth: e.tensor_tensor(
                    out=t["dif"][:, 0:16], in0=t[oth][:, 16:32], in1=t["uu"][:, 16:32], op=ALU.subtract),
                    reads=[(oth,), ("uu",), ("dif",)], writes=[("dif",)])
            for b in range(NB):
                bs = slice(b * 512, (b + 1) * 512)
                bk = cx.bank()
                sc.op("pe", lambda e, g=g, bk=bk, bs=bs: e.matmul(
                    ps[:, bk, :], lhsT=t["pwt"][:, g, :], rhs=t["dif"][:, bs], start=True, stop=True),
                    reads=[("pwt",), ("dif",)], writes=[("ps", bk)])
                sc.op("dve", lambda e, g=g, bk=bk, bs=bs: e.tensor_scalar(
                    out=t["h"][:, 4 + g, bs], in0=ps[:, bk, :], scalar1=t["pscale"][:, g:g + 1],
                    scalar2=None, op0=ALU.mult),
                    reads=[("ps", bk), ("pscale",)], writes=[("h", 4 + g, b)])
        for s in range(4):
            sa = cx.slot("wa", 3)
            sc.op("pool", lambda e, s=s, sa=sa: e.dma_start(
                out=t["wa"][sa][:], in_=woutv[:, :, 256 * s:256 * s + 256]),
                writes=[("wa", sa)], dma=f"wa{sa}")
            for f2 in range(2):
                j = 2 * s + f2
                for b in range(NB):
                    bs = slice(b * 512, (b + 1) * 512)
                    bk = cx.bank()
                    for c in range(8):
                        sc.op("pe", lambda e, c=c, sa=sa, f2=f2, bk=bk, bs=bs: e.matmul(
                            ps[:, bk, :], lhsT=t["wa"][sa][:, c, 128 * f2:128 * f2 + 128],
                            rhs=t["h"][:, c, bs], start=(c == 0), stop=(c == 7)),
                            reads=[("wa", sa), ("h", c, b)], writes=[("ps", bk)])
                    sc.op("dve", lambda e, j=j, bk=bk, bs=bs: e.tensor_tensor(
                        out=t["x"][:, j, bs], in0=ps[:, bk, :], in1=t["x"][:, j, bs], op=ALU.add),
                        reads=[("ps", bk), ("x", j, b)], writes=[("x", j, b)])
        ffn(cx, t, "g3", wg, wu, wd)
        store_x(cx, t, x_out, hf)
        rmsnorm_final(cx, t, y_out, hf)
    return cx.finish()


def rmsnorm_final(cx, t, y_out, hf):
    sc = cx.sc
    ps = t["psum"]
    yv = y_out.rearrange("(c p) n -> p c n", p=128)
    for b in range(NB):
        bs = slice(b * 512, (b + 1) * 512)
        bk = cx.bank()
        for c in range(8):
            s = cx.slot("sq", 4)
            sc.op("act", lambda e, c=c, s=s, bs=bs: e.activation(
                out=t["sq"][s][:], in_=t["x"][:, c, bs], func=AF.Square),
                reads=[("x", c, b)], writes=[("sq", s)])
            sc.op("pe", lambda e, c=c, s=s, bk=bk: e.matmul(
                ps[:, bk, :], lhsT=t["ones"][:], rhs=t["sq"][s][:], start=(c == 0), stop=(c == 7)),
                reads=[("sq", s), ("ones",)], writes=[("ps", bk)])
        sc.op("act", lambda e, bk=bk: e.activation(
            out=t["std"][:], in_=ps[:, bk, :], func=AF.Sqrt, bias=t["eps"][:, 0:1], scale=1.0 / D),
            reads=[("ps", bk), ("eps",)], writes=[("std",)])
        sc.op("dve", lambda e, b=b: e.reciprocal(out=t["rstd"][b][:], in_=t["std"][:]),
              reads=[("std",)], writes=[("rstd", b)])
        for c in range(8):
            so = cx.slot("tmpf", 4)
            sc.op("dve", lambda e, c=c, b=b, bs=bs, so=so: e.scalar_tensor_tensor(
                out=t["tmpf"][so][:], in0=t["x"][:, c, bs], scalar=t["gf"][:, c:c + 1],
                in1=t["rstd"][b][:], op0=ALU.mult, op1=ALU.mult),
                reads=[("x", c, b), ("gf",), ("rstd", b)], writes=[("tmpf", so)])
            c0 = hf * HALF + b * 512
            sc.op("sp", lambda e, c=c, c0=c0, so=so: e.dma_start(
                out=yv[:, c, c0:c0 + 512], in_=t["tmpf"][so][:]),
                reads=[("tmpf", so)], dma=f"tmpf{so}")


_PROGS = {}


def _prog(name):
    if name not in _PROGS:
        _PROGS[name] = {"A": build_A, "B": build_B, "C": build_C}[name]()
    return _PROGS[name]


def _run(name, in_maps):
    res = run_bass_kernel_spmd(_prog(name), in_maps, core_ids=list(range(NCORES)))
    return res.results


def _vec8(g):
    return np.ascontiguousarray(np.asarray(g, np.float32).reshape(8, 128).T)


def _rope_tables():
    d = 64
    inv = (10000.0 ** (-np.arange(0, d, 2, dtype=np.float32) / d)).astype(np.float32)
    ang = np.arange(S, dtype=np.float32)[:, None] * inv[None, :]
    ang = np.concatenate([ang, ang], axis=-1)
    cos = np.cos(ang).astype(np.float32).T
    sin = np.sin(ang).astype(np.float32).T
    sin[:32] *= -1.0
    cos2 = np.concatenate([cos, cos], axis=0)
    sin2 = np.concatenate([sin, sin], axis=0)
    return cos2, sin2


def kernel(x, ffn1_norm, ffn1_w_gate, ffn1_w_up, ffn1_w_down, mix_norm, w_in,
           lambda_q1, lambda_k1, lambda_q2, lambda_k2, subln_gain, pool_w, pool_scale,
           w_out, ffn2_norm, ffn2_w_gate, ffn2_w_up, ffn2_w_down, final_norm):
    f = lambda a: np.ascontiguousarray(np.asarray(a, dtype=np.float32))
    x = f(x)
    xT = [np.ascontiguousarray(x[0, c * T:(c + 1) * T, :].T) for c in range(NCORES)]
    cos2, sin2 = _rope_tables()
    epsd = np.full((128, 1), EPS, np.float32)
    kp = np.arange(128)[:, None, None] + 128 * np.arange(4)[None, :, None]
    qq = np.arange(512)[None, None, :]
    mask = (kp <= qq).astype(np.float32).reshape(128, 4 * 512).astype(ml_dtypes.bfloat16)
    perm = (np.arange(1024).reshape(16, 2, 32)[:, ::-1, :]).reshape(-1)
    WINS = (2, 4, 8, 16)
    y = None
    for l in range(DEPTH):
        wl = f(w_in[l])
        winp = np.ascontiguousarray(wl[:, :1024][:, perm])
        ins = []
        for c in range(NCORES):
            ins.append(dict(x_in=xT[c], g1=_vec8(ffn1_norm[l]), g2=_vec8(mix_norm[l]), epsd=epsd,
                            wg=f(ffn1_w_gate[l]), wu=f(ffn1_w_up[l]), wd=f(ffn1_w_down[l]),
                            win=wl, winp=winp,
                            cosd=np.ascontiguousarray(cos2[:, c * T:(c + 1) * T]),
                            sind=np.ascontiguousarray(sin2[:, c * T:(c + 1) * T])))
        ra = _run("A", ins)
        xT = [ra[c]["x_out"] for c in range(NCORES)]
        qT = np.concatenate([ra[c]["q_out"] for c in range(NCORES)], axis=1)
        kT = np.concatenate([ra[c]["k_out"] for c in range(NCORES)], axis=1)
        v = np.concatenate([ra[c]["v_out"] for c in range(NCORES)], axis=0)
        uT = np.concatenate([ra[c]["u_out"] for c in range(NCORES)], axis=1)
        ins = []
        for c in range(NCORES):
            h, m = c // 2, c % 2
            vh = v[:, h * 128:(h + 1) * 128].reshape(S // 128, 128, 128).transpose(1, 0, 2)
            ins.append(dict(q_in=np.ascontiguousarray(qT[c * 64:(c + 1) * 64]),
                            k_in=np.ascontiguousarray(kT[c * 64:(c + 1) * 64]),
                            v_in=np.ascontiguousarray(vh).reshape(128, -1),
                            m_in=mask))
        rb = _run("B", ins)
        oT = np.concatenate([rb[c]["o_out"] for c in range(NCORES)], axis=0)
        lam_init = 0.8 - 0.6 * math.exp(-0.3 * l)
        lam4 = np.concatenate([f(lambda_q1[l]), f(lambda_k1[l]), f(lambda_q2[l]), f(lambda_k2[l])])
        lam4 = np.ascontiguousarray(np.broadcast_to(lam4[None, :], (128, 256)))
        lconst = np.ascontiguousarray(np.broadcast_to(
            np.array([lam_init, 1.0 - lam_init], np.float32)[None, :], (128, 2)))
        upad = np.concatenate([np.zeros((512, 16), np.float32), uT], axis=1)
        ins = []
        for c in range(NCORES):
            icnt = np.zeros((128, 64), np.float32)
            for g in range(4):
                pos = c * T + np.arange(16)
                icnt[:, g * 16:(g + 1) * 16] = (1.0 / np.minimum(pos + 1, WINS[g]))[None, :]
            ins.append(dict(x_in=xT[c], o_in=np.ascontiguousarray(oT[:, c * T:(c + 1) * T]),
                            u_in=np.ascontiguousarray(upad[:, c * T:c * T + 16 + T]),
                            icnt=icnt, lam4=lam4, lconst=lconst,
                            sgain=f(subln_gain[l]).reshape(128, 1),
                            pw=f(pool_w[l]).reshape(512, 128),
                            pscale=np.ascontiguousarray(f(pool_scale[l]).reshape(4, 128).T),
                            wout=f(w_out[l]), g3=_vec8(ffn2_norm[l]), gf=_vec8(final_norm), epsd=epsd,
                            wg=f(ffn2_w_gate[l]), wu=f(ffn2_w_up[l]), wd=f(ffn2_w_down[l])))
        rc = _run("C", ins)
        xT = [rc[c]["x_out"] for c in range(NCORES)]
        y = [rc[c]["y_out"] for c in range(NCORES)]
    out = np.concatenate([yc.T for yc in y], axis=0)[None]
    return np.ascontiguousarray(out.astype(np.float32))
```

```python
import math
from contextlib import ExitStack

import numpy as np
import ml_dtypes

import concourse.bass as bass
import concourse.mybir as mybir
from concourse.bass_utils import run_bass_kernel_spmd

F32 = mybir.dt.float32
BF16 = mybir.dt.bfloat16
AF = mybir.ActivationFunctionType
ALU = mybir.AluOpType

NCORES = 8
D = 1024
S = 16384
DEPTH = 4
DFF = 2816
NFC = DFF // 128
T = S // NCORES
HALF = 1024
NB = 2
EPS = 1e-6
SAME_ENGINE_SYNC = True


class Sched:
    ENGS = ("pe", "act", "dve", "pool", "sp")

    def __init__(self):
        self.ops = []
        self.last_writer = {}
        self.readers = {}
        self.dma_count = {}

    def op(self, eng, fn, reads=(), writes=(), dma=None):
        idx = len(self.ops)
        deps = set()
        for r in reads:
            if r in self.last_writer:
                deps.add(self.last_writer[r])
        for w in writes:
            if w in self.last_writer:
                deps.add(self.last_writer[w])
            for rd in self.readers.get(w, ()):
                deps.add(rd)
        best = {}
        for d in deps:
            od = self.ops[d]
            k = ("dma", od["dma"]) if od["dma"] is not None else ("eng", od["eng"])
            if k not in best or d > best[k]:
                best[k] = d
        deps = set(best.values())
        o = dict(eng=eng, fn=fn, deps=deps, dma=dma, signal=False, sig_idx=None,
                 dma_val=None)
        if dma is not None:
            self.dma_count[dma] = self.dma_count.get(dma, 0) + 1
            o["dma_val"] = 16 * self.dma_count[dma]
        self.ops.append(o)
        for w in writes:
            self.last_writer[w] = idx
            self.readers[w] = []
        for r in reads:
            self.readers.setdefault(r, []).append(idx)
        return idx

    def finalize(self):
        ops = self.ops
        for o in ops:
            for d in o["deps"]:
                od = ops[d]
                if od["dma"] is None:
                    if od["eng"] != o["eng"] or (SAME_ENGINE_SYNC and od["eng"] != "pe"
                                                 and o["dma"] is None):
                        od["signal"] = True
                    elif o["dma"] is not None and od["eng"] == o["eng"]:
                        od["signal"] = True
        cnt = {e: 0 for e in self.ENGS}
        for o in ops:
            if o["signal"]:
                cnt[o["eng"]] += 1
                o["sig_idx"] = cnt[o["eng"]]
        waited = {e: {} for e in self.ENGS}
        for o in ops:
            w = {}
            for d in o["deps"]:
                od = ops[d]
                if od["dma"] is not None:
                    key = ("dma", od["dma"])
                    val = od["dma_val"]
                else:
                    if not od["signal"]:
                        continue
                    key = ("tl", od["eng"])
                    val = od["sig_idx"]
                w[key] = max(w.get(key, 0), val)
            wl = []
            for key, val in w.items():
                if waited[o["eng"]].get(key, 0) >= val:
                    continue
                waited[o["eng"]][key] = val
                wl.append((key, val))
            o["waits"] = wl

    def emit(self, nc, stack):
        self.finalize()
        sems = {}
        for e in self.ENGS:
            sems[("tl", e)] = stack.enter_context(nc.semaphore("tl_" + e))
        for k in self.dma_count:
            sems[("dma", k)] = stack.enter_context(nc.semaphore("dma_" + str(k)))
        block = stack.enter_context(nc.Block())
        ops = self.ops
        dma_final = dict(self.dma_count)

        def run(eng_name, e, final=False):
            for o in ops:
                if o["eng"] != eng_name:
                    continue
                for key, val in o["waits"]:
                    e.wait_ge(sems[key], val)
                inst = o["fn"](e)
                if o["dma"] is not None:
                    inst.then_inc(sems[("dma", o["dma"])], 16)
                elif o["signal"]:
                    inst.then_inc(sems[("tl", eng_name)], 1)
            if final:
                for k, n in dma_final.items():
                    e.wait_ge(sems[("dma", k)], 16 * n)

        @block.tensor
        def _(e):
            run("pe", e)

        @block.scalar
        def _(e):
            run("act", e)

        @block.vector
        def _(e):
            run("dve", e)

        @block.gpsimd
        def _(e):
            run("pool", e)

        @block.sync
        def _(e):
            run("sp", e, final=True)


class Ctx:
    def __init__(self):
        self.nc = bass.Bass("TRN2", target_bir_lowering=False)
        self.sc = Sched()
        self.stack = ExitStack()
        self.bank_ctr = 0
        self.rot = {}

    def din(self, name, shape, dt=F32):
        return self.nc.dram_tensor(name, list(shape), dt, kind="ExternalInput").ap()

    def dout(self, name, shape, dt=F32):
        return self.nc.dram_tensor(name, list(shape), dt, kind="ExternalOutput").ap()

    def sb(self, name, shape, dt):
        return self.stack.enter_context(self.nc.sbuf_tensor("s_" + name, list(shape), dt))

    def ps(self, name, shape, dt=F32):
        return self.stack.enter_context(self.nc.psum_tensor("p_" + name, list(shape), dt))

    def bank(self):
        b = self.bank_ctr % 8
        self.bank_ctr += 1
        return b

    def slot(self, name, n):
        v = self.rot.get(name, 0)
        self.rot[name] = v + 1
        return v % n

    def finish(self):
        self.sc.emit(self.nc, self.stack)
        self.stack.close()
        return self.nc


def alloc_common(cx):
    t = {}
    t["x"] = cx.sb("x", [128, 8, HALF], F32)
    t["h"] = cx.sb("h", [128, 8, HALF], BF16)
    t["act"] = cx.sb("act", [128, NFC, HALF], BF16)
    t["wa"] = [cx.sb(f"wa{i}", [128, 8, 256], BF16) for i in range(3)]
    t["wb"] = [cx.sb(f"wb{i}", [128, 8, 256], BF16) for i in range(3)]
    t["wd"] = [cx.sb(f"wd{i}", [128, 4, 512], BF16) for i in range(3)]
    t["sq"] = [cx.sb(f"sq{i}", [128, 512], BF16) for i in range(4)]
    t["std"] = cx.sb("std", [128, 512], F32)
    t["rstd"] = [cx.sb(f"rstd{i}", [128, 512], F32) for i in range(2)]
    t["tmpf"] = [cx.sb(f"tmpf{i}", [128, 512], F32) for i in range(4)]
    t["ones"] = cx.sb("ones", [128, 128], BF16)
    t["psum"] = cx.ps("psum", [128, 8, 512], F32)
    cx.sc.op("pool", lambda e: e.memset(t["ones"][:], 1.0), writes=[("ones",)])
    return t


def load_vec(cx, t, name, dram_ap, ncol):
    t[name] = cx.sb(name, [128, ncol], F32)
    cx.sc.op("sp", lambda e: e.dma_start(out=t[name][:], in_=dram_ap),
             writes=[(name,)], dma=name)


def rmsnorm_to_h(cx, t, gname, nchunk=8, src="x", dst="h", scale_d=D):
    sc = cx.sc
    ps = t["psum"]
    for b in range(NB):
        bs = slice(b * 512, (b + 1) * 512)
        bk = cx.bank()
        for c in range(nchunk):
            s = cx.slot("sq", 4)
            sc.op("act", lambda e, c=c, s=s, bs=bs: e.activation(
                out=t["sq"][s][:], in_=t[src][:, c, bs], func=AF.Square),
                reads=[(src, c, b)], writes=[("sq", s)])
            sc.op("pe", lambda e, c=c, s=s, bk=bk: e.matmul(
                ps[:, bk, :], lhsT=t["ones"][:], rhs=t["sq"][s][:],
                start=(c == 0), stop=(c == nchunk - 1)),
                reads=[("sq", s), ("ones",)], writes=[("ps", bk)])
        sc.op("act", lambda e, bk=bk: e.activation(
            out=t["std"][:], in_=ps[:, bk, :], func=AF.Sqrt, bias=t["eps"][:, 0:1],
            scale=1.0 / scale_d),
            reads=[("ps", bk), ("eps",)], writes=[("std",)])
        sc.op("dve", lambda e, b=b: e.reciprocal(out=t["rstd"][b][:], in_=t["std"][:]),
              reads=[("std",)], writes=[("rstd", b)])
        for c in range(nchunk):
            sc.op("dve", lambda e, c=c, b=b, bs=bs: e.scalar_tensor_tensor(
                out=t[dst][:, c, bs], in0=t[src][:, c, bs], scalar=t[gname][:, c:c + 1],
                in1=t["rstd"][b][:], op0=ALU.mult, op1=ALU.mult),
                reads=[(src, c, b), (gname,), ("rstd", b)], writes=[(dst, c, b)])


def ffn(cx, t, gname, wg, wu, wd):
    sc = cx.sc
    ps = t["psum"]
    rmsnorm_to_h(cx, t, gname)
    wgv = wg.rearrange("(c p) f -> p c f", p=128)
    wuv = wu.rearrange("(c p) f -> p c f", p=128)
    nslab = NFC // 2
    for s in range(nslab):
        sa = cx.slot("wa", 3)
        sb_ = cx.slot("wb", 3)
        sc.op("pool", lambda e, s=s, sa=sa: e.dma_start(
            out=t["wa"][sa][:], in_=wgv[:, :, 256 * s:256 * s + 256]),
            writes=[("wa", sa)], dma=f"wa{sa}")
        sc.op("pool", lambda e, s=s, sb_=sb_: e.dma_start(
            out=t["wb"][sb_][:], in_=wuv[:, :, 256 * s:256 * s + 256]),
            writes=[("wb", sb_)], dma=f"wb{sb_}")
        for f2 in range(2):
            fc = 2 * s + f2
            for b in range(NB):
                bs = slice(b * 512, (b + 1) * 512)
                bg = cx.bank()
                bu = cx.bank()
                for c in range(8):
                    sc.op("pe", lambda e, c=c, sa=sa, f2=f2, bg=bg, bs=bs: e.matmul(
                        ps[:, bg, :], lhsT=t["wa"][sa][:, c, 128 * f2:128 * f2 + 128],
                        rhs=t["h"][:, c, bs], start=(c == 0), stop=(c == 7)),
                        reads=[("wa", sa), ("h", c, b)], writes=[("ps", bg)])
                for c in range(8):
                    sc.op("pe", lambda e, c=c, sb_=sb_, f2=f2, bu=bu, bs=bs: e.matmul(
                        ps[:, bu, :], lhsT=t["wb"][sb_][:, c, 128 * f2:128 * f2 + 128],
                        rhs=t["h"][:, c, bs], start=(c == 0), stop=(c == 7)),
                        reads=[("wb", sb_), ("h", c, b)], writes=[("ps", bu)])
                ts = cx.slot("tmpf", 4)
                sc.op("act", lambda e, bg=bg, ts=ts: e.activation(
                    out=t["tmpf"][ts][:], in_=ps[:, bg, :], func=AF.Silu),
                    reads=[("ps", bg)], writes=[("tmpf", ts)])
                sc.op("dve", lambda e, bu=bu, ts=ts, fc=fc, bs=bs: e.tensor_tensor(
                    out=t["act"][:, fc, bs], in0=ps[:, bu, :], in1=t["tmpf"][ts][:],
                    op=ALU.mult),
                    reads=[("ps", bu), ("tmpf", ts)], writes=[("act", fc, b)])
    wdv = wd.rearrange("(s p) d -> p s d", p=128)
    for p_ in range(2):
        banks = [[cx.bank() for b in range(NB)] for jj in range(4)]
        nsl = (NFC + 3) // 4
        for s in range(nsl):
            nch = min(4, NFC - 4 * s)
            sd = cx.slot("wd", 3)
            sc.op("pool", lambda e, s=s, sd=sd, nch=nch, p_=p_: e.dma_start(
                out=t["wd"][sd][:, 0:nch, :],
                in_=wdv[:, 4 * s:4 * s + nch, 512 * p_:512 * p_ + 512]),
                writes=[("wd", sd)], dma=f"wd{sd}")
            for f4 in range(nch):
                fc = 4 * s + f4
                for jj in range(4):
                    for b in range(NB):
                        bs = slice(b * 512, (b + 1) * 512)
                        bk = banks[jj][b]
                        sc.op("pe", lambda e, sd=sd, f4=f4, jj=jj, bk=bk, fc=fc, bs=bs: e.matmul(
                            ps[:, bk, :], lhsT=t["wd"][sd][:, f4, 128 * jj:128 * jj + 128],
                            rhs=t["act"][:, fc, bs], start=(fc == 0), stop=(fc == NFC - 1)),
                            reads=[("wd", sd), ("act", fc, b)], writes=[("ps", bk)])
        for jj in range(4):
            j = 4 * p_ + jj
            for b in range(NB):
                bs = slice(b * 512, (b + 1) * 512)
                bk = banks[jj][b]
                sc.op("dve", lambda e, j=j, bk=bk, bs=bs: e.scalar_tensor_tensor(
                    out=t["x"][:, j, bs], in0=ps[:, bk, :], scalar=0.5,
                    in1=t["x"][:, j, bs], op0=ALU.mult, op1=ALU.add),
                    reads=[("ps", bk), ("x", j, b)], writes=[("x", j, b)])


def load_x(cx, t, x_dram, hf):
    xv = x_dram.rearrange("(c p) n -> p c n", p=128)
    cx.sc.op("sp", lambda e: e.dma_start(out=t["x"][:], in_=xv[:, :, hf * HALF:(hf + 1) * HALF]),
             writes=[("x", c, b) for c in range(8) for b in range(NB)], dma="xin")


def store_x(cx, t, x_dram, hf):
    xv = x_dram.rearrange("(c p) n -> p c n", p=128)
    cx.sc.op("sp", lambda e: e.dma_start(out=xv[:, :, hf * HALF:(hf + 1) * HALF], in_=t["x"][:]),
             reads=[("x", c, b) for c in range(8) for b in range(NB)], dma="xout")


def build_A():
    cx = Ctx()
    sc = cx.sc
    x_in = cx.din("x_in", [D, T])
    g1 = cx.din("g1", [128, 8])
    g2 = cx.din("g2", [128, 8])
    epsd = cx.din("epsd", [128, 1])
    wg = cx.din("wg", [D, DFF])
    wu = cx.din("wu", [D, DFF])
    wd = cx.din("wd", [DFF, D])
    win = cx.din("win", [D, 2048])
    winp = cx.din("winp", [D, 1024])
    cosd = cx.din("cosd", [128, T])
    sind = cx.din("sind", [128, T])
    x_out = cx.dout("x_out", [D, T])
    q_out = cx.dout("q_out", [512, T], BF16)
    k_out = cx.dout("k_out", [512, T], BF16)
    v_out = cx.dout("v_out", [T, 512], BF16)
    u_out = cx.dout("u_out", [512, T])

    t = alloc_common(cx)
    load_vec(cx, t, "g1", g1, 8)
    load_vec(cx, t, "g2", g2, 8)
    load_vec(cx, t, "eps", epsd, 1)
    t["cos"] = cx.sb("cos", [128, HALF], F32)
    t["sin"] = cx.sb("sin", [128, HALF], F32)
    t["stg16"] = [cx.sb(f"stg16_{i}", [128, 512], BF16) for i in range(4)]
    t["stg32"] = [cx.sb(f"stg32_{i}", [128, 512], F32) for i in range(2)]
    ps = t["psum"]
    winv = win.rearrange("(c p) f -> p c f", p=128)
    winpv = winp.rearrange("(c p) f -> p c f", p=128)

    for hf in range(2):
        load_x(cx, t, x_in, hf)
        ffn(cx, t, "g1", wg, wu, wd)
        store_x(cx, t, x_out, hf)
        rmsnorm_to_h(cx, t, "g2")
        sc.op("sp", lambda e, hf=hf: e.dma_start(out=t["cos"][:], in_=cosd[:, hf * HALF:(hf + 1) * HALF]),
              writes=[("cos",)], dma="cos")
        sc.op("sp", lambda e, hf=hf: e.dma_start(out=t["sin"][:], in_=sind[:, hf * HALF:(hf + 1) * HALF]),
              writes=[("sin",)], dma="sin")
        for s in range(4):
            sa = cx.slot("wa", 3)
            sb_ = cx.slot("wb", 3)
            sc.op("pool", lambda e, s=s, sa=sa: e.dma_start(
                out=t["wa"][sa][:], in_=winv[:, :, 256 * s:256 * s + 256]),
                writes=[("wa", sa)], dma=f"wa{sa}")
            sc.op("pool", lambda e, s=s, sb_=sb_: e.dma_start(
                out=t["wb"][sb_][:], in_=winpv[:, :, 256 * s:256 * s + 256]),
                writes=[("wb", sb_)], dma=f"wb{sb_}")
            for f2 in range(2):
                ch = 2 * s + f2
                dst = q_out if ch < 4 else k_out
                row0 = (ch % 4) * 128
                for b in range(NB):
                    bs = slice(b * 512, (b + 1) * 512)
                    b1 = cx.bank()
                    b2 = cx.bank()
                    for c in range(8):
                        sc.op("pe", lambda e, c=c, sa=sa, f2=f2, b1=b1, bs=bs: e.matmul(
                            ps[:, b1, :], lhsT=t["wa"][sa][:, c, 128 * f2:128 * f2 + 128],
                            rhs=t["h"][:, c, bs], start=(c == 0), stop=(c == 7)),
                            reads=[("wa", sa), ("h", c, b)], writes=[("ps", b1)])
                    for c in range(8):
                        sc.op("pe", lambda e, c=c, sb_=sb_, f2=f2, b2=b2, bs=bs: e.matmul(
                            ps[:, b2, :], lhsT=t["wb"][sb_][:, c, 128 * f2:128 * f2 + 128],
                            rhs=t["h"][:, c, bs], start=(c == 0), stop=(c == 7)),
                            reads=[("wb", sb_), ("h", c, b)], writes=[("ps", b2)])
                    s1 = cx.slot("tmpf", 4)
                    s2 = cx.slot("tmpf", 4)
                    so = cx.slot("stg16", 4)
                    sc.op("dve", lambda e, b1=b1, s1=s1, bs=bs: e.tensor_tensor(
                        out=t["tmpf"][s1][:], in0=ps[:, b1, :], in1=t["cos"][:, bs], op=ALU.mult),
                        reads=[("ps", b1), ("cos",)], writes=[("tmpf", s1)])
                    sc.op("dve", lambda e, b2=b2, s2=s2, bs=bs: e.tensor_tensor(
                        out=t["tmpf"][s2][:], in0=ps[:, b2, :], in1=t["sin"][:, bs], op=ALU.mult),
                        reads=[("ps", b2), ("sin",)], writes=[("tmpf", s2)])
                    sc.op("pool", lambda e, s1=s1, s2=s2, so=so: e.tensor_tensor(
                        out=t["stg16"][so][:], in0=t["tmpf"][s1][:], in1=t["tmpf"][s2][:], op=ALU.add),
                        reads=[("tmpf", s1), ("tmpf", s2)], writes=[("stg16", so)])
                    c0 = hf * HALF + b * 512
                    sc.op("sp", lambda e, dst=dst, row0=row0, c0=c0, so=so: e.dma_start(
                        out=dst[row0:row0 + 128, c0:c0 + 512], in_=t["stg16"][so][:]),
                        reads=[("stg16", so)], dma=f"stg16_{so}")
        for s in range(2):
            sa = cx.slot("wa", 3)
            sc.op("pool", lambda e, s=s, sa=sa: e.dma_start(
                out=t["wa"][sa][:], in_=winv[:, :, 1536 + 256 * s:1536 + 256 * s + 256]),
                writes=[("wa", sa)], dma=f"wa{sa}")
            for f2 in range(2):
                ch = 2 * s + f2
                for b in range(NB):
                    bs = slice(b * 512, (b + 1) * 512)
                    b1 = cx.bank()
                    for c in range(8):
                        sc.op("pe", lambda e, c=c, sa=sa, f2=f2, b1=b1, bs=bs: e.matmul(
                            ps[:, b1, :], lhsT=t["wa"][sa][:, c, 128 * f2:128 * f2 + 128],
                            rhs=t["h"][:, c, bs], start=(c == 0), stop=(c == 7)),
                            reads=[("wa", sa), ("h", c, b)], writes=[("ps", b1)])
                    so = cx.slot("stg32", 2)
                    sc.op("act", lambda e, b1=b1, so=so: e.activation(
                        out=t["stg32"][so][:], in_=ps[:, b1, :], func=AF.Copy),
                        reads=[("ps", b1)], writes=[("stg32", so)])
                    c0 = hf * HALF + b * 512
                    sc.op("sp", lambda e, ch=ch, c0=c0, so=so: e.dma_start(
                        out=u_out[ch * 128:ch * 128 + 128, c0:c0 + 512], in_=t["stg32"][so][:]),
                        reads=[("stg32", so)], dma=f"stg32_{so}")
        for s in range(2):
            sa = cx.slot("wa", 3)
            sc.op("pool", lambda e, s=s, sa=sa: e.dma_start(
                out=t["wa"][sa][:], in_=winv[:, :, 1024 + 256 * s:1024 + 256 * s + 256]),
                writes=[("wa", sa)], dma=f"wa{sa}")
            for tt in range(8):
                b = tt // 4
                b1 = cx.bank()
                for c in range(8):
                    sc.op("pe", lambda e, c=c, sa=sa, tt=tt, b1=b1: e.matmul(
                        ps[:, b1, 0:256], lhsT=t["h"][:, c, tt * 128:tt * 128 + 128],
                        rhs=t["wa"][sa][:, c, :], start=(c == 0), stop=(c == 7)),
                        reads=[("wa", sa), ("h", c, b)], writes=[("ps", b1)])
                so = cx.slot("stg16", 4)
                sc.op("act", lambda e, b1=b1, so=so: e.activation(
                    out=t["stg16"][so][:, 0:256], in_=ps[:, b1, 0:256], func=AF.Copy),
                    reads=[("ps", b1)], writes=[("stg16", so)])
                r0 = hf * HALF + tt * 128
                sc.op("sp", lambda e, r0=r0, s=s, so=so: e.dma_start(
                    out=v_out[r0:r0 + 128, 256 * s:256 * s + 256], in_=t["stg16"][so][:, 0:256]),
                    reads=[("stg16", so)], dma=f"stg16_{so}")
    return cx.finish()


def build_B():
    cx = Ctx()
    sc = cx.sc
    NQB = S // 512
    NKT = S // 128
    q_in = cx.din("q_in", [128, S], BF16)
    k_in = cx.din("k_in", [128, S // 2], BF16)
    v_in = cx.din("v_in", [128, NKT * 128], BF16)
    m_in = cx.din("m_in", [128, 4 * 512], BF16)
    o_out = cx.dout("o_out", [128, S])

    qT = cx.sb("qT", [128, S], BF16)
    kT = cx.sb("kT", [128, S // 2], BF16)
    pTs = [cx.sb(f"pTs{i}", [128, 512], BF16) for i in range(4)]
    vv = cx.sb("vv", [128, NKT * 128], BF16)
    mk = cx.sb("mk", [128, 4 * 512], BF16)
    mkv = mk[:].rearrange("p (j q) -> p j q", j=4)
    ones = cx.sb("ones", [128, 128], BF16)
    NP = 4
    pT = [cx.sb(f"pT{i}", [128, 2, 512], BF16) for i in range(NP)]
    rl = cx.sb("rl", [128, 512], F32)
    ostg = [cx.sb(f"ostg{i}", [128, 512], F32) for i in range(2)]
    ps = cx.ps("psum", [128, 8, 512], F32)

    sc.op("pool", lambda e: e.memset(ones[:], 1.0), writes=[("ones",)])
    NCH = 8
    cw = S // NCH
    for i in range(NCH):
        sc.op("sp", lambda e, i=i: e.dma_start(out=qT[:, i * cw:(i + 1) * cw], in_=q_in[:, i * cw:(i + 1) * cw]),
              writes=[("q", i)], dma=f"q{i}")
        sc.op("sp", lambda e, i=i: e.dma_start(out=kT[:, i * (cw // 2):(i + 1) * (cw // 2)],
                                               in_=k_in[:, i * (cw // 2):(i + 1) * (cw // 2)]),
              writes=[("k", i)], dma=f"k{i}")
        sc.op("sp", lambda e, i=i: e.dma_start(out=vv[:, i * cw:(i + 1) * cw], in_=v_in[:, i * cw:(i + 1) * cw]),
              writes=[("v", i)], dma=f"v{i}")
    sc.op("sp", lambda e: e.dma_start(out=mk[:], in_=m_in), writes=[("mk",)], dma="mk")

    groups = [(qb, g) for qb in range(NQB) for g in range(2 * (qb + 1))]
    LOOK = 2
    ginfo = {}

    def emit_S(i):
        qb, g = groups[i]
        b0 = 2 * (i % 2)
        sl = i % NP
        ginfo[i] = (b0, sl)
        for t2 in range(2):
            sc.op("pe", lambda e, t2=t2: e.matmul(
                ps[0:128, b0 + t2, :], lhsT=kT[64 * t2:64 * t2 + 64, g * 128:(g + 1) * 128],
                rhs=qT[64 * t2:64 * t2 + 64, qb * 512:(qb + 1) * 512], start=True, stop=True),
                reads=[("k", g * 128 // (cw // 2)), ("q", qb * 512 // cw)], writes=[("ps", b0 + t2)])
        sc.op("act", lambda e: e.activation(out=pT[sl][:], in_=ps[:, b0:b0 + 2, :], func=AF.Exp, scale=0.125),
              reads=[("ps", b0), ("ps", b0 + 1)], writes=[("pT", sl)])
        j2 = g - 2 * qb
        if j2 >= 0:
            sc.op("dve", lambda e: e.tensor_tensor(out=pT[sl][:], in0=pT[sl][:],
                                                  in1=mkv[:, j2 * 2:j2 * 2 + 2, :], op=ALU.mult),
                  reads=[("pT", sl), ("mk",)], writes=[("pT", sl)])

    def emit_PV(i):
        qb, g = groups[i]
        b0, sl = ginfo[i]
        ob = 4 + (qb % 2)
        lb = 6 + (qb % 2)
        ng = 2 * (qb + 1)
        for t2 in range(2):
            kt = 2 * g + t2
            sc.op("pe", lambda e, kt=kt, t2=t2: e.matmul(
                ps[:, ob, :], lhsT=vv[:, kt * 128:(kt + 1) * 128], rhs=pT[sl][:, t2, :],
                start=(kt == 0), stop=(kt == 2 * ng - 1)),
                reads=[("pT", sl), ("v", kt * 128 // cw)], writes=[("ps", ob)])
        s2 = i % 4
        sc.op("dve", lambda e: e.tensor_tensor(out=pTs[s2][:], in0=pT[sl][:, 0, :], in1=pT[sl][:, 1, :], op=ALU.add),
              reads=[("pT", sl)], writes=[("pTs", s2)])

    def emit_L(i):
        qb, g = groups[i]
        ob = 4 + (qb % 2)
        lb = 6 + (qb % 2)
        ng = 2 * (qb + 1)
        s2 = i % 4
        sc.op("pe", lambda e: e.matmul(
            ps[:, lb, :], lhsT=ones[:], rhs=pTs[s2][:], start=(g == 0), stop=(g == ng - 1)),
            reads=[("pTs", s2), ("ones",)], writes=[("ps", lb)])
        if g == ng - 1:
            so = qb % 2
            sc.op("dve", lambda e: e.reciprocal(out=rl[:], in_=ps[:, lb, :]),
                  reads=[("ps", lb)], writes=[("rl",)])
            sc.op("dve", lambda e: e.tensor_tensor(out=ostg[so][:], in0=ps[:, ob, :], in1=rl[:], op=ALU.mult),
                  reads=[("ps", ob), ("rl",)], writes=[("ostg", so)])
            sc.op("sp", lambda e: e.dma_start(out=o_out[:, qb * 512:(qb + 1) * 512], in_=ostg[so][:]),
                  reads=[("ostg", so)], dma=f"ostg{so}")

    n = len(groups)
    for i in range(n + LOOK + 1):
        if i < n:
            emit_S(i)
        if 0 <= i - LOOK < n:
            emit_PV(i - LOOK)
        if 0 <= i - LOOK - 1 < n:
            emit_L(i - LOOK - 1)
    return cx.finish()


def build_C():
    cx = Ctx()
    sc = cx.sc
    x_in = cx.din("x_in", [D, T])
    o_in = cx.din("o_in", [8 * 128, T])
    u_in = cx.din("u_in", [512, 16 + T])
    icnt = cx.din("icnt", [128, 4 * 16])
    lam4 = cx.din("lam4", [128, 4 * 64])
    lconst = cx.din("lconst", [128, 2])
    sgain = cx.din("sgain", [128, 1])
    pw = cx.din("pw", [4 * 128, 128])
    pscale = cx.din("pscale", [128, 4])
    wout = cx.din("wout", [D, D])
    g3 = cx.din("g3", [128, 8])
    gf = cx.din("gf", [128, 8])
    epsd = cx.din("epsd", [128, 1])
    wg = cx.din("wg", [D, DFF])
    wu = cx.din("wu", [D, DFF])
    wd = cx.din("wd", [DFF, D])
    x_out = cx.dout("x_out", [D, T])
    y_out = cx.dout("y_out", [D, T])

    t = alloc_common(cx)
    ps = t["psum"]
    for nm, ap, n in (("g3", g3, 8), ("gf", gf, 8), ("eps", epsd, 1), ("lam4", lam4, 256),
                      ("lconst", lconst, 2), ("sgain", sgain, 1), ("pscale", pscale, 4),
                      ("icnt", icnt, 64)):
        load_vec(cx, t, nm, ap, n)
    t["lt"] = cx.sb("lt", [128, 128], F32)
    t["ls"] = cx.sb("ls", [128, 8], F32)
    sc.op("dve", lambda e: e.tensor_tensor(out=t["lt"][:, 0:64], in0=t["lam4"][:, 0:64],
                                           in1=t["lam4"][:, 64:128], op=ALU.mult),
          reads=[("lam4",)], writes=[("lt", 0)])
    sc.op("dve", lambda e: e.tensor_tensor(out=t["lt"][:, 64:128], in0=t["lam4"][:, 128:192],
                                           in1=t["lam4"][:, 192:256], op=ALU.mult),
          reads=[("lam4",)], writes=[("lt", 1)])
    sc.op("dve", lambda e: e.tensor_reduce(out=t["ls"][:, 0:1], in_=t["lt"][:, 0:64],
                                           axis=mybir.AxisListType.X, op=ALU.add),
          reads=[("lt", 0)], writes=[("ls", 0)])
    sc.op("dve", lambda e: e.tensor_reduce(out=t["ls"][:, 1:2], in_=t["lt"][:, 64:128],
                                           axis=mybir.AxisListType.X, op=ALU.add),
          reads=[("lt", 1)], writes=[("ls", 1)])
    sc.op("act", lambda e: e.activation(out=t["ls"][:, 2:4], in_=t["ls"][:, 0:2], func=AF.Exp),
          reads=[("ls", 0), ("ls", 1)], writes=[("ls", 2)])
    sc.op("dve", lambda e: e.tensor_tensor(out=t["ls"][:, 4:5], in0=t["ls"][:, 3:4], in1=t["ls"][:, 2:3],
                                           op=ALU.subtract),
          reads=[("ls", 2)], writes=[("ls", 4)])
    sc.op("dve", lambda e: e.tensor_tensor(out=t["ls"][:, 5:6], in0=t["ls"][:, 4:5], in1=t["lconst"][:, 0:1],
                                           op=ALU.subtract),
          reads=[("ls", 4), ("lconst",)], writes=[("neglam",)])
    sc.op("dve", lambda e: e.tensor_tensor(out=t["ls"][:, 6:7], in0=t["sgain"][:, 0:1], in1=t["lconst"][:, 1:2],
                                           op=ALU.mult),
          reads=[("sgain",), ("lconst",)], writes=[("sg",)])

    t["o1"] = [cx.sb(f"o1_{i}", [128, 512], F32) for i in range(2)]
    t["o2"] = [cx.sb(f"o2_{i}", [128, 512], F32) for i in range(2)]
    t["od"] = [cx.sb(f"od_{i}", [128, 512], F32) for i in range(2)]
    t["uu"] = cx.sb("uu", [128, 16 + HALF], F32)
    t["ua"] = cx.sb("ua", [128, 16 + HALF], F32)
    t["ub"] = cx.sb("ub", [128, 16 + HALF], F32)
    t["dif"] = cx.sb("dif", [128, HALF], BF16)
    t["pwt"] = cx.sb("pwt", [128, 4, 128], BF16)
    sc.op("pool", lambda e: e.dma_start(out=t["pwt"][:], in_=pw.rearrange("(g c) e -> c g e", c=128)),
          writes=[("pwt",)], dma="pwt")
    woutv = wout.rearrange("(c p) f -> p c f", p=128)
    WINS = (2, 4, 8, 16)

    for hf in range(2):
        load_x(cx, t, x_in, hf)
        for hd in range(4):
            for b in range(NB):
                bs = slice(b * 512, (b + 1) * 512)
                c0 = hf * HALF + b * 512
                s1 = cx.slot("o1", 2)
                s2 = cx.slot("o2", 2)
                sd_ = cx.slot("od", 2)
                sc.op("sp", lambda e, hd=hd, c0=c0, s1=s1: e.dma_start(
                    out=t["o1"][s1][:], in_=o_in[(2 * hd) * 128:(2 * hd) * 128 + 128, c0:c0 + 512]),
                    writes=[("o1", s1)], dma=f"o1_{s1}")
                sc.op("sp", lambda e, hd=hd, c0=c0, s2=s2: e.dma_start(
                    out=t["o2"][s2][:], in_=o_in[(2 * hd + 1) * 128:(2 * hd + 1) * 128 + 128, c0:c0 + 512]),
                    writes=[("o2", s2)], dma=f"o2_{s2}")
                sc.op("dve", lambda e, s1=s1, s2=s2, sd_=sd_: e.scalar_tensor_tensor(
                    out=t["od"][sd_][:], in0=t["o2"][s2][:], scalar=t["ls"][:, 5:6], in1=t["o1"][s1][:],
                    op0=ALU.mult, op1=ALU.add),
                    reads=[("o1", s1), ("o2", s2), ("neglam",)], writes=[("od", sd_)])
                sq = cx.slot("sq", 4)
                sc.op("act", lambda e, sd_=sd_, sq=sq: e.activation(
                    out=t["sq"][sq][:], in_=t["od"][sd_][:], func=AF.Square),
                    reads=[("od", sd_)], writes=[("sq", sq)])
                bk = cx.bank()
                sc.op("pe", lambda e, sq=sq, bk=bk: e.matmul(
                    ps[:, bk, :], lhsT=t["ones"][:], rhs=t["sq"][sq][:], start=True, stop=True),
                    reads=[("sq", sq), ("ones",)], writes=[("ps", bk)])
                sc.op("act", lambda e, bk=bk: e.activation(
                    out=t["std"][:], in_=ps[:, bk, :], func=AF.Sqrt, bias=t["eps"][:, 0:1], scale=1.0 / 128),
                    reads=[("ps", bk), ("eps",)], writes=[("std",)])
                rs = cx.slot("rstd", 2)
                sc.op("dve", lambda e, rs=rs: e.reciprocal(out=t["rstd"][rs][:], in_=t["std"][:]),
                      reads=[("std",)], writes=[("rstd", rs)])
                sc.op("dve", lambda e, hd=hd, bs=bs, sd_=sd_, rs=rs: e.scalar_tensor_tensor(
                    out=t["h"][:, hd, bs], in0=t["od"][sd_][:], scalar=t["ls"][:, 6:7],
                    in1=t["rstd"][rs][:], op0=ALU.mult, op1=ALU.mult),
                    reads=[("od", sd_), ("sg",), ("rstd", rs)], writes=[("h", hd, b)])
        for g in range(4):
            w = WINS[g]
            c0 = hf * HALF
            sc.op("sp", lambda e, g=g, c0=c0: e.dma_start(
                out=t["uu"][:], in_=u_in[g * 128:g * 128 + 128, c0:c0 + 16 + HALF]),
                writes=[("uu",)], dma="uu")
            src = "uu"
            sh = 1
            flip = 0
            while sh < w:
                dst = "ua" if flip == 0 else "ub"
                sc.op("dve", lambda e, src=src, dst=dst, sh=sh: e.tensor_tensor(
                    out=t[dst][:, sh:16 + HALF], in0=t[src][:, sh:16 + HALF],
                    in1=t[src][:, 0:16 + HALF - sh], op=ALU.add),
                    reads=[(src,)], writes=[(dst,)])
                src = dst
                flip ^= 1
                sh *= 2
            oth = "ua" if src == "ub" else "ub"
            sc.op("dve", lambda e, src=src, w=w: e.scalar_tensor_tensor(
                out=t["dif"][:, 0:HALF], in0=t[src][:, 16:16 + HALF], scalar=1.0 / w,
                in1=t["uu"][:, 16:16 + HALF], op0=ALU.mult, op1=ALU.subtract),
                reads=[(src,), ("uu",)], writes=[("dif",)])
            if hf == 0:
                sc.op("dve", lambda e, src=src, oth=oth, g=g: e.tensor_tensor(
                    out=t[oth][:, 16:32], in0=t[src][:, 16:32],
                    in1=t["icnt"][:, g * 16:g * 16 + 16], op=ALU.mult),
                    reads=[(src,), ("icnt",)], writes=[(oth,)])
                sc.op("dve", lambda e, oth=oth: e.tensor_tensor(
                    out=t["dif"][:, 0:16], in0=t[oth][:, 16:32], in1=t["uu"][:, 16:32], op=ALU.subtract),
                    reads=[(oth,), ("uu",), ("dif",)], writes=[("dif",)])
            for b in range(NB):
                bs = slice(b * 512, (b + 1) * 512)
                bk = cx.bank()
                sc.op("pe", lambda e, g=g, bk=bk, bs=bs: e.matmul(
                    ps[:, bk, :], lhsT=t["pwt"][:, g, :], rhs=t["dif"][:, bs], start=True, stop=True),
                    reads=[("pwt",), ("dif",)], writes=[("ps", bk)])
                sc.op("dve", lambda e, g=g, bk=bk, bs=bs: e.tensor_scalar(
                    out=t["h"][:, 4 + g, bs], in0=ps[:, bk, :], scalar1=t["pscale"][:, g:g + 1],
                    scalar2=None, op0=ALU.mult),
                    reads=[("ps", bk), ("pscale",)], writes=[("h", 4 + g, b)])
        for s in range(4):
            sa = cx.slot("wa", 3)
            sc.op("pool", lambda e, s=s, sa=sa: e.dma_start(
                out=t["wa"][sa][:], in_=woutv[:, :, 256 * s:256 * s + 256]),
                writes=[("wa", sa)], dma=f"wa{sa}")
            for f2 in range(2):
                j = 2 * s + f2
                for b in range(NB):
                    bs = slice(b * 512, (b + 1) * 512)
                    bk = cx.bank()
                    for c in range(8):
                        sc.op("pe", lambda e, c=c, sa=sa, f2=f2, bk=bk, bs=bs: e.matmul(
                            ps[:, bk, :], lhsT=t["wa"][sa][:, c, 128 * f2:128 * f2 + 128],
                            rhs=t["h"][:, c, bs], start=(c == 0), stop=(c == 7)),
                            reads=[("wa", sa), ("h", c, b)], writes=[("ps", bk)])
                    sc.op("dve", lambda e, j=j, bk=bk, bs=bs: e.tensor_tensor(
                        out=t["x"][:, j, bs], in0=ps[:, bk, :], in1=t["x"][:, j, bs], op=ALU.add),
                        reads=[("ps", bk), ("x", j, b)], writes=[("x", j, b)])
        ffn(cx, t, "g3", wg, wu, wd)
        store_x(cx, t, x_out, hf)
        rmsnorm_final(cx, t, y_out, hf)
    return cx.finish()


def rmsnorm_final(cx, t, y_out, hf):
    sc = cx.sc
    ps = t["psum"]
    yv = y_out.rearrange("(c p) n -> p c n", p=128)
    for b in range(NB):
        bs = slice(b * 512, (b + 1) * 512)
        bk = cx.bank()
        for c in range(8):
            s = cx.slot("sq", 4)
            sc.op("act", lambda e, c=c, s=s, bs=bs: e.activation(
                out=t["sq"][s][:], in_=t["x"][:, c, bs], func=AF.Square),
                reads=[("x", c, b)], writes=[("sq", s)])
            sc.op("pe", lambda e, c=c, s=s, bk=bk: e.matmul(
                ps[:, bk, :], lhsT=t["ones"][:], rhs=t["sq"][s][:], start=(c == 0), stop=(c == 7)),
                reads=[("sq", s), ("ones",)], writes=[("ps", bk)])
        sc.op("act", lambda e, bk=bk: e.activation(
            out=t["std"][:], in_=ps[:, bk, :], func=AF.Sqrt, bias=t["eps"][:, 0:1], scale=1.0 / D),
            reads=[("ps", bk), ("eps",)], writes=[("std",)])
        sc.op("dve", lambda e, b=b: e.reciprocal(out=t["rstd"][b][:], in_=t["std"][:]),
              reads=[("std",)], writes=[("rstd", b)])
        for c in range(8):
            so = cx.slot("tmpf", 4)
            sc.op("dve", lambda e, c=c, b=b, bs=bs, so=so: e.scalar_tensor_tensor(
                out=t["tmpf"][so][:], in0=t["x"][:, c, bs], scalar=t["gf"][:, c:c + 1],
                in1=t["rstd"][b][:], op0=ALU.mult, op1=ALU.mult),
                reads=[("x", c, b), ("gf",), ("rstd", b)], writes=[("tmpf", so)])
            c0 = hf * HALF + b * 512
            sc.op("sp", lambda e, c=c, c0=c0, so=so: e.dma_start(
                out=yv[:, c, c0:c0 + 512], in_=t["tmpf"][so][:]),
                reads=[("tmpf", so)], dma=f"tmpf{so}")


_PROGS = {}


def _prog(name):
    if name not in _PROGS:
        _PROGS[name] = {"A": build_A, "B": build_B, "C": build_C}[name]()
    return _PROGS[name]


def _run(name, in_maps):
    res = run_bass_kernel_spmd(_prog(name), in_maps, core_ids=list(range(NCORES)))
    return res.results


def _vec8(g):
    return np.ascontiguousarray(np.asarray(g, np.float32).reshape(8, 128).T)


def _rope_tables():
    d = 64
    inv = (10000.0 ** (-np.arange(0, d, 2, dtype=np.float32) / d)).astype(np.float32)
    ang = np.arange(S, dtype=np.float32)[:, None] * inv[None, :]
    ang = np.concatenate([ang, ang], axis=-1)
    cos = np.cos(ang).astype(np.float32).T
    sin = np.sin(ang).astype(np.float32).T
    sin[:32] *= -1.0
    cos2 = np.concatenate([cos, cos], axis=0)
    sin2 = np.concatenate([sin, sin], axis=0)
    return cos2, sin2


def kernel(x, ffn1_norm, ffn1_w_gate, ffn1_w_up, ffn1_w_down, mix_norm, w_in,
           lambda_q1, lambda_k1, lambda_q2, lambda_k2, subln_gain, pool_w, pool_scale,
           w_out, ffn2_norm, ffn2_w_gate, ffn2_w_up, ffn2_w_down, final_norm):
    f = lambda a: np.ascontiguousarray(np.asarray(a, dtype=np.float32))
    x = f(x)
    xT = [np.ascontiguousarray(x[0, c * T:(c + 1) * T, :].T) for c in range(NCORES)]
    cos2, sin2 = _rope_tables()
    epsd = np.full((128, 1), EPS, np.float32)
    kp = np.arange(128)[:, None, None] + 128 * np.arange(4)[None, :, None]
    qq = np.arange(512)[None, None, :]
    mask = (kp <= qq).astype(np.float32).reshape(128, 4 * 512).astype(ml_dtypes.bfloat16)
    perm = (np.arange(1024).reshape(16, 2, 32)[:, ::-1, :]).reshape(-1)
    WINS = (2, 4, 8, 16)
    y = None
    for l in range(DEPTH):
        wl = f(w_in[l])
        winp = np.ascontiguousarray(wl[:, :1024][:, perm])
        ins = []
        for c in range(NCORES):
            ins.append(dict(x_in=xT[c], g1=_vec8(ffn1_norm[l]), g2=_vec8(mix_norm[l]), epsd=epsd,
                            wg=f(ffn1_w_gate[l]), wu=f(ffn1_w_up[l]), wd=f(ffn1_w_down[l]),
                            win=wl, winp=winp,
                            cosd=np.ascontiguousarray(cos2[:, c * T:(c + 1) * T]),
                            sind=np.ascontiguousarray(sin2[:, c * T:(c + 1) * T])))
        ra = _run("A", ins)
        xT = [ra[c]["x_out"] for c in range(NCORES)]
        qT = np.concatenate([ra[c]["q_out"] for c in range(NCORES)], axis=1)
        kT = np.concatenate([ra[c]["k_out"] for c in range(NCORES)], axis=1)
        v = np.concatenate([ra[c]["v_out"] for c in range(NCORES)], axis=0)
        uT = np.concatenate([ra[c]["u_out"] for c in range(NCORES)], axis=1)
        ins = []
        for c in range(NCORES):
            h, m = c // 2, c % 2
            vh = v[:, h * 128:(h + 1) * 128].reshape(S // 128, 128, 128).transpose(1, 0, 2)
            qc = qT[c * 64:(c + 1) * 64]
            kc = kT[c * 64:(c + 1) * 64].reshape(64, S // 256, 2, 128).transpose(2, 0, 1, 3)
            ins.append(dict(q_in=np.ascontiguousarray(np.concatenate([qc, qc], axis=0)),
                            k_in=np.ascontiguousarray(kc).reshape(128, S // 2),
                            v_in=np.ascontiguousarray(vh).reshape(128, -1),
                            m_in=mask))
        rb = _run("B", ins)
        oT = np.concatenate([rb[c]["o_out"] for c in range(NCORES)], axis=0)
        lam_init = 0.8 - 0.6 * math.exp(-0.3 * l)
        lam4 = np.concatenate([f(lambda_q1[l]), f(lambda_k1[l]), f(lambda_q2[l]), f(lambda_k2[l])])
        lam4 = np.ascontiguousarray(np.broadcast_to(lam4[None, :], (128, 256)))
        lconst = np.ascontiguousarray(np.broadcast_to(
            np.array([lam_init, 1.0 - lam_init], np.float32)[None, :], (128, 2)))
        upad = np.concatenate([np.zeros((512, 16), np.float32), uT], axis=1)
        ins = []
        for c in range(NCORES):
            icnt = np.zeros((128, 64), np.float32)
            for g in range(4):
                pos = c * T + np.arange(16)
                icnt[:, g * 16:(g + 1) * 16] = (1.0 / np.minimum(pos + 1, WINS[g]))[None, :]
            ins.append(dict(x_in=xT[c], o_in=np.ascontiguousarray(oT[:, c * T:(c + 1) * T]),
                            u_in=np.ascontiguousarray(upad[:, c * T:c * T + 16 + T]),
                            icnt=icnt, lam4=lam4, lconst=lconst,
                            sgain=f(subln_gain[l]).reshape(128, 1),
                            pw=f(pool_w[l]).reshape(512, 128),
                            pscale=np.ascontiguousarray(f(pool_scale[l]).reshape(4, 128).T),
                            wout=f(w_out[l]), g3=_vec8(ffn2_norm[l]), gf=_vec8(final_norm), epsd=epsd,
                            wg=f(ffn2_w_gate[l]), wu=f(ffn2_w_up[l]), wd=f(ffn2_w_down[l])))
        rc = _run("C", ins)
        xT = [rc[c]["x_out"] for c in range(NCORES)]
        y = [rc[c]["y_out"] for c in range(NCORES)]
    out = np.concatenate([yc.T for yc in y], axis=0)[None]
    return np.ascontiguousarray(out.astype(np.float32))
```

```python
import math
from contextlib import ExitStack

import numpy as np
import ml_dtypes

import concourse.bass as bass
import concourse.mybir as mybir
from concourse.bass_utils import run_bass_kernel_spmd

F32 = mybir.dt.float32
BF16 = mybir.dt.bfloat16
AF = mybir.ActivationFunctionType
ALU = mybir.AluOpType

NCORES = 8
D = 1024
S = 16384
DEPTH = 4
DFF = 2816
NFC = DFF // 128
T = S // NCORES
HALF = 1024
NB = 2
EPS = 1e-6
SAME_ENGINE_SYNC = True


class Sched:
    ENGS = ("pe", "act", "dve", "pool", "sp")

    def __init__(self):
        self.ops = []
        self.last_writer = {}
        self.readers = {}
        self.dma_count = {}

    def op(self, eng, fn, reads=(), writes=(), dma=None):
        idx = len(self.ops)
        deps = set()
        for r in reads:
            if r in self.last_writer:
                deps.add(self.last_writer[r])
        for w in writes:
            if w in self.last_writer:
                deps.add(self.last_writer[w])
            for rd in self.readers.get(w, ()):
                deps.add(rd)
        best = {}
        for d in deps:
            od = self.ops[d]
            k = ("dma", od["dma"]) if od["dma"] is not None else ("eng", od["eng"])
            if k not in best or d > best[k]:
                best[k] = d
        deps = set(best.values())
        o = dict(eng=eng, fn=fn, deps=deps, dma=dma, signal=False, sig_idx=None,
                 dma_val=None)
        if dma is not None:
            self.dma_count[dma] = self.dma_count.get(dma, 0) + 1
            o["dma_val"] = 16 * self.dma_count[dma]
        self.ops.append(o)
        for w in writes:
            self.last_writer[w] = idx
            self.readers[w] = []
        for r in reads:
            self.readers.setdefault(r, []).append(idx)
        return idx

    def finalize(self):
        ops = self.ops
        for o in ops:
            for d in o["deps"]:
                od = ops[d]
                if od["dma"] is None:
                    if od["eng"] != o["eng"] or (SAME_ENGINE_SYNC and od["eng"] != "pe"
                                                 and o["dma"] is None):
                        od["signal"] = True
                    elif o["dma"] is not None and od["eng"] == o["eng"]:
                        od["signal"] = True
        cnt = {e: 0 for e in self.ENGS}
        for o in ops:
            if o["signal"]:
                cnt[o["eng"]] += 1
                o["sig_idx"] = cnt[o["eng"]]
        waited = {e: {} for e in self.ENGS}
        for o in ops:
            w = {}
            for d in o["deps"]:
                od = ops[d]
                if od["dma"] is not None:
                    key = ("dma", od["dma"])
                    val = od["dma_val"]
                else:
                    if not od["signal"]:
                        continue
                    key = ("tl", od["eng"])
                    val = od["sig_idx"]
                w[key] = max(w.get(key, 0), val)
            wl = []
            for key, val in w.items():
                if waited[o["eng"]].get(key, 0) >= val:
                    continue
                waited[o["eng"]][key] = val
                wl.append((key, val))
            o["waits"] = wl

    def emit(self, nc, stack):
        self.finalize()
        sems = {}
        for e in self.ENGS:
            sems[("tl", e)] = stack.enter_context(nc.semaphore("tl_" + e))
        for k in self.dma_count:
            sems[("dma", k)] = stack.enter_context(nc.semaphore("dma_" + str(k)))
        block = stack.enter_context(nc.Block())
        ops = self.ops
        dma_final = dict(self.dma_count)

        def run(eng_name, e, final=False):
            for o in ops:
                if o["eng"] != eng_name:
                    continue
                for key, val in o["waits"]:
                    e.wait_ge(sems[key], val)
                inst = o["fn"](e)
                if o["dma"] is not None:
                    inst.then_inc(sems[("dma", o["dma"])], 16)
                elif o["signal"]:
                    inst.then_inc(sems[("tl", eng_name)], 1)
            if final:
                for k, n in dma_final.items():
                    e.wait_ge(sems[("dma", k)], 16 * n)

        @block.tensor
        def _(e):
            run("pe", e)

        @block.scalar
        def _(e):
            run("act", e)

        @block.vector
        def _(e):
            run("dve", e)

        @block.gpsimd
        def _(e):
            run("pool", e)

        @block.sync
        def _(e):
            run("sp", e, final=True)


class Ctx:
    def __init__(self):
        self.nc = bass.Bass("TRN2", target_bir_lowering=False)
        self.sc = Sched()
        self.stack = ExitStack()
        self.bank_ctr = 0
        self.rot = {}

    def din(self, name, shape, dt=F32):
        return self.nc.dram_tensor(name, list(shape), dt, kind="ExternalInput").ap()

    def dout(self, name, shape, dt=F32):
        return self.nc.dram_tensor(name, list(shape), dt, kind="ExternalOutput").ap()

    def sb(self, name, shape, dt):
        return self.stack.enter_context(self.nc.sbuf_tensor("s_" + name, list(shape), dt))

    def ps(self, name, shape, dt=F32):
        return self.stack.enter_context(self.nc.psum_tensor("p_" + name, list(shape), dt))

    def bank(self):
        b = self.bank_ctr % 8
        self.bank_ctr += 1
        return b

    def slot(self, name, n):
        v = self.rot.get(name, 0)
        self.rot[name] = v + 1
        return v % n

    def finish(self):
        self.sc.emit(self.nc, self.stack)
        self.stack.close()
        return self.nc


def alloc_common(cx):
    t = {}
    t["x"] = cx.sb("x", [128, 8, HALF], F32)
    t["h"] = cx.sb("h", [128, 8, HALF], BF16)
    t["act"] = cx.sb("act", [128, NFC, HALF], BF16)
    t["wa"] = [cx.sb(f"wa{i}", [128, 8, 256], BF16) for i in range(3)]
    t["wb"] = [cx.sb(f"wb{i}", [128, 8, 256], BF16) for i in range(3)]
    t["wd"] = [cx.sb(f"wd{i}", [128, 4, 512], BF16) for i in range(3)]
    t["sq"] = [cx.sb(f"sq{i}", [128, 512], BF16) for i in range(4)]
    t["std"] = [cx.sb(f"std{i}", [128, 512], F32) for i in range(2)]
    t["rscr"] = cx.sb("rscr", [128, 512], F32)
    t["rstd"] = [cx.sb(f"rstd{i}", [128, 512], F32) for i in range(2)]
    t["tmpf"] = [cx.sb(f"tmpf{i}", [128, 512], F32) for i in range(4)]
    t["ones"] = cx.sb("ones", [128, 128], BF16)
    t["psum"] = cx.ps("psum", [128, 8, 512], F32)
    cx.sc.op("pool", lambda e: e.memset(t["ones"][:], 1.0), writes=[("ones",)])
    return t


def load_vec(cx, t, name, dram_ap, ncol):
    t[name] = cx.sb(name, [128, ncol], F32)
    cx.sc.op("sp", lambda e: e.dma_start(out=t[name][:], in_=dram_ap),
             writes=[(name,)], dma=name)


def rmsnorm_to_h(cx, t, gname, nchunk=8, src="x", dst="h", scale_d=D):
    sc = cx.sc
    ps = t["psum"]
    for b in range(NB):
        bs = slice(b * 512, (b + 1) * 512)
        bk = cx.bank()
        for c in range(nchunk):
            s = cx.slot("sq", 4)
            sc.op("act", lambda e, c=c, s=s, bs=bs: e.activation(
                out=t["sq"][s][:], in_=t[src][:, c, bs], func=AF.Square),
                reads=[(src, c, b)], writes=[("sq", s)])
            sc.op("pe", lambda e, c=c, s=s, bk=bk: e.matmul(
                ps[:, bk, :], lhsT=t["ones"][:], rhs=t["sq"][s][:],
                start=(c == 0), stop=(c == nchunk - 1)),
                reads=[("sq", s), ("ones",)], writes=[("ps", bk)])
        ss = cx.slot("std", 2)
        sc.op("act", lambda e, bk=bk, ss=ss: e.activation(
            out=t["std"][ss][:], in_=ps[:, bk, :], func=AF.Sqrt, bias=t["eps"][:, 0:1],
            scale=1.0 / scale_d),
            reads=[("ps", bk), ("eps",)], writes=[("std", ss)])
        sc.op("dve", lambda e, b=b, ss=ss: e.reciprocal(
            out=t["rstd"][b][:], in_=t["std"][ss][:]),
              reads=[("std", ss)], writes=[("rstd", b), ("rscr",)])
        for c in range(nchunk):
            sc.op("dve", lambda e, c=c, b=b, bs=bs: e.scalar_tensor_tensor(
                out=t[dst][:, c, bs], in0=t[src][:, c, bs], scalar=t[gname][:, c:c + 1],
                in1=t["rstd"][b][:], op0=ALU.mult, op1=ALU.mult),
                reads=[(src, c, b), (gname,), ("rstd", b)], writes=[(dst, c, b)])


def ffn(cx, t, gname, wg, wu, wd):
    sc = cx.sc
    ps = t["psum"]
    rmsnorm_to_h(cx, t, gname)
    wgv = wg.rearrange("(c p) f -> p c f", p=128)
    wuv = wu.rearrange("(c p) f -> p c f", p=128)
    nslab = NFC // 2
    for s in range(nslab):
        sa = cx.slot("wa", 3)
        sb_ = cx.slot("wb", 3)
        sc.op("pool", lambda e, s=s, sa=sa: e.dma_start(
            out=t["wa"][sa][:], in_=wgv[:, :, 256 * s:256 * s + 256]),
            writes=[("wa", sa)], dma=f"wa{sa}")
        sc.op("pool", lambda e, s=s, sb_=sb_: e.dma_start(
            out=t["wb"][sb_][:], in_=wuv[:, :, 256 * s:256 * s + 256]),
            writes=[("wb", sb_)], dma=f"wb{sb_}")
        for f2 in range(2):
            fc = 2 * s + f2
            for b in range(NB):
                bs = slice(b * 512, (b + 1) * 512)
                bg = cx.bank()
                bu = cx.bank()
                for c in range(8):
                    sc.op("pe", lambda e, c=c, sa=sa, f2=f2, bg=bg, bs=bs: e.matmul(
                        ps[:, bg, :], lhsT=t["wa"][sa][:, c, 128 * f2:128 * f2 + 128],
                        rhs=t["h"][:, c, bs], start=(c == 0), stop=(c == 7)),
                        reads=[("wa", sa), ("h", c, b)], writes=[("ps", bg)])
                for c in range(8):
                    sc.op("pe", lambda e, c=c, sb_=sb_, f2=f2, bu=bu, bs=bs: e.matmul(
                        ps[:, bu, :], lhsT=t["wb"][sb_][:, c, 128 * f2:128 * f2 + 128],
                        rhs=t["h"][:, c, bs], start=(c == 0), stop=(c == 7)),
                        reads=[("wb", sb_), ("h", c, b)], writes=[("ps", bu)])
                ts = cx.slot("tmpf", 4)
                sc.op("act", lambda e, bg=bg, ts=ts: e.activation(
                    out=t["tmpf"][ts][:], in_=ps[:, bg, :], func=AF.Silu),
                    reads=[("ps", bg)], writes=[("tmpf", ts)])
                sc.op("dve", lambda e, bu=bu, ts=ts, fc=fc, bs=bs: e.tensor_tensor(
                    out=t["act"][:, fc, bs], in0=ps[:, bu, :], in1=t["tmpf"][ts][:],
                    op=ALU.mult),
                    reads=[("ps", bu), ("tmpf", ts)], writes=[("act", fc, b)])
    wdv = wd.rearrange("(s p) d -> p s d", p=128)
    for p_ in range(2):
        banks = [[cx.bank() for b in range(NB)] for jj in range(4)]
        nsl = (NFC + 3) // 4
        for s in range(nsl):
            nch = min(4, NFC - 4 * s)
            sd = cx.slot("wd", 3)
            sc.op("pool", lambda e, s=s, sd=sd, nch=nch, p_=p_: e.dma_start(
                out=t["wd"][sd][:, 0:nch, :],
                in_=wdv[:, 4 * s:4 * s + nch, 512 * p_:512 * p_ + 512]),
                writes=[("wd", sd)], dma=f"wd{sd}")
            for f4 in range(nch):
                fc = 4 * s + f4
                for jj in range(4):
                    for b in range(NB):
                        bs = slice(b * 512, (b + 1) * 512)
                        bk = banks[jj][b]
                        sc.op("pe", lambda e, sd=sd, f4=f4, jj=jj, bk=bk, fc=fc, bs=bs: e.matmul(
                            ps[:, bk, :], lhsT=t["wd"][sd][:, f4, 128 * jj:128 * jj + 128],
                            rhs=t["act"][:, fc, bs], start=(fc == 0), stop=(fc == NFC - 1)),
                            reads=[("wd", sd), ("act", fc, b)], writes=[("ps", bk)])
        for jj in range(4):
            j = 4 * p_ + jj
            for b in range(NB):
                bs = slice(b * 512, (b + 1) * 512)
                bk = banks[jj][b]
                sc.op("dve", lambda e, j=j, bk=bk, bs=bs: e.scalar_tensor_tensor(
                    out=t["x"][:, j, bs], in0=ps[:, bk, :], scalar=0.5,
                    in1=t["x"][:, j, bs], op0=ALU.mult, op1=ALU.add),
                    reads=[("ps", bk), ("x", j, b)], writes=[("x", j, b)])


def load_x(cx, t, x_dram, hf):
    xv = x_dram.rearrange("(c p) n -> p c n", p=128)
    for b in range(NB):
        c0 = hf * HALF + b * 512
        cx.sc.op("sp", lambda e, b=b, c0=c0: e.dma_start(out=t["x"][:, :, b * 512:(b + 1) * 512],
                                                     in_=xv[:, :, c0:c0 + 512]),
                 writes=[("x", c, b) for c in range(8)], dma=f"xin{b}")


def store_x(cx, t, x_dram, hf):
    xv = x_dram.rearrange("(c p) n -> p c n", p=128)
    cx.sc.op("sp", lambda e: e.dma_start(out=xv[:, :, hf * HALF:(hf + 1) * HALF], in_=t["x"][:]),
             reads=[("x", c, b) for c in range(8) for b in range(NB)], dma="xout")


def build_A():
    cx = Ctx()
    sc = cx.sc
    x_in = cx.din("x_in", [D, T])
    g1 = cx.din("g1", [128, 8])
    g2 = cx.din("g2", [128, 8])
    epsd = cx.din("epsd", [128, 1])
    wg = cx.din("wg", [D, DFF])
    wu = cx.din("wu", [D, DFF])
    wd = cx.din("wd", [DFF, D])
    win = cx.din("win", [D, 2048])
    winp = cx.din("winp", [D, 1024])
    cosd = cx.din("cosd", [128, T])
    sind = cx.din("sind", [128, T])
    x_out = cx.dout("x_out", [D, T])
    q_out = cx.dout("q_out", [512, T], BF16)
    k_out = cx.dout("k_out", [512, T], BF16)
    v_out = cx.dout("v_out", [T, 512], BF16)
    u_out = cx.dout("u_out", [512, T])

    t = alloc_common(cx)
    load_vec(cx, t, "g1", g1, 8)
    load_vec(cx, t, "g2", g2, 8)
    load_vec(cx, t, "eps", epsd, 1)
    t["cos"] = cx.sb("cos", [128, HALF], F32)
    t["sin"] = cx.sb("sin", [128, HALF], F32)
    t["stg16"] = [cx.sb(f"stg16_{i}", [128, 512], BF16) for i in range(4)]
    t["stg32"] = [cx.sb(f"stg32_{i}", [128, 512], F32) for i in range(2)]
    ps = t["psum"]
    winv = win.rearrange("(c p) f -> p c f", p=128)
    winpv = winp.rearrange("(c p) f -> p c f", p=128)

    for hf in range(2):
        load_x(cx, t, x_in, hf)
        ffn(cx, t, "g1", wg, wu, wd)
        store_x(cx, t, x_out, hf)
        rmsnorm_to_h(cx, t, "g2")
        sc.op("sp", lambda e, hf=hf: e.dma_start(out=t["cos"][:], in_=cosd[:, hf * HALF:(hf + 1) * HALF]),
              writes=[("cos",)], dma="cos")
        sc.op("sp", lambda e, hf=hf: e.dma_start(out=t["sin"][:], in_=sind[:, hf * HALF:(hf + 1) * HALF]),
              writes=[("sin",)], dma="sin")
        for s in range(4):
            sa = cx.slot("wa", 3)
            sb_ = cx.slot("wb", 3)
            sc.op("pool", lambda e, s=s, sa=sa: e.dma_start(
                out=t["wa"][sa][:], in_=winv[:, :, 256 * s:256 * s + 256]),
                writes=[("wa", sa)], dma=f"wa{sa}")
            sc.op("pool", lambda e, s=s, sb_=sb_: e.dma_start(
                out=t["wb"][sb_][:], in_=winpv[:, :, 256 * s:256 * s + 256]),
                writes=[("wb", sb_)], dma=f"wb{sb_}")
            for f2 in range(2):
                ch = 2 * s + f2
                dst = q_out if ch < 4 else k_out
                row0 = (ch % 4) * 128
                for b in range(NB):
                    bs = slice(b * 512, (b + 1) * 512)
                    b1 = cx.bank()
                    b2 = cx.bank()
                    for c in range(8):
                        sc.op("pe", lambda e, c=c, sa=sa, f2=f2, b1=b1, bs=bs: e.matmul(
                            ps[:, b1, :], lhsT=t["wa"][sa][:, c, 128 * f2:128 * f2 + 128],
                            rhs=t["h"][:, c, bs], start=(c == 0), stop=(c == 7)),
                            reads=[("wa", sa), ("h", c, b)], writes=[("ps", b1)])
                    for c in range(8):
                        sc.op("pe", lambda e, c=c, sb_=sb_, f2=f2, b2=b2, bs=bs: e.matmul(
                            ps[:, b2, :], lhsT=t["wb"][sb_][:, c, 128 * f2:128 * f2 + 128],
                            rhs=t["h"][:, c, bs], start=(c == 0), stop=(c == 7)),
                            reads=[("wb", sb_), ("h", c, b)], writes=[("ps", b2)])
                    s1 = cx.slot("tmpf", 4)
                    s2 = cx.slot("tmpf", 4)
                    so = cx.slot("stg16", 4)
                    sc.op("dve", lambda e, b1=b1, s1=s1, bs=bs: e.tensor_tensor(
                        out=t["tmpf"][s1][:], in0=ps[:, b1, :], in1=t["cos"][:, bs], op=ALU.mult),
                        reads=[("ps", b1), ("cos",)], writes=[("tmpf", s1)])
                    sc.op("dve", lambda e, b2=b2, s2=s2, bs=bs: e.tensor_tensor(
                        out=t["tmpf"][s2][:], in0=ps[:, b2, :], in1=t["sin"][:, bs], op=ALU.mult),
                        reads=[("ps", b2), ("sin",)], writes=[("tmpf", s2)])
                    sc.op("dve", lambda e, s1=s1, s2=s2, so=so: e.tensor_tensor(
                        out=t["stg16"][so][:], in0=t["tmpf"][s1][:], in1=t["tmpf"][s2][:], op=ALU.add),
                        reads=[("tmpf", s1), ("tmpf", s2)], writes=[("stg16", so)])
                    c0 = hf * HALF + b * 512
                    sc.op("sp", lambda e, dst=dst, row0=row0, c0=c0, so=so: e.dma_start(
                        out=dst[row0:row0 + 128, c0:c0 + 512], in_=t["stg16"][so][:]),
                        reads=[("stg16", so)], dma=f"stg16_{so}")
        for s in range(2):
            sa = cx.slot("wa", 3)
            sc.op("pool", lambda e, s=s, sa=sa: e.dma_start(
                out=t["wa"][sa][:], in_=winv[:, :, 1536 + 256 * s:1536 + 256 * s + 256]),
                writes=[("wa", sa)], dma=f"wa{sa}")
            for f2 in range(2):
                ch = 2 * s + f2
                for b in range(NB):
                    bs = slice(b * 512, (b + 1) * 512)
                    b1 = cx.bank()
                    for c in range(8):
                        sc.op("pe", lambda e, c=c, sa=sa, f2=f2, b1=b1, bs=bs: e.matmul(
                            ps[:, b1, :], lhsT=t["wa"][sa][:, c, 128 * f2:128 * f2 + 128],
                            rhs=t["h"][:, c, bs], start=(c == 0), stop=(c == 7)),
                            reads=[("wa", sa), ("h", c, b)], writes=[("ps", b1)])
                    so = cx.slot("stg32", 2)
                    sc.op("act", lambda e, b1=b1, so=so: e.activation(
                        out=t["stg32"][so][:], in_=ps[:, b1, :], func=AF.Copy),
                        reads=[("ps", b1)], writes=[("stg32", so)])
                    c0 = hf * HALF + b * 512
                    sc.op("sp", lambda e, ch=ch, c0=c0, so=so: e.dma_start(
                        out=u_out[ch * 128:ch * 128 + 128, c0:c0 + 512], in_=t["stg32"][so][:]),
                        reads=[("stg32", so)], dma=f"stg32_{so}")
        for s in range(2):
            sa = cx.slot("wa", 3)
            sc.op("pool", lambda e, s=s, sa=sa: e.dma_start(
                out=t["wa"][sa][:], in_=winv[:, :, 1024 + 256 * s:1024 + 256 * s + 256]),
                writes=[("wa", sa)], dma=f"wa{sa}")
            for tt in range(8):
                b = tt // 4
                b1 = cx.bank()
                for c in range(8):
                    sc.op("pe", lambda e, c=c, sa=sa, tt=tt, b1=b1: e.matmul(
                        ps[:, b1, 0:256], lhsT=t["h"][:, c, tt * 128:tt * 128 + 128],
                        rhs=t["wa"][sa][:, c, :], start=(c == 0), stop=(c == 7)),
                        reads=[("wa", sa), ("h", c, b)], writes=[("ps", b1)])
                so = cx.slot("stg16", 4)
                sc.op("act", lambda e, b1=b1, so=so: e.activation(
                    out=t["stg16"][so][:, 0:256], in_=ps[:, b1, 0:256], func=AF.Copy),
                    reads=[("ps", b1)], writes=[("stg16", so)])
                r0 = hf * HALF + tt * 128
                sc.op("sp", lambda e, r0=r0, s=s, so=so: e.dma_start(
                    out=v_out[r0:r0 + 128, 256 * s:256 * s + 256], in_=t["stg16"][so][:, 0:256]),
                    reads=[("stg16", so)], dma=f"stg16_{so}")
    return cx.finish()


def build_B():
    cx = Ctx()
    sc = cx.sc
    NQB = S // 512
    NKT = S // 128
    q_in = cx.din("q_in", [128, S], BF16)
    k_in = cx.din("k_in", [128, S // 2], BF16)
    v_in = cx.din("v_in", [128, NKT * 128], BF16)
    m_in = cx.din("m_in", [128, 4 * 512], BF16)
    o_out = cx.dout("o_out", [128, S])

    qT = cx.sb("qT", [128, S], BF16)
    kT = cx.sb("kT", [128, S // 2], BF16)
    pTs = [cx.sb(f"pTs{i}", [128, 512], BF16) for i in range(4)]
    vv = cx.sb("vv", [128, NKT * 128], BF16)
    mk = cx.sb("mk", [128, 4 * 512], BF16)
    mkv = mk[:].rearrange("p (j q) -> p j q", j=4)
    ones = cx.sb("ones", [128, 128], BF16)
    NP = 4
    pT = [cx.sb(f"pT{i}", [128, 2, 512], BF16) for i in range(NP)]
    rl = cx.sb("rl", [128, 512], F32)
    ostg = [cx.sb(f"ostg{i}", [128, 512], F32) for i in range(2)]
    ps = cx.ps("psum", [128, 8, 512], F32)

    sc.op("pool", lambda e: e.memset(ones[:], 1.0), writes=[("ones",)])
    NCH = 8
    cw = S // NCH
    for i in range(NCH):
        sc.op("sp", lambda e, i=i: e.dma_start(out=qT[:, i * cw:(i + 1) * cw], in_=q_in[:, i * cw:(i + 1) * cw]),
              writes=[("q", i)], dma=f"q{i}")
        sc.op("sp", lambda e, i=i: e.dma_start(out=kT[:, i * (cw // 2):(i + 1) * (cw // 2)],
                                               in_=k_in[:, i * (cw // 2):(i + 1) * (cw // 2)]),
              writes=[("k", i)], dma=f"k{i}")
        sc.op("sp", lambda e, i=i: e.dma_start(out=vv[:, i * cw:(i + 1) * cw], in_=v_in[:, i * cw:(i + 1) * cw]),
              writes=[("v", i)], dma=f"v{i}")
    sc.op("sp", lambda e: e.dma_start(out=mk[:], in_=m_in), writes=[("mk",)], dma="mk")

    groups = [(qb, g) for qb in range(NQB) for g in range(2 * (qb + 1))]
    LOOK = 2
    ginfo = {}

    def emit_S(i):
        qb, g = groups[i]
        b0 = 2 * (i % 2)
        sl = i % NP
        ginfo[i] = (b0, sl)
        for t2 in range(2):
            sc.op("pe", lambda e, t2=t2: e.matmul(
                ps[0:128, b0 + t2, :], lhsT=kT[64 * t2:64 * t2 + 64, g * 128:(g + 1) * 128],
                rhs=qT[64 * t2:64 * t2 + 64, qb * 512:(qb + 1) * 512], start=True, stop=True),
                reads=[("k", g * 128 // (cw // 2)), ("q", qb * 512 // cw)], writes=[("ps", b0 + t2)])
        sc.op("act", lambda e: e.activation(out=pT[sl][:], in_=ps[:, b0:b0 + 2, :], func=AF.Exp, scale=0.125),
              reads=[("ps", b0), ("ps", b0 + 1)], writes=[("pT", sl)])
        j2 = g - 2 * qb
        if j2 >= 0:
            sc.op("dve", lambda e: e.tensor_tensor(out=pT[sl][:], in0=pT[sl][:],
                                                  in1=mkv[:, j2 * 2:j2 * 2 + 2, :], op=ALU.mult),
                  reads=[("pT", sl), ("mk",)], writes=[("pT", sl)])

    def emit_PV(i):
        qb, g = groups[i]
        b0, sl = ginfo[i]
        ob = 4 + (qb % 2)
        lb = 6 + (qb % 2)
        ng = 2 * (qb + 1)
        for t2 in range(2):
            kt = 2 * g + t2
            sc.op("pe", lambda e, kt=kt, t2=t2: e.matmul(
                ps[:, ob, :], lhsT=vv[:, kt * 128:(kt + 1) * 128], rhs=pT[sl][:, t2, :],
                start=(kt == 0), stop=(kt == 2 * ng - 1)),
                reads=[("pT", sl), ("v", kt * 128 // cw)], writes=[("ps", ob)])
        s2 = i % 4
        sc.op("dve", lambda e: e.tensor_tensor(out=pTs[s2][:], in0=pT[sl][:, 0, :], in1=pT[sl][:, 1, :], op=ALU.add),
              reads=[("pT", sl)], writes=[("pTs", s2)])

    def emit_L(i):
        qb, g = groups[i]
        ob = 4 + (qb % 2)
        lb = 6 + (qb % 2)
        ng = 2 * (qb + 1)
        s2 = i % 4
        sc.op("pe", lambda e: e.matmul(
            ps[:, lb, :], lhsT=ones[:], rhs=pTs[s2][:], start=(g == 0), stop=(g == ng - 1)),
            reads=[("pTs", s2), ("ones",)], writes=[("ps", lb)])
        if g == ng - 1:
            so = qb % 2
            sc.op("dve", lambda e: e.reciprocal(out=rl[:], in_=ps[:, lb, :]),
                  reads=[("ps", lb)], writes=[("rl",)])
            sc.op("dve", lambda e: e.tensor_tensor(out=ostg[so][:], in0=ps[:, ob, :], in1=rl[:], op=ALU.mult),
                  reads=[("ps", ob), ("rl",)], writes=[("ostg", so)])
            sc.op("sp", lambda e: e.dma_start(out=o_out[:, qb * 512:(qb + 1) * 512], in_=ostg[so][:]),
                  reads=[("ostg", so)], dma=f"ostg{so}")

    n = len(groups)
    for i in range(n + LOOK + 1):
        if i < n:
            emit_S(i)
        if 0 <= i - LOOK < n:
            emit_PV(i - LOOK)
        if 0 <= i - LOOK - 1 < n:
            emit_L(i - LOOK - 1)
    return cx.finish()


def build_C():
    cx = Ctx()
    sc = cx.sc
    x_in = cx.din("x_in", [D, T])
    o_in = cx.din("o_in", [8 * 128, T])
    u_in = cx.din("u_in", [512, 16 + T])
    icnt = cx.din("icnt", [128, 4 * 16])
    lam4 = cx.din("lam4", [128, 4 * 64])
    lconst = cx.din("lconst", [128, 2])
    sgain = cx.din("sgain", [128, 1])
    pw = cx.din("pw", [4 * 128, 128])
    pscale = cx.din("pscale", [128, 4])
    wout = cx.din("wout", [D, D])
    g3 = cx.din("g3", [128, 8])
    gf = cx.din("gf", [128, 8])
    epsd = cx.din("epsd", [128, 1])
    wg = cx.din("wg", [D, DFF])
    wu = cx.din("wu", [D, DFF])
    wd = cx.din("wd", [DFF, D])
    x_out = cx.dout("x_out", [D, T])
    y_out = cx.dout("y_out", [D, T])

    t = alloc_common(cx)
    ps = t["psum"]
    for nm, ap, n in (("g3", g3, 8), ("gf", gf, 8), ("eps", epsd, 1), ("lam4", lam4, 256),
                      ("lconst", lconst, 2), ("sgain", sgain, 1), ("pscale", pscale, 4),
                      ("icnt", icnt, 64)):
        load_vec(cx, t, nm, ap, n)
    t["lt"] = cx.sb("lt", [128, 128], F32)
    t["ls"] = cx.sb("ls", [128, 8], F32)
    sc.op("dve", lambda e: e.tensor_tensor(out=t["lt"][:, 0:64], in0=t["lam4"][:, 0:64],
                                           in1=t["lam4"][:, 64:128], op=ALU.mult),
          reads=[("lam4",)], writes=[("lt", 0)])
    sc.op("dve", lambda e: e.tensor_tensor(out=t["lt"][:, 64:128], in0=t["lam4"][:, 128:192],
                                           in1=t["lam4"][:, 192:256], op=ALU.mult),
          reads=[("lam4",)], writes=[("lt", 1)])
    sc.op("dve", lambda e: e.tensor_reduce(out=t["ls"][:, 0:1], in_=t["lt"][:, 0:64],
                                           axis=mybir.AxisListType.X, op=ALU.add),
          reads=[("lt", 0)], writes=[("ls", 0)])
    sc.op("dve", lambda e: e.tensor_reduce(out=t["ls"][:, 1:2], in_=t["lt"][:, 64:128],
                                           axis=mybir.AxisListType.X, op=ALU.add),
          reads=[("lt", 1)], writes=[("ls", 1)])
    sc.op("act", lambda e: e.activation(out=t["ls"][:, 2:4], in_=t["ls"][:, 0:2], func=AF.Exp),
          reads=[("ls", 0), ("ls", 1)], writes=[("ls", 2)])
    sc.op("dve", lambda e: e.tensor_tensor(out=t["ls"][:, 4:5], in0=t["ls"][:, 3:4], in1=t["ls"][:, 2:3],
                                           op=ALU.subtract),
          reads=[("ls", 2)], writes=[("ls", 4)])
    sc.op("dve", lambda e: e.tensor_tensor(out=t["ls"][:, 5:6], in0=t["ls"][:, 4:5], in1=t["lconst"][:, 0:1],
                                           op=ALU.subtract),
          reads=[("ls", 4), ("lconst",)], writes=[("neglam",)])
    sc.op("dve", lambda e: e.tensor_tensor(out=t["ls"][:, 6:7], in0=t["sgain"][:, 0:1], in1=t["lconst"][:, 1:2],
                                           op=ALU.mult),
          reads=[("sgain",), ("lconst",)], writes=[("sg",)])

    t["o1"] = [cx.sb(f"o1_{i}", [128, 512], F32) for i in range(3)]
    t["o2"] = [cx.sb(f"o2_{i}", [128, 512], F32) for i in range(3)]
    t["od"] = [cx.sb(f"od_{i}", [128, 512], F32) for i in range(3)]
    t["uu"] = cx.sb("uu", [128, 16 + HALF], F32)
    t["ua"] = cx.sb("ua", [128, 16 + HALF], F32)
    t["ub"] = cx.sb("ub", [128, 16 + HALF], F32)
    t["dif"] = cx.sb("dif", [128, HALF], BF16)
    t["pwt"] = cx.sb("pwt", [128, 4, 128], BF16)
    sc.op("pool", lambda e: e.dma_start(out=t["pwt"][:], in_=pw.rearrange("(g c) e -> c g e", c=128)),
          writes=[("pwt",)], dma="pwt")
    woutv = wout.rearrange("(c p) f -> p c f", p=128)
    WINS = (2, 4, 8, 16)

    for hf in range(2):
        load_x(cx, t, x_in, hf)
        items = [(hd, b) for hd in range(4) for b in range(NB)]
        st = {}

        def sub1(i):
            hd, b = items[i]
            c0 = hf * HALF + b * 512
            s1 = cx.slot("o1", 3)
            s2 = cx.slot("o2", 3)
            sd_ = cx.slot("od", 3)
            sc.op("sp", lambda e: e.dma_start(
                out=t["o1"][s1][:], in_=o_in[(2 * hd) * 128:(2 * hd) * 128 + 128, c0:c0 + 512]),
                writes=[("o1", s1)], dma=f"o1_{s1}")
            sc.op("sp", lambda e: e.dma_start(
                out=t["o2"][s2][:], in_=o_in[(2 * hd + 1) * 128:(2 * hd + 1) * 128 + 128, c0:c0 + 512]),
                writes=[("o2", s2)], dma=f"o2_{s2}")
            sc.op("dve", lambda e: e.scalar_tensor_tensor(
                out=t["od"][sd_][:], in0=t["o2"][s2][:], scalar=t["ls"][:, 5:6], in1=t["o1"][s1][:],
                op0=ALU.mult, op1=ALU.add),
                reads=[("o1", s1), ("o2", s2), ("neglam",)], writes=[("od", sd_)])
            sq = cx.slot("sq", 4)
            sc.op("act", lambda e: e.activation(
                out=t["sq"][sq][:], in_=t["od"][sd_][:], func=AF.Square),
                reads=[("od", sd_)], writes=[("sq", sq)])
            bk = cx.bank()
            sc.op("pe", lambda e: e.matmul(
                ps[:, bk, :], lhsT=t["ones"][:], rhs=t["sq"][sq][:], start=True, stop=True),
                reads=[("sq", sq), ("ones",)], writes=[("ps", bk)])
            st[i] = (sd_, bk)

        def sub2(i):
            hd, b = items[i]
            bs = slice(b * 512, (b + 1) * 512)
            sd_, bk = st[i]
            ss = cx.slot("std", 2)
            rs = cx.slot("rstd", 2)
            sc.op("act", lambda e: e.activation(
                out=t["std"][ss][:], in_=ps[:, bk, :], func=AF.Sqrt, bias=t["eps"][:, 0:1], scale=1.0 / 128),
                reads=[("ps", bk), ("eps",)], writes=[("std", ss)])
            sc.op("dve", lambda e: e.reciprocal(
                out=t["rstd"][rs][:], in_=t["std"][ss][:]),
                reads=[("std", ss)], writes=[("rstd", rs), ("rscr",)])
            sc.op("dve", lambda e: e.scalar_tensor_tensor(
                out=t["h"][:, hd, bs], in0=t["od"][sd_][:], scalar=t["ls"][:, 6:7],
                in1=t["rstd"][rs][:], op0=ALU.mult, op1=ALU.mult),
                reads=[("od", sd_), ("sg",), ("rstd", rs)], writes=[("h", hd, b)])

        for i in range(len(items) + 1):
            if i < len(items):
                sub1(i)
            if i >= 1:
                sub2(i - 1)
        for g in range(4):
            w = WINS[g]
            c0 = hf * HALF
            sc.op("sp", lambda e, g=g, c0=c0: e.dma_start(
                out=t["uu"][:], in_=u_in[g * 128:g * 128 + 128, c0:c0 + 16 + HALF]),
                writes=[("uu",)], dma="uu")
            src = "uu"
            sh = 1
            flip = 0
            while sh < w:
                dst = "ua" if flip == 0 else "ub"
                sc.op("dve", lambda e, src=src, dst=dst, sh=sh: e.tensor_tensor(
                    out=t[dst][:, sh:16 + HALF], in0=t[src][:, sh:16 + HALF],
                    in1=t[src][:, 0:16 + HALF - sh], op=ALU.add),
                    reads=[(src,)], writes=[(dst,)])
                src = dst
                flip ^= 1
                sh *= 2
            oth = "ua" if src == "ub" else "ub"
            sc.op("dve", lambda e, src=src, w=w: e.scalar_tensor_tensor(
                out=t["dif"][:, 0:HALF], in0=t[src][:, 16:16 + HALF], scalar=1.0 / w,
                in1=t["uu"][:, 16:16 + HALF], op0=ALU.mult, op1=ALU.subtract),
                reads=[(src,), ("uu",)], writes=[("dif",)])
            if hf == 0:
                sc.op("dve", lambda e, src=src, oth=oth, g=g: e.tensor_tensor(
                    out=t[oth][:, 16:32], in0=t[src][:, 16:32],
                    in1=t["icnt"][:, g * 16:g * 16 + 16], op=ALU.mult),
                    reads=[(src,), ("icnt",)], writes=[(oth,)])
                sc.op("dve", lambda e, oth=oth: e.tensor_tensor(
                    out=t["dif"][:, 0:16], in0=t[oth][:, 16:32], in1=t["uu"][:, 16:32], op=ALU.subtract),
                    reads=[(oth,), ("uu",), ("dif",)], writes=[("dif",)])
            for b in range(NB):
                bs = slice(b * 512, (b + 1) * 512)
                bk = cx.bank()
                sc.op("pe", lambda e, g=g, bk=bk, bs=bs: e.matmul(
                    ps[:, bk, :], lhsT=t["pwt"][:, g, :], rhs=t["dif"][:, bs], start=True, stop=True),
                    reads=[("pwt",), ("dif",)], writes=[("ps", bk)])
                sc.op("dve", lambda e, g=g, bk=bk, bs=bs: e.tensor_scalar(
                    out=t["h"][:, 4 + g, bs], in0=ps[:, bk, :], scalar1=t["pscale"][:, g:g + 1],
                    scalar2=None, op0=ALU.mult),
                    reads=[("ps", bk), ("pscale",)], writes=[("h", 4 + g, b)])
        for s in range(4):
            sa = cx.slot("wa", 3)
            sc.op("pool", lambda e, s=s, sa=sa: e.dma_start(
                out=t["wa"][sa][:], in_=woutv[:, :, 256 * s:256 * s + 256]),
                writes=[("wa", sa)], dma=f"wa{sa}")
            for f2 in range(2):
                j = 2 * s + f2
                for b in range(NB):
                    bs = slice(b * 512, (b + 1) * 512)
                    bk = cx.bank()
                    for c in range(8):
                        sc.op("pe", lambda e, c=c, sa=sa, f2=f2, bk=bk, bs=bs: e.matmul(
                            ps[:, bk, :], lhsT=t["wa"][sa][:, c, 128 * f2:128 * f2 + 128],
                            rhs=t["h"][:, c, bs], start=(c == 0), stop=(c == 7)),
                            reads=[("wa", sa), ("h", c, b)], writes=[("ps", bk)])
                    sc.op("dve", lambda e, j=j, bk=bk, bs=bs: e.tensor_tensor(
                        out=t["x"][:, j, bs], in0=ps[:, bk, :], in1=t["x"][:, j, bs], op=ALU.add),
                        reads=[("ps", bk), ("x", j, b)], writes=[("x", j, b)])
        ffn(cx, t, "g3", wg, wu, wd)
        store_x(cx, t, x_out, hf)
        rmsnorm_final(cx, t, y_out, hf)
    return cx.finish()


def rmsnorm_final(cx, t, y_out, hf):
    sc = cx.sc
    ps = t["psum"]
    yv = y_out.rearrange("(c p) n -> p c n", p=128)
    for b in range(NB):
        bs = slice(b * 512, (b + 1) * 512)
        bk = cx.bank()
        for c in range(8):
            s = cx.slot("sq", 4)
            sc.op("act", lambda e, c=c, s=s, bs=bs: e.activation(
                out=t["sq"][s][:], in_=t["x"][:, c, bs], func=AF.Square),
                reads=[("x", c, b)], writes=[("sq", s)])
            sc.op("pe", lambda e, c=c, s=s, bk=bk: e.matmul(
                ps[:, bk, :], lhsT=t["ones"][:], rhs=t["sq"][s][:], start=(c == 0), stop=(c == 7)),
                reads=[("sq", s), ("ones",)], writes=[("ps", bk)])
        ss = cx.slot("std", 2)
        sc.op("act", lambda e, bk=bk, ss=ss: e.activation(
            out=t["std"][ss][:], in_=ps[:, bk, :], func=AF.Sqrt, bias=t["eps"][:, 0:1], scale=1.0 / D),
            reads=[("ps", bk), ("eps",)], writes=[("std", ss)])
        sc.op("dve", lambda e, b=b, ss=ss: e.reciprocal(
            out=t["rstd"][b][:], in_=t["std"][ss][:]),
              reads=[("std", ss)], writes=[("rstd", b), ("rscr",)])
        for c in range(8):
            so = cx.slot("tmpf", 4)
            sc.op("dve", lambda e, c=c, b=b, bs=bs, so=so: e.scalar_tensor_tensor(
                out=t["tmpf"][so][:], in0=t["x"][:, c, bs], scalar=t["gf"][:, c:c + 1],
                in1=t["rstd"][b][:], op0=ALU.mult, op1=ALU.mult),
                reads=[("x", c, b), ("gf",), ("rstd", b)], writes=[("tmpf", so)])
            c0 = hf * HALF + b * 512
            sc.op("sp", lambda e, c=c, c0=c0, so=so: e.dma_start(
                out=yv[:, c, c0:c0 + 512], in_=t["tmpf"][so][:]),
                reads=[("tmpf", so)], dma=f"tmpf{so}")


_PROGS = {}


def _prog(name):
    if name not in _PROGS:
        _PROGS[name] = {"A": build_A, "B": build_B, "C": build_C}[name]()
    return _PROGS[name]


def _run(name, in_maps):
    res = run_bass_kernel_spmd(_prog(name), in_maps, core_ids=list(range(NCORES)))
    return res.results


def _vec8(g):
    return np.ascontiguousarray(np.asarray(g, np.float32).reshape(8, 128).T)


def _rope_tables():
    d = 64
    inv = (10000.0 ** (-np.arange(0, d, 2, dtype=np.float32) / d)).astype(np.float32)
    ang = np.arange(S, dtype=np.float32)[:, None] * inv[None, :]
    ang = np.concatenate([ang, ang], axis=-1)
    cos = np.cos(ang).astype(np.float32).T
    sin = np.sin(ang).astype(np.float32).T
    sin[:32] *= -1.0
    cos2 = np.concatenate([cos, cos], axis=0)
    sin2 = np.concatenate([sin, sin], axis=0)
    return cos2, sin2


def kernel(x, ffn1_norm, ffn1_w_gate, ffn1_w_up, ffn1_w_down, mix_norm, w_in,
           lambda_q1, lambda_k1, lambda_q2, lambda_k2, subln_gain, pool_w, pool_scale,
           w_out, ffn2_norm, ffn2_w_gate, ffn2_w_up, ffn2_w_down, final_norm):
    f = lambda a: np.ascontiguousarray(np.asarray(a, dtype=np.float32))
    x = f(x)
    xT = [np.ascontiguousarray(x[0, c * T:(c + 1) * T, :].T) for c in range(NCORES)]
    cos2, sin2 = _rope_tables()
    epsd = np.full((128, 1), EPS, np.float32)
    kp = np.arange(128)[:, None, None] + 128 * np.arange(4)[None, :, None]
    qq = np.arange(512)[None, None, :]
    mask = (kp <= qq).astype(np.float32).reshape(128, 4 * 512).astype(ml_dtypes.bfloat16)
    perm = (np.arange(1024).reshape(16, 2, 32)[:, ::-1, :]).reshape(-1)
    WINS = (2, 4, 8, 16)
    y = None
    for l in range(DEPTH):
        wl = f(w_in[l])
        winp = np.ascontiguousarray(wl[:, :1024][:, perm])
        ins = []
        for c in range(NCORES):
            ins.append(dict(x_in=xT[c], g1=_vec8(ffn1_norm[l]), g2=_vec8(mix_norm[l]), epsd=epsd,
                            wg=f(ffn1_w_gate[l]), wu=f(ffn1_w_up[l]), wd=f(ffn1_w_down[l]),
                            win=wl, winp=winp,
                            cosd=np.ascontiguousarray(cos2[:, c * T:(c + 1) * T]),
                            sind=np.ascontiguousarray(sin2[:, c * T:(c + 1) * T])))
        ra = _run("A", ins)
        xT = [ra[c]["x_out"] for c in range(NCORES)]
        qT = np.concatenate([ra[c]["q_out"] for c in range(NCORES)], axis=1)
        kT = np.concatenate([ra[c]["k_out"] for c in range(NCORES)], axis=1)
        v = np.concatenate([ra[c]["v_out"] for c in range(NCORES)], axis=0)
        uT = np.concatenate([ra[c]["u_out"] for c in range(NCORES)], axis=1)
        ins = []
        for c in range(NCORES):
            h, m = c // 2, c % 2
            vh = v[:, h * 128:(h + 1) * 128].reshape(S // 128, 128, 128).transpose(1, 0, 2)
            qc = qT[c * 64:(c + 1) * 64]
            kc = kT[c * 64:(c + 1) * 64].reshape(64, S // 256, 2, 128).transpose(2, 0, 1, 3)
            ins.append(dict(q_in=np.ascontiguousarray(np.concatenate([qc, qc], axis=0)),
                            k_in=np.ascontiguousarray(kc).reshape(128, S // 2),
                            v_in=np.ascontiguousarray(vh).reshape(128, -1),
                            m_in=mask))
        rb = _run("B", ins)
        oT = np.concatenate([rb[c]["o_out"] for c in range(NCORES)], axis=0)
        lam_init = 0.8 - 0.6 * math.exp(-0.3 * l)
        lam4 = np.concatenate([f(lambda_q1[l]), f(lambda_k1[l]), f(lambda_q2[l]), f(lambda_k2[l])])
        lam4 = np.ascontiguousarray(np.broadcast_to(lam4[None, :], (128, 256)))
        lconst = np.ascontiguousarray(np.broadcast_to(
            np.array([lam_init, 1.0 - lam_init], np.float32)[None, :], (128, 2)))
        upad = np.concatenate([np.zeros((512, 16), np.float32), uT], axis=1)
        ins = []
        for c in range(NCORES):
            icnt = np.zeros((128, 64), np.float32)
            for g in range(4):
                pos = c * T + np.arange(16)
                icnt[:, g * 16:(g + 1) * 16] = (1.0 / np.minimum(pos + 1, WINS[g]))[None, :]
            ins.append(dict(x_in=xT[c], o_in=np.ascontiguousarray(oT[:, c * T:(c + 1) * T]),
                            u_in=np.ascontiguousarray(upad[:, c * T:c * T + 16 + T]),
                            icnt=icnt, lam4=lam4, lconst=lconst,
                            sgain=f(subln_gain[l]).reshape(128, 1),
                            pw=f(pool_w[l]).reshape(512, 128),
                            pscale=np.ascontiguousarray(f(pool_scale[l]).reshape(4, 128).T),
                            wout=f(w_out[l]), g3=_vec8(ffn2_norm[l]), gf=_vec8(final_norm), epsd=epsd,
                            wg=f(ffn2_w_gate[l]), wu=f(ffn2_w_up[l]), wd=f(ffn2_w_down[l])))
        rc = _run("C", ins)
        xT = [rc[c]["x_out"] for c in range(NCORES)]
        y = [rc[c]["y_out"] for c in range(NCORES)]
    out = np.concatenate([yc.T for yc in y], axis=0)[None]
    return np.ascontiguousarray(out.astype(np.float32))
```

```python
import math
from contextlib import ExitStack

import numpy as np
import ml_dtypes

import concourse.bass as bass
import concourse.mybir as mybir
from concourse.bass_utils import run_bass_kernel_spmd

F32 = mybir.dt.float32
BF16 = mybir.dt.bfloat16
AF = mybir.ActivationFunctionType
ALU = mybir.AluOpType

NCORES = 8
D = 1024
S = 16384
DEPTH = 4
DFF = 2816
NFC = DFF // 128
T = S // NCORES
HALF = 1024
NB = 2
EPS = 1e-6
SAME_ENGINE_SYNC = True


class Sched:
    ENGS = ("pe", "act", "dve", "pool", "sp")

    def __init__(self):
        self.ops = []
        self.last_writer = {}
        self.readers = {}
        self.dma_count = {}

    def op(self, eng, fn, reads=(), writes=(), dma=None):
        idx = len(self.ops)
        deps = set()
        for r in reads:
            if r in self.last_writer:
                deps.add(self.last_writer[r])
        for w in writes:
            if w in self.last_writer:
                deps.add(self.last_writer[w])
            for rd in self.readers.get(w, ()):
                deps.add(rd)
        best = {}
        for d in deps:
            od = self.ops[d]
            k = ("dma", od["dma"]) if od["dma"] is not None else ("eng", od["eng"])
            if k not in best or d > best[k]:
                best[k] = d
        deps = set(best.values())
        o = dict(eng=eng, fn=fn, deps=deps, dma=dma, signal=False, sig_idx=None,
                 dma_val=None)
        if dma is not None:
            self.dma_count[dma] = self.dma_count.get(dma, 0) + 1
            o["dma_val"] = 16 * self.dma_count[dma]
        self.ops.append(o)
        for w in writes:
            self.last_writer[w] = idx
            self.readers[w] = []
        for r in reads:
            self.readers.setdefault(r, []).append(idx)
        return idx

    def finalize(self):
        ops = self.ops
        for o in ops:
            for d in o["deps"]:
                od = ops[d]
                if od["dma"] is None:
                    if od["eng"] != o["eng"] or (SAME_ENGINE_SYNC and od["eng"] != "pe"
                                                 and o["dma"] is None):
                        od["signal"] = True
                    elif o["dma"] is not None and od["eng"] == o["eng"]:
                        od["signal"] = True
        cnt = {e: 0 for e in self.ENGS}
        for o in ops:
            if o["signal"]:
                cnt[o["eng"]] += 1
                o["sig_idx"] = cnt[o["eng"]]
        waited = {e: {} for e in self.ENGS}
        for o in ops:
            w = {}
            for d in o["deps"]:
                od = ops[d]
                if od["dma"] is not None:
                    key = ("dma", od["dma"])
                    val = od["dma_val"]
                else:
                    if not od["signal"]:
                        continue
                    key = ("tl", od["eng"])
                    val = od["sig_idx"]
                w[key] = max(w.get(key, 0), val)
            wl = []
            for key, val in w.items():
                if waited[o["eng"]].get(key, 0) >= val:
                    continue
                waited[o["eng"]][key] = val
                wl.append((key, val))
            o["waits"] = wl

    def emit(self, nc, stack):
        self.finalize()
        sems = {}
        for e in self.ENGS:
            sems[("tl", e)] = stack.enter_context(nc.semaphore("tl_" + e))
        for k in self.dma_count:
            sems[("dma", k)] = stack.enter_context(nc.semaphore("dma_" + str(k)))
        block = stack.enter_context(nc.Block())
        ops = self.ops
        dma_final = dict(self.dma_count)

        def run(eng_name, e, final=False):
            for o in ops:
                if o["eng"] != eng_name:
                    continue
                for key, val in o["waits"]:
                    e.wait_ge(sems[key], val)
                inst = o["fn"](e)
                if o["dma"] is not None:
                    inst.then_inc(sems[("dma", o["dma"])], 16)
                elif o["signal"]:
                    inst.then_inc(sems[("tl", eng_name)], 1)
            if final:
                for k, n in dma_final.items():
                    e.wait_ge(sems[("dma", k)], 16 * n)

        @block.tensor
        def _(e):
            run("pe", e)

        @block.scalar
        def _(e):
            run("act", e)

        @block.vector
        def _(e):
            run("dve", e)

        @block.gpsimd
        def _(e):
            run("pool", e)

        @block.sync
        def _(e):
            run("sp", e, final=True)


class Ctx:
    def __init__(self):
        self.nc = bass.Bass("TRN2", target_bir_lowering=False)
        self.sc = Sched()
        self.stack = ExitStack()
        self.bank_ctr = 0
        self.rot = {}

    def din(self, name, shape, dt=F32):
        return self.nc.dram_tensor(name, list(shape), dt, kind="ExternalInput").ap()

    def dout(self, name, shape, dt=F32):
        return self.nc.dram_tensor(name, list(shape), dt, kind="ExternalOutput").ap()

    def sb(self, name, shape, dt):
        return self.stack.enter_context(self.nc.sbuf_tensor("s_" + name, list(shape), dt))

    def ps(self, name, shape, dt=F32):
        return self.stack.enter_context(self.nc.psum_tensor("p_" + name, list(shape), dt))

    def bank(self):
        b = self.bank_ctr % 8
        self.bank_ctr += 1
        return b

    def slot(self, name, n):
        v = self.rot.get(name, 0)
        self.rot[name] = v + 1
        return v % n

    def finish(self):
        self.sc.emit(self.nc, self.stack)
        self.stack.close()
        return self.nc


def alloc_common(cx):
    t = {}
    t["x"] = cx.sb("x", [128, 8, HALF], F32)
    t["h"] = cx.sb("h", [128, 8, HALF], BF16)
    t["act"] = cx.sb("act", [128, NFC, HALF], BF16)
    t["wa"] = [cx.sb(f"wa{i}", [128, 8, 256], BF16) for i in range(3)]
    t["wb"] = [cx.sb(f"wb{i}", [128, 8, 256], BF16) for i in range(3)]
    t["wd"] = [cx.sb(f"wd{i}", [128, 4, 512], BF16) for i in range(3)]
    t["sq"] = [cx.sb(f"sq{i}", [128, 512], BF16) for i in range(4)]
    t["std"] = [cx.sb(f"std{i}", [128, 512], F32) for i in range(2)]
    t["rscr"] = cx.sb("rscr", [128, 512], F32)
    t["rstd"] = [cx.sb(f"rstd{i}", [128, 512], F32) for i in range(2)]
    t["tmpf"] = [cx.sb(f"tmpf{i}", [128, 512], F32) for i in range(4)]
    t["ones"] = cx.sb("ones", [128, 128], BF16)
    t["psum"] = cx.ps("psum", [128, 8, 512], F32)
    cx.sc.op("pool", lambda e: e.memset(t["ones"][:], 1.0), writes=[("ones",)])
    return t


def load_vec(cx, t, name, dram_ap, ncol):
    t[name] = cx.sb(name, [128, ncol], F32)
    cx.sc.op("sp", lambda e: e.dma_start(out=t[name][:], in_=dram_ap),
             writes=[(name,)], dma=name)


def rmsnorm_to_h(cx, t, gname, nchunk=8, src="x", dst="h", scale_d=D):
    sc = cx.sc
    ps = t["psum"]
    for b in range(NB):
        bs = slice(b * 512, (b + 1) * 512)
        bk = cx.bank()
        for c in range(nchunk):
            s = cx.slot("sq", 4)
            sc.op("act", lambda e, c=c, s=s, bs=bs: e.activation(
                out=t["sq"][s][:], in_=t[src][:, c, bs], func=AF.Square),
                reads=[(src, c, b)], writes=[("sq", s)])
            sc.op("pe", lambda e, c=c, s=s, bk=bk: e.matmul(
                ps[:, bk, :], lhsT=t["ones"][:], rhs=t["sq"][s][:],
                start=(c == 0), stop=(c == nchunk - 1)),
                reads=[("sq", s), ("ones",)], writes=[("ps", bk)])
        ss = cx.slot("std", 2)
        sc.op("act", lambda e, bk=bk, ss=ss: e.activation(
            out=t["std"][ss][:], in_=ps[:, bk, :], func=AF.Ln, bias=t["eps"][:, 0:1],
            scale=1.0 / scale_d),
            reads=[("ps", bk), ("eps",)], writes=[("std", ss)])
        sc.op("act", lambda e, b=b, ss=ss: e.activation(
            out=t["rstd"][b][:], in_=t["std"][ss][:], func=AF.Exp, scale=-0.5),
              reads=[("std", ss)], writes=[("rstd", b)])
        for c in range(nchunk):
            sc.op("dve", lambda e, c=c, b=b, bs=bs: e.scalar_tensor_tensor(
                out=t[dst][:, c, bs], in0=t[src][:, c, bs], scalar=t[gname][:, c:c + 1],
                in1=t["rstd"][b][:], op0=ALU.mult, op1=ALU.mult),
                reads=[(src, c, b), (gname,), ("rstd", b)], writes=[(dst, c, b)])


def ffn(cx, t, gname, wg, wu, wd):
    sc = cx.sc
    ps = t["psum"]
    rmsnorm_to_h(cx, t, gname)
    wgv = wg.rearrange("(c p) f -> p c f", p=128)
    wuv = wu.rearrange("(c p) f -> p c f", p=128)
    nslab = NFC // 2
    for s in range(nslab):
        sa = cx.slot("wa", 3)
        sb_ = cx.slot("wb", 3)
        sc.op("pool", lambda e, s=s, sa=sa: e.dma_start(
            out=t["wa"][sa][:], in_=wgv[:, :, 256 * s:256 * s + 256]),
            writes=[("wa", sa)], dma=f"wa{sa}")
        sc.op("pool", lambda e, s=s, sb_=sb_: e.dma_start(
            out=t["wb"][sb_][:], in_=wuv[:, :, 256 * s:256 * s + 256]),
            writes=[("wb", sb_)], dma=f"wb{sb_}")
        for f2 in range(2):
            fc = 2 * s + f2
            for b in range(NB):
                bs = slice(b * 512, (b + 1) * 512)
                bg = cx.bank()
                bu = cx.bank()
                for c in range(8):
                    sc.op("pe", lambda e, c=c, sa=sa, f2=f2, bg=bg, bs=bs: e.matmul(
                        ps[:, bg, :], lhsT=t["wa"][sa][:, c, 128 * f2:128 * f2 + 128],
                        rhs=t["h"][:, c, bs], start=(c == 0), stop=(c == 7)),
                        reads=[("wa", sa), ("h", c, b)], writes=[("ps", bg)])
                for c in range(8):
                    sc.op("pe", lambda e, c=c, sb_=sb_, f2=f2, bu=bu, bs=bs: e.matmul(
                        ps[:, bu, :], lhsT=t["wb"][sb_][:, c, 128 * f2:128 * f2 + 128],
                        rhs=t["h"][:, c, bs], start=(c == 0), stop=(c == 7)),
                        reads=[("wb", sb_), ("h", c, b)], writes=[("ps", bu)])
                ts = cx.slot("tmpf", 4)
                sc.op("act", lambda e, bg=bg, ts=ts: e.activation(
                    out=t["tmpf"][ts][:], in_=ps[:, bg, :], func=AF.Silu),
                    reads=[("ps", bg)], writes=[("tmpf", ts)])
                sc.op("dve", lambda e, bu=bu, ts=ts, fc=fc, bs=bs: e.tensor_tensor(
                    out=t["act"][:, fc, bs], in0=ps[:, bu, :], in1=t["tmpf"][ts][:],
                    op=ALU.mult),
                    reads=[("ps", bu), ("tmpf", ts)], writes=[("act", fc, b)])
    wdv = wd.rearrange("(s p) d -> p s d", p=128)
    for p_ in range(2):
        banks = [[cx.bank() for b in range(NB)] for jj in range(4)]
        nsl = (NFC + 3) // 4
        for s in range(nsl):
            nch = min(4, NFC - 4 * s)
            sd = cx.slot("wd", 3)
            sc.op("pool", lambda e, s=s, sd=sd, nch=nch, p_=p_: e.dma_start(
                out=t["wd"][sd][:, 0:nch, :],
                in_=wdv[:, 4 * s:4 * s + nch, 512 * p_:512 * p_ + 512]),
                writes=[("wd", sd)], dma=f"wd{sd}")
            for f4 in range(nch):
                fc = 4 * s + f4
                for jj in range(4):
                    for b in range(NB):
                        bs = slice(b * 512, (b + 1) * 512)
                        bk = banks[jj][b]
                        sc.op("pe", lambda e, sd=sd, f4=f4, jj=jj, bk=bk, fc=fc, bs=bs: e.matmul(
                            ps[:, bk, :], lhsT=t["wd"][sd][:, f4, 128 * jj:128 * jj + 128],
                            rhs=t["act"][:, fc, bs], start=(fc == 0), stop=(fc == NFC - 1)),
                            reads=[("wd", sd), ("act", fc, b)], writes=[("ps", bk)])
        for jj in range(4):
            j = 4 * p_ + jj
            for b in range(NB):
                bs = slice(b * 512, (b + 1) * 512)
                bk = banks[jj][b]
                sc.op("dve", lambda e, j=j, bk=bk, bs=bs: e.scalar_tensor_tensor(
                    out=t["x"][:, j, bs], in0=ps[:, bk, :], scalar=0.5,
                    in1=t["x"][:, j, bs], op0=ALU.mult, op1=ALU.add),
                    reads=[("ps", bk), ("x", j, b)], writes=[("x", j, b)])


def load_x(cx, t, x_dram, hf):
    xv = x_dram.rearrange("(c p) n -> p c n", p=128)
    for b in range(NB):
        c0 = hf * HALF + b * 512
        cx.sc.op("sp", lambda e, b=b, c0=c0: e.dma_start(out=t["x"][:, :, b * 512:(b + 1) * 512],
                                                     in_=xv[:, :, c0:c0 + 512]),
                 writes=[("x", c, b) for c in range(8)], dma=f"xin{b}")


def store_x(cx, t, x_dram, hf):
    xv = x_dram.rearrange("(c p) n -> p c n", p=128)
    cx.sc.op("sp", lambda e: e.dma_start(out=xv[:, :, hf * HALF:(hf + 1) * HALF], in_=t["x"][:]),
             reads=[("x", c, b) for c in range(8) for b in range(NB)], dma="xout")


def build_A():
    cx = Ctx()
    sc = cx.sc
    x_in = cx.din("x_in", [D, T])
    g1 = cx.din("g1", [128, 8])
    g2 = cx.din("g2", [128, 8])
    epsd = cx.din("epsd", [128, 1])
    wg = cx.din("wg", [D, DFF])
    wu = cx.din("wu", [D, DFF])
    wd = cx.din("wd", [DFF, D])
    win = cx.din("win", [D, 2048])
    winp = cx.din("winp", [D, 1024])
    cosd = cx.din("cosd", [128, T])
    sind = cx.din("sind", [128, T])
    x_out = cx.dout("x_out", [D, T])
    q_out = cx.dout("q_out", [512, T], BF16)
    k_out = cx.dout("k_out", [512, T], BF16)
    v_out = cx.dout("v_out", [T, 512], BF16)
    u_out = cx.dout("u_out", [512, T])

    t = alloc_common(cx)
    load_vec(cx, t, "g1", g1, 8)
    load_vec(cx, t, "g2", g2, 8)
    load_vec(cx, t, "eps", epsd, 1)
    t["cos"] = cx.sb("cos", [128, HALF], F32)
    t["sin"] = cx.sb("sin", [128, HALF], F32)
    t["stg16"] = [cx.sb(f"stg16_{i}", [128, 512], BF16) for i in range(4)]
    t["stg32"] = [cx.sb(f"stg32_{i}", [128, 512], F32) for i in range(2)]
    ps = t["psum"]
    winv = win.rearrange("(c p) f -> p c f", p=128)
    winpv = winp.rearrange("(c p) f -> p c f", p=128)

    for hf in range(2):
        load_x(cx, t, x_in, hf)
        ffn(cx, t, "g1", wg, wu, wd)
        store_x(cx, t, x_out, hf)
        rmsnorm_to_h(cx, t, "g2")
        sc.op("sp", lambda e, hf=hf: e.dma_start(out=t["cos"][:], in_=cosd[:, hf * HALF:(hf + 1) * HALF]),
              writes=[("cos",)], dma="cos")
        sc.op("sp", lambda e, hf=hf: e.dma_start(out=t["sin"][:], in_=sind[:, hf * HALF:(hf + 1) * HALF]),
              writes=[("sin",)], dma="sin")
        for s in range(4):
            sa = cx.slot("wa", 3)
            sb_ = cx.slot("wb", 3)
            sc.op("pool", lambda e, s=s, sa=sa: e.dma_start(
                out=t["wa"][sa][:], in_=winv[:, :, 256 * s:256 * s + 256]),
                writes=[("wa", sa)], dma=f"wa{sa}")
            sc.op("pool", lambda e, s=s, sb_=sb_: e.dma_start(
                out=t["wb"][sb_][:], in_=winpv[:, :, 256 * s:256 * s + 256]),
                writes=[("wb", sb_)], dma=f"wb{sb_}")
            for f2 in range(2):
                ch = 2 * s + f2
                dst = q_out if ch < 4 else k_out
                row0 = (ch % 4) * 128
                for b in range(NB):
                    bs = slice(b * 512, (b + 1) * 512)
                    b1 = cx.bank()
                    b2 = cx.bank()
                    for c in range(8):
                        sc.op("pe", lambda e, c=c, sa=sa, f2=f2, b1=b1, bs=bs: e.matmul(
                            ps[:, b1, :], lhsT=t["wa"][sa][:, c, 128 * f2:128 * f2 + 128],
                            rhs=t["h"][:, c, bs], start=(c == 0), stop=(c == 7)),
                            reads=[("wa", sa), ("h", c, b)], writes=[("ps", b1)])
                    for c in range(8):
                        sc.op("pe", lambda e, c=c, sb_=sb_, f2=f2, b2=b2, bs=bs: e.matmul(
                            ps[:, b2, :], lhsT=t["wb"][sb_][:, c, 128 * f2:128 * f2 + 128],
                            rhs=t["h"][:, c, bs], start=(c == 0), stop=(c == 7)),
                            reads=[("wb", sb_), ("h", c, b)], writes=[("ps", b2)])
                    s1 = cx.slot("tmpf", 4)
                    s2 = cx.slot("tmpf", 4)
                    so = cx.slot("stg16", 4)
                    sc.op("dve", lambda e, b1=b1, s1=s1, bs=bs: e.tensor_tensor(
                        out=t["tmpf"][s1][:], in0=ps[:, b1, :], in1=t["cos"][:, bs], op=ALU.mult),
                        reads=[("ps", b1), ("cos",)], writes=[("tmpf", s1)])
                    sc.op("dve", lambda e, b2=b2, s2=s2, bs=bs: e.tensor_tensor(
                        out=t["tmpf"][s2][:], in0=ps[:, b2, :], in1=t["sin"][:, bs], op=ALU.mult),
                        reads=[("ps", b2), ("sin",)], writes=[("tmpf", s2)])
                    sc.op("dve", lambda e, s1=s1, s2=s2, so=so: e.tensor_tensor(
                        out=t["stg16"][so][:], in0=t["tmpf"][s1][:], in1=t["tmpf"][s2][:], op=ALU.add),
                        reads=[("tmpf", s1), ("tmpf", s2)], writes=[("stg16", so)])
                    c0 = hf * HALF + b * 512
                    sc.op("sp", lambda e, dst=dst, row0=row0, c0=c0, so=so: e.dma_start(
                        out=dst[row0:row0 + 128, c0:c0 + 512], in_=t["stg16"][so][:]),
                        reads=[("stg16", so)], dma=f"stg16_{so}")
        for s in range(2):
            sa = cx.slot("wa", 3)
            sc.op("pool", lambda e, s=s, sa=sa: e.dma_start(
                out=t["wa"][sa][:], in_=winv[:, :, 1536 + 256 * s:1536 + 256 * s + 256]),
                writes=[("wa", sa)], dma=f"wa{sa}")
            for f2 in range(2):
                ch = 2 * s + f2
                for b in range(NB):
                    bs = slice(b * 512, (b + 1) * 512)
                    b1 = cx.bank()
                    for c in range(8):
                        sc.op("pe", lambda e, c=c, sa=sa, f2=f2, b1=b1, bs=bs: e.matmul(
                            ps[:, b1, :], lhsT=t["wa"][sa][:, c, 128 * f2:128 * f2 + 128],
                            rhs=t["h"][:, c, bs], start=(c == 0), stop=(c == 7)),
                            reads=[("wa", sa), ("h", c, b)], writes=[("ps", b1)])
                    so = cx.slot("stg32", 2)
                    sc.op("act", lambda e, b1=b1, so=so: e.activation(
                        out=t["stg32"][so][:], in_=ps[:, b1, :], func=AF.Copy),
                        reads=[("ps", b1)], writes=[("stg32", so)])
                    c0 = hf * HALF + b * 512
                    sc.op("sp", lambda e, ch=ch, c0=c0, so=so: e.dma_start(
                        out=u_out[ch * 128:ch * 128 + 128, c0:c0 + 512], in_=t["stg32"][so][:]),
                        reads=[("stg32", so)], dma=f"stg32_{so}")
        for s in range(2):
            sa = cx.slot("wa", 3)
            sc.op("pool", lambda e, s=s, sa=sa: e.dma_start(
                out=t["wa"][sa][:], in_=winv[:, :, 1024 + 256 * s:1024 + 256 * s + 256]),
                writes=[("wa", sa)], dma=f"wa{sa}")
            for tt in range(8):
                b = tt // 4
                b1 = cx.bank()
                for c in range(8):
                    sc.op("pe", lambda e, c=c, sa=sa, tt=tt, b1=b1: e.matmul(
                        ps[:, b1, 0:256], lhsT=t["h"][:, c, tt * 128:tt * 128 + 128],
                        rhs=t["wa"][sa][:, c, :], start=(c == 0), stop=(c == 7)),
                        reads=[("wa", sa), ("h", c, b)], writes=[("ps", b1)])
                so = cx.slot("stg16", 4)
                sc.op("act", lambda e, b1=b1, so=so: e.activation(
                    out=t["stg16"][so][:, 0:256], in_=ps[:, b1, 0:256], func=AF.Copy),
                    reads=[("ps", b1)], writes=[("stg16", so)])
                r0 = hf * HALF + tt * 128
                sc.op("sp", lambda e, r0=r0, s=s, so=so: e.dma_start(
                    out=v_out[r0:r0 + 128, 256 * s:256 * s + 256], in_=t["stg16"][so][:, 0:256]),
                    reads=[("stg16", so)], dma=f"stg16_{so}")
    return cx.finish()


def build_B():
    cx = Ctx()
    sc = cx.sc
    NQB = S // 512
    NKT = S // 128
    q_in = cx.din("q_in", [128, S], BF16)
    k_in = cx.din("k_in", [128, S // 2], BF16)
    v_in = cx.din("v_in", [128, NKT * 128], BF16)
    m_in = cx.din("m_in", [128, 4 * 512], BF16)
    o_out = cx.dout("o_out", [128, S])

    qT = cx.sb("qT", [128, S], BF16)
    kT = cx.sb("kT", [128, S // 2], BF16)
    pTs = [cx.sb(f"pTs{i}", [128, 512], BF16) for i in range(8)]
    vv = cx.sb("vv", [128, NKT * 128], BF16)
    mk = cx.sb("mk", [128, 4 * 512], BF16)
    mkv = mk[:].rearrange("p (j q) -> p j q", j=4)
    ones = cx.sb("ones", [128, 128], BF16)
    NP = 8
    pT = [cx.sb(f"pT{i}", [128, 2, 512], BF16) for i in range(NP)]
    rl = cx.sb("rl", [128, 512], F32)
    ostg = [cx.sb(f"ostg{i}", [128, 512], F32) for i in range(2)]
    ps = cx.ps("psum", [128, 8, 512], F32)

    sc.op("pool", lambda e: e.memset(ones[:], 1.0), writes=[("ones",)])
    sc.op("sp", lambda e: e.dma_start(out=mk[:], in_=m_in), writes=[("mk",)], dma="mk")
    NCH = 8
    cw = S // NCH
    for i in range(NCH):
        sc.op("sp", lambda e, i=i: e.dma_start(out=qT[:, i * cw:(i + 1) * cw], in_=q_in[:, i * cw:(i + 1) * cw]),
              writes=[("q", i)], dma=f"q{i}")
        sc.op("sp", lambda e, i=i: e.dma_start(out=kT[:, i * (cw // 2):(i + 1) * (cw // 2)],
                                               in_=k_in[:, i * (cw // 2):(i + 1) * (cw // 2)]),
              writes=[("k", i)], dma=f"k{i}")
        sc.op("sp", lambda e, i=i: e.dma_start(out=vv[:, i * cw:(i + 1) * cw], in_=v_in[:, i * cw:(i + 1) * cw]),
              writes=[("v", i)], dma=f"v{i}")

    groups = [(qb, g) for qb in range(NQB) for g in range(2 * (qb + 1))]
    LOOK = 2
    LLAG = 4
    ginfo = {}

    def emit_S(i):
        qb, g = groups[i]
        b0 = 2 * (i % 2)
        sl = i % NP
        ginfo[i] = (b0, sl)
        for t2 in range(2):
            sc.op("pe", lambda e, t2=t2: e.matmul(
                ps[0:128, b0 + t2, :], lhsT=kT[64 * t2:64 * t2 + 64, g * 128:(g + 1) * 128],
                rhs=qT[64 * t2:64 * t2 + 64, qb * 512:(qb + 1) * 512], start=True, stop=True),
                reads=[("k", g * 128 // (cw // 2)), ("q", qb * 512 // cw)], writes=[("ps", b0 + t2)])
        sc.op("act", lambda e: e.activation(out=pT[sl][:], in_=ps[:, b0:b0 + 2, :], func=AF.Exp, scale=0.125),
              reads=[("ps", b0), ("ps", b0 + 1)], writes=[("pT", sl)])
        j2 = g - 2 * qb
        if j2 >= 0:
            sc.op("dve", lambda e: e.tensor_tensor(out=pT[sl][:], in0=pT[sl][:],
                                                  in1=mkv[:, j2 * 2:j2 * 2 + 2, :], op=ALU.mult),
                  reads=[("pT", sl), ("mk",)], writes=[("pT", sl)])

    def emit_PV(i):
        qb, g = groups[i]
        b0, sl = ginfo[i]
        ob = 4 + (qb % 2)
        lb = 6 + (qb % 2)
        ng = 2 * (qb + 1)
        for t2 in range(2):
            kt = 2 * g + t2
            sc.op("pe", lambda e, kt=kt, t2=t2: e.matmul(
                ps[:, ob, :], lhsT=vv[:, kt * 128:(kt + 1) * 128], rhs=pT[sl][:, t2, :],
                start=(kt == 0), stop=(kt == 2 * ng - 1)),
                reads=[("pT", sl), ("v", kt * 128 // cw)], writes=[("ps", ob)])
        s2 = i % 8
        sc.op("dve", lambda e: e.tensor_tensor(out=pTs[s2][:], in0=pT[sl][:, 0, :], in1=pT[sl][:, 1, :], op=ALU.add),
              reads=[("pT", sl)], writes=[("pTs", s2)])

    def emit_L(i):
        qb, g = groups[i]
        ob = 4 + (qb % 2)
        lb = 6 + (qb % 2)
        ng = 2 * (qb + 1)
        s2 = i % 8
        sc.op("pe", lambda e: e.matmul(
            ps[:, lb, :], lhsT=ones[:], rhs=pTs[s2][:], start=(g == 0), stop=(g == ng - 1)),
            reads=[("pTs", s2), ("ones",)], writes=[("ps", lb)])
        if g == ng - 1:
            so = qb % 2
            sc.op("dve", lambda e: e.reciprocal(out=rl[:], in_=ps[:, lb, :]),
                  reads=[("ps", lb)], writes=[("rl",)])
            sc.op("dve", lambda e: e.tensor_tensor(out=ostg[so][:], in0=ps[:, ob, :], in1=rl[:], op=ALU.mult),
                  reads=[("ps", ob), ("rl",)], writes=[("ostg", so)])
            sc.op("sp", lambda e: e.dma_start(out=o_out[:, qb * 512:(qb + 1) * 512], in_=ostg[so][:]),
                  reads=[("ostg", so)], dma=f"ostg{so}")

    n = len(groups)
    for i in range(n + LOOK + LLAG):
        if i < n:
            emit_S(i)
        if 0 <= i - LOOK < n:
            emit_PV(i - LOOK)
        if 0 <= i - LOOK - LLAG < n:
            emit_L(i - LOOK - LLAG)
    return cx.finish()


def build_C():
    cx = Ctx()
    sc = cx.sc
    x_in = cx.din("x_in", [D, T])
    o_in = cx.din("o_in", [8 * 128, T])
    u_in = cx.din("u_in", [512, 16 + T])
    icnt = cx.din("icnt", [128, 4 * 16])
    lam4 = cx.din("lam4", [128, 4 * 64])
    lconst = cx.din("lconst", [128, 2])
    sgain = cx.din("sgain", [128, 1])
    pw = cx.din("pw", [4 * 128, 128])
    pscale = cx.din("pscale", [128, 4])
    wout = cx.din("wout", [D, D])
    g3 = cx.din("g3", [128, 8])
    gf = cx.din("gf", [128, 8])
    epsd = cx.din("epsd", [128, 1])
    wg = cx.din("wg", [D, DFF])
    wu = cx.din("wu", [D, DFF])
    wd = cx.din("wd", [DFF, D])
    x_out = cx.dout("x_out", [D, T])
    y_out = cx.dout("y_out", [D, T])

    t = alloc_common(cx)
    ps = t["psum"]
    for nm, ap, n in (("g3", g3, 8), ("gf", gf, 8), ("eps", epsd, 1), ("lam4", lam4, 256),
                      ("lconst", lconst, 2), ("sgain", sgain, 1), ("pscale", pscale, 4),
                      ("icnt", icnt, 64)):
        load_vec(cx, t, nm, ap, n)
    t["lt"] = cx.sb("lt", [128, 128], F32)
    t["ls"] = cx.sb("ls", [128, 8], F32)
    sc.op("dve", lambda e: e.tensor_tensor(out=t["lt"][:, 0:64], in0=t["lam4"][:, 0:64],
                                           in1=t["lam4"][:, 64:128], op=ALU.mult),
          reads=[("lam4",)], writes=[("lt", 0)])
    sc.op("dve", lambda e: e.tensor_tensor(out=t["lt"][:, 64:128], in0=t["lam4"][:, 128:192],
                                           in1=t["lam4"][:, 192:256], op=ALU.mult),
          reads=[("lam4",)], writes=[("lt", 1)])
    sc.op("dve", lambda e: e.tensor_reduce(out=t["ls"][:, 0:1], in_=t["lt"][:, 0:64],
                                           axis=mybir.AxisListType.X, op=ALU.add),
          reads=[("lt", 0)], writes=[("ls", 0)])
    sc.op("dve", lambda e: e.tensor_reduce(out=t["ls"][:, 1:2], in_=t["lt"][:, 64:128],
                                           axis=mybir.AxisListType.X, op=ALU.add),
          reads=[("lt", 1)], writes=[("ls", 1)])
    sc.op("act", lambda e: e.activation(out=t["ls"][:, 2:4], in_=t["ls"][:, 0:2], func=AF.Exp),
          reads=[("ls", 0), ("ls", 1)], writes=[("ls", 2)])
    sc.op("dve", lambda e: e.tensor_tensor(out=t["ls"][:, 4:5], in0=t["ls"][:, 3:4], in1=t["ls"][:, 2:3],
                                           op=ALU.subtract),
          reads=[("ls", 2)], writes=[("ls", 4)])
    sc.op("dve", lambda e: e.tensor_tensor(out=t["ls"][:, 5:6], in0=t["ls"][:, 4:5], in1=t["lconst"][:, 0:1],
                                           op=ALU.subtract),
          reads=[("ls", 4), ("lconst",)], writes=[("neglam",)])
    sc.op("dve", lambda e: e.tensor_tensor(out=t["ls"][:, 6:7], in0=t["sgain"][:, 0:1], in1=t["lconst"][:, 1:2],
                                           op=ALU.mult),
          reads=[("sgain",), ("lconst",)], writes=[("sg",)])

    t["o1"] = [cx.sb(f"o1_{i}", [128, 512], F32) for i in range(3)]
    t["o2"] = [cx.sb(f"o2_{i}", [128, 512], F32) for i in range(3)]
    t["od"] = [cx.sb(f"od_{i}", [128, 512], F32) for i in range(3)]
    t["uu"] = cx.sb("uu", [128, 16 + HALF], F32)
    t["ua"] = cx.sb("ua", [128, 16 + HALF], F32)
    t["ub"] = cx.sb("ub", [128, 16 + HALF], F32)
    t["dif"] = cx.sb("dif", [128, HALF], BF16)
    t["pwt"] = cx.sb("pwt", [128, 4, 128], BF16)
    sc.op("pool", lambda e: e.dma_start(out=t["pwt"][:], in_=pw.rearrange("(g c) e -> c g e", c=128)),
          writes=[("pwt",)], dma="pwt")
    woutv = wout.rearrange("(c p) f -> p c f", p=128)
    WINS = (2, 4, 8, 16)

    for hf in range(2):
        load_x(cx, t, x_in, hf)
        items = [(hd, b) for hd in range(4) for b in range(NB)]
        st = {}

        def sub1(i):
            hd, b = items[i]
            c0 = hf * HALF + b * 512
            s1 = cx.slot("o1", 3)
            s2 = cx.slot("o2", 3)
            sd_ = cx.slot("od", 3)
            sc.op("sp", lambda e: e.dma_start(
                out=t["o1"][s1][:], in_=o_in[(2 * hd) * 128:(2 * hd) * 128 + 128, c0:c0 + 512]),
                writes=[("o1", s1)], dma=f"o1_{s1}")
            sc.op("sp", lambda e: e.dma_start(
                out=t["o2"][s2][:], in_=o_in[(2 * hd + 1) * 128:(2 * hd + 1) * 128 + 128, c0:c0 + 512]),
                writes=[("o2", s2)], dma=f"o2_{s2}")
            sc.op("dve", lambda e: e.scalar_tensor_tensor(
                out=t["od"][sd_][:], in0=t["o2"][s2][:], scalar=t["ls"][:, 5:6], in1=t["o1"][s1][:],
                op0=ALU.mult, op1=ALU.add),
                reads=[("o1", s1), ("o2", s2), ("neglam",)], writes=[("od", sd_)])
            sq = cx.slot("sq", 4)
            sc.op("act", lambda e: e.activation(
                out=t["sq"][sq][:], in_=t["od"][sd_][:], func=AF.Square),
                reads=[("od", sd_)], writes=[("sq", sq)])
            bk = cx.bank()
            sc.op("pe", lambda e: e.matmul(
                ps[:, bk, :], lhsT=t["ones"][:], rhs=t["sq"][sq][:], start=True, stop=True),
                reads=[("sq", sq), ("ones",)], writes=[("ps", bk)])
            st[i] = (sd_, bk)

        def sub2(i):
            hd, b = items[i]
            bs = slice(b * 512, (b + 1) * 512)
            sd_, bk = st[i]
            ss = cx.slot("std", 2)
            rs = cx.slot("rstd", 2)
            sc.op("act", lambda e: e.activation(
                out=t["std"][ss][:], in_=ps[:, bk, :], func=AF.Ln, bias=t["eps"][:, 0:1], scale=1.0 / 128),
                reads=[("ps", bk), ("eps",)], writes=[("std", ss)])
            sc.op("act", lambda e: e.activation(
                out=t["rstd"][rs][:], in_=t["std"][ss][:], func=AF.Exp, scale=-0.5),
                reads=[("std", ss)], writes=[("rstd", rs)])
            sc.op("dve", lambda e: e.scalar_tensor_tensor(
                out=t["h"][:, hd, bs], in0=t["od"][sd_][:], scalar=t["ls"][:, 6:7],
                in1=t["rstd"][rs][:], op0=ALU.mult, op1=ALU.mult),
                reads=[("od", sd_), ("sg",), ("rstd", rs)], writes=[("h", hd, b)])

        for i in range(len(items) + 1):
            if i < len(items):
                sub1(i)
            if i >= 1:
                sub2(i - 1)
        for g in range(4):
            w = WINS[g]
            c0 = hf * HALF
            sc.op("sp", lambda e, g=g, c0=c0: e.dma_start(
                out=t["uu"][:], in_=u_in[g * 128:g * 128 + 128, c0:c0 + 16 + HALF]),
                writes=[("uu",)], dma="uu")
            src = "uu"
            sh = 1
            flip = 0
            while sh < w:
                dst = "ua" if flip == 0 else "ub"
                sc.op("dve", lambda e, src=src, dst=dst, sh=sh: e.tensor_tensor(
                    out=t[dst][:, sh:16 + HALF], in0=t[src][:, sh:16 + HALF],
                    in1=t[src][:, 0:16 + HALF - sh], op=ALU.add),
                    reads=[(src,)], writes=[(dst,)])
                src = dst
                flip ^= 1
                sh *= 2
            oth = "ua" if src == "ub" else "ub"
            sc.op("dve", lambda e, src=src, w=w: e.scalar_tensor_tensor(
                out=t["dif"][:, 0:HALF], in0=t[src][:, 16:16 + HALF], scalar=1.0 / w,
                in1=t["uu"][:, 16:16 + HALF], op0=ALU.mult, op1=ALU.subtract),
                reads=[(src,), ("uu",)], writes=[("dif",)])
            if hf == 0:
                sc.op("dve", lambda e, src=src, oth=oth, g=g: e.tensor_tensor(
                    out=t[oth][:, 16:32], in0=t[src][:, 16:32],
                    in1=t["icnt"][:, g * 16:g * 16 + 16], op=ALU.mult),
                    reads=[(src,), ("icnt",)], writes=[(oth,)])
                sc.op("dve", lambda e, oth=oth: e.tensor_tensor(
                    out=t["dif"][:, 0:16], in0=t[oth][:, 16:32], in1=t["uu"][:, 16:32], op=ALU.subtract),
                    reads=[(oth,), ("uu",), ("dif",)], writes=[("dif",)])
            for b in range(NB):
                bs = slice(b * 512, (b + 1) * 512)
                bk = cx.bank()
                sc.op("pe", lambda e, g=g, bk=bk, bs=bs: e.matmul(
                    ps[:, bk, :], lhsT=t["pwt"][:, g, :], rhs=t["dif"][:, bs], start=True, stop=True),
                    reads=[("pwt",), ("dif",)], writes=[("ps", bk)])
                sc.op("dve", lambda e, g=g, bk=bk, bs=bs: e.tensor_scalar(
                    out=t["h"][:, 4 + g, bs], in0=ps[:, bk, :], scalar1=t["pscale"][:, g:g + 1],
                    scalar2=None, op0=ALU.mult),
                    reads=[("ps", bk), ("pscale",)], writes=[("h", 4 + g, b)])
        for s in range(4):
            sa = cx.slot("wa", 3)
            sc.op("pool", lambda e, s=s, sa=sa: e.dma_start(
                out=t["wa"][sa][:], in_=woutv[:, :, 256 * s:256 * s + 256]),
                writes=[("wa", sa)], dma=f"wa{sa}")
            for f2 in range(2):
                j = 2 * s + f2
                for b in range(NB):
                    bs = slice(b * 512, (b + 1) * 512)
                    bk = cx.bank()
                    for c in range(8):
                        sc.op("pe", lambda e, c=c, sa=sa, f2=f2, bk=bk, bs=bs: e.matmul(
                            ps[:, bk, :], lhsT=t["wa"][sa][:, c, 128 * f2:128 * f2 + 128],
                            rhs=t["h"][:, c, bs], start=(c == 0), stop=(c == 7)),
                            reads=[("wa", sa), ("h", c, b)], writes=[("ps", bk)])
                    sc.op("dve", lambda e, j=j, bk=bk, bs=bs: e.tensor_tensor(
                        out=t["x"][:, j, bs], in0=ps[:, bk, :], in1=t["x"][:, j, bs], op=ALU.add),
                        reads=[("ps", bk), ("x", j, b)], writes=[("x", j, b)])
        ffn(cx, t, "g3", wg, wu, wd)
        store_x(cx, t, x_out, hf)
        rmsnorm_final(cx, t, y_out, hf)
    return cx.finish()


def rmsnorm_final(cx, t, y_out, hf):
    sc = cx.sc
    ps = t["psum"]
    yv = y_out.rearrange("(c p) n -> p c n", p=128)
    for b in range(NB):
        bs = slice(b * 512, (b + 1) * 512)
        bk = cx.bank()
        for c in range(8):
            s = cx.slot("sq", 4)
            sc.op("act", lambda e, c=c, s=s, bs=bs: e.activation(
                out=t["sq"][s][:], in_=t["x"][:, c, bs], func=AF.Square),
                reads=[("x", c, b)], writes=[("sq", s)])
            sc.op("pe", lambda e, c=c, s=s, bk=bk: e.matmul(
                ps[:, bk, :], lhsT=t["ones"][:], rhs=t["sq"][s][:], start=(c == 0), stop=(c == 7)),
                reads=[("sq", s), ("ones",)], writes=[("ps", bk)])
        ss = cx.slot("std", 2)
        sc.op("act", lambda e, bk=bk, ss=ss: e.activation(
            out=t["std"][ss][:], in_=ps[:, bk, :], func=AF.Ln, bias=t["eps"][:, 0:1], scale=1.0 / D),
            reads=[("ps", bk), ("eps",)], writes=[("std", ss)])
        sc.op("act", lambda e, b=b, ss=ss: e.activation(
            out=t["rstd"][b][:], in_=t["std"][ss][:], func=AF.Exp, scale=-0.5),
              reads=[("std", ss)], writes=[("rstd", b)])
        for c in range(8):
            so = cx.slot("tmpf", 4)
            sc.op("dve", lambda e, c=c, b=b, bs=bs, so=so: e.scalar_tensor_tensor(
                out=t["tmpf"][so][:], in0=t["x"][:, c, bs], scalar=t["gf"][:, c:c + 1],
                in1=t["rstd"][b][:], op0=ALU.mult, op1=ALU.mult),
                reads=[("x", c, b), ("gf",), ("rstd", b)], writes=[("tmpf", so)])
            c0 = hf * HALF + b * 512
            sc.op("sp", lambda e, c=c, c0=c0, so=so: e.dma_start(
                out=yv[:, c, c0:c0 + 512], in_=t["tmpf"][so][:]),
                reads=[("tmpf", so)], dma=f"tmpf{so}")


_PROGS = {}


def _prog(name):
    if name not in _PROGS:
        _PROGS[name] = {"A": build_A, "B": build_B, "C": build_C}[name]()
    return _PROGS[name]


def _run(name, in_maps):
    res = run_bass_kernel_spmd(_prog(name), in_maps, core_ids=list(range(NCORES)))
    return res.results


def _vec8(g):
    return np.ascontiguousarray(np.asarray(g, np.float32).reshape(8, 128).T)


def _rope_tables():
    d = 64
    inv = (10000.0 ** (-np.arange(0, d, 2, dtype=np.float32) / d)).astype(np.float32)
    ang = np.arange(S, dtype=np.float32)[:, None] * inv[None, :]
    ang = np.concatenate([ang, ang], axis=-1)
    cos = np.cos(ang).astype(np.float32).T
    sin = np.sin(ang).astype(np.float32).T
    sin[:32] *= -1.0
    cos2 = np.concatenate([cos, cos], axis=0)
    sin2 = np.concatenate([sin, sin], axis=0)
    return cos2, sin2


def kernel(x, ffn1_norm, ffn1_w_gate, ffn1_w_up, ffn1_w_down, mix_norm, w_in,
           lambda_q1, lambda_k1, lambda_q2, lambda_k2, subln_gain, pool_w, pool_scale,
           w_out, ffn2_norm, ffn2_w_gate, ffn2_w_up, ffn2_w_down, final_norm):
    f = lambda a: np.ascontiguousarray(np.asarray(a, dtype=np.float32))
    x = f(x)
    xT = [np.ascontiguousarray(x[0, c * T:(c + 1) * T, :].T) for c in range(NCORES)]
    cos2, sin2 = _rope_tables()
    epsd = np.full((128, 1), EPS, np.float32)
    kp = np.arange(128)[:, None, None] + 128 * np.arange(4)[None, :, None]
    qq = np.arange(512)[None, None, :]
    mask = (kp <= qq).astype(np.float32).reshape(128, 4 * 512).astype(ml_dtypes.bfloat16)
    perm = (np.arange(1024).reshape(16, 2, 32)[:, ::-1, :]).reshape(-1)
    WINS = (2, 4, 8, 16)
    y = None
    for l in range(DEPTH):
        wl = f(w_in[l])
        winp = np.ascontiguousarray(wl[:, :1024][:, perm])
        ins = []
        for c in range(NCORES):
            ins.append(dict(x_in=xT[c], g1=_vec8(ffn1_norm[l]), g2=_vec8(mix_norm[l]), epsd=epsd,
                            wg=f(ffn1_w_gate[l]), wu=f(ffn1_w_up[l]), wd=f(ffn1_w_down[l]),
                            win=wl, winp=winp,
                            cosd=np.ascontiguousarray(cos2[:, c * T:(c + 1) * T]),
                            sind=np.ascontiguousarray(sin2[:, c * T:(c + 1) * T])))
        ra = _run("A", ins)
        xT = [ra[c]["x_out"] for c in range(NCORES)]
        qT = np.concatenate([ra[c]["q_out"] for c in range(NCORES)], axis=1)
        kT = np.concatenate([ra[c]["k_out"] for c in range(NCORES)], axis=1)
        v = np.concatenate([ra[c]["v_out"] for c in range(NCORES)], axis=0)
        uT = np.concatenate([ra[c]["u_out"] for c in range(NCORES)], axis=1)
        ins = []
        for c in range(NCORES):
            h, m = c // 2, c % 2
            vh = v[:, h * 128:(h + 1) * 128].reshape(S // 128, 128, 128).transpose(1, 0, 2)
            qc = qT[c * 64:(c + 1) * 64]
            kc = kT[c * 64:(c + 1) * 64].reshape(64, S // 256, 2, 128).transpose(2, 0, 1, 3)
            ins.append(dict(q_in=np.ascontiguousarray(np.concatenate([qc, qc], axis=0)),
                            k_in=np.ascontiguousarray(kc).reshape(128, S // 2),
                            v_in=np.ascontiguousarray(vh).reshape(128, -1),
                            m_in=mask))
        rb = _run("B", ins)
        oT = np.concatenate([rb[c]["o_out"] for c in range(NCORES)], axis=0)
        lam_init = 0.8 - 0.6 * math.exp(-0.3 * l)
        lam4 = np.concatenate([f(lambda_q1[l]), f(lambda_k1[l]), f(lambda_q2[l]), f(lambda_k2[l])])
        lam4 = np.ascontiguousarray(np.broadcast_to(lam4[None, :], (128, 256)))
        lconst = np.ascontiguousarray(np.broadcast_to(
            np.array([lam_init, 1.0 - lam_init], np.float32)[None, :], (128, 2)))
        upad = np.concatenate([np.zeros((512, 16), np.float32), uT], axis=1)
        ins = []
        for c in range(NCORES):
            icnt = np.zeros((128, 64), np.float32)
            for g in range(4):
                pos = c * T + np.arange(16)
                icnt[:, g * 16:(g + 1) * 16] = (1.0 / np.minimum(pos + 1, WINS[g]))[None, :]
            ins.append(dict(x_in=xT[c], o_in=np.ascontiguousarray(oT[:, c * T:(c + 1) * T]),
                            u_in=np.ascontiguousarray(upad[:, c * T:c * T + 16 + T]),
                            icnt=icnt, lam4=lam4, lconst=lconst,
                            sgain=f(subln_gain[l]).reshape(128, 1),
                            pw=f(pool_w[l]).reshape(512, 128),
                            pscale=np.ascontiguousarray(f(pool_scale[l]).reshape(4, 128).T),
                            wout=f(w_out[l]), g3=_vec8(ffn2_norm[l]), gf=_vec8(final_norm), epsd=epsd,
                            wg=f(ffn2_w_gate[l]), wu=f(ffn2_w_up[l]), wd=f(ffn2_w_down[l])))
        rc = _run("C", ins)
        xT = [rc[c]["x_out"] for c in range(NCORES)]
        y = [rc[c]["y_out"] for c in range(NCORES)]
    out = np.concatenate([yc.T for yc in y], axis=0)[None]
    return np.ascontiguousarray(out.astype(np.float32))
```

```python
import math
from contextlib import ExitStack

import numpy as np
import ml_dtypes

import concourse.bass as bass
import concourse.mybir as mybir
from concourse.bass_utils import run_bass_kernel_spmd

F32 = mybir.dt.float32
BF16 = mybir.dt.bfloat16
AF = mybir.ActivationFunctionType
ALU = mybir.AluOpType

NCORES = 8
D = 1024
S = 16384
DEPTH = 4
DFF = 2816
NFC = DFF // 128
T = S // NCORES
HALF = 1024
NB = 2
EPS = 1e-6
SAME_ENGINE_SYNC = True


class Sched:
    ENGS = ("pe", "act", "dve", "pool", "sp")

    def __init__(self):
        self.ops = []
        self.last_writer = {}
        self.readers = {}
        self.dma_count = {}

    def op(self, eng, fn, reads=(), writes=(), dma=None):
        idx = len(self.ops)
        deps = set()
        for r in reads:
            if r in self.last_writer:
                deps.add(self.last_writer[r])
        for w in writes:
            if w in self.last_writer:
                deps.add(self.last_writer[w])
            for rd in self.readers.get(w, ()):
                deps.add(rd)
        best = {}
        for d in deps:
            od = self.ops[d]
            k = ("dma", od["dma"]) if od["dma"] is not None else ("eng", od["eng"])
            if k not in best or d > best[k]:
                best[k] = d
        deps = set(best.values())
        o = dict(eng=eng, fn=fn, deps=deps, dma=dma, signal=False, sig_idx=None,
                 dma_val=None)
        if dma is not None:
            self.dma_count[dma] = self.dma_count.get(dma, 0) + 1
            o["dma_val"] = 16 * self.dma_count[dma]
        self.ops.append(o)
        for w in writes:
            self.last_writer[w] = idx
            self.readers[w] = []
        for r in reads:
            self.readers.setdefault(r, []).append(idx)
        return idx

    def finalize(self):
        ops = self.ops
        for o in ops:
            for d in o["deps"]:
                od = ops[d]
                if od["dma"] is None:
                    if od["eng"] != o["eng"] or (SAME_ENGINE_SYNC and od["eng"] != "pe"
                                                 and o["dma"] is None):
                        od["signal"] = True
                    elif o["dma"] is not None and od["eng"] == o["eng"]:
                        od["signal"] = True
        cnt = {e: 0 for e in self.ENGS}
        for o in ops:
            if o["signal"]:
                cnt[o["eng"]] += 1
                o["sig_idx"] = cnt[o["eng"]]
        waited = {e: {} for e in self.ENGS}
        for o in ops:
            w = {}
            for d in o["deps"]:
                od = ops[d]
                if od["dma"] is not None:
                    key = ("dma", od["dma"])
                    val = od["dma_val"]
                else:
                    if not od["signal"]:
                        continue
                    key = ("tl", od["eng"])
                    val = od["sig_idx"]
                w[key] = max(w.get(key, 0), val)
            wl = []
            for key, val in w.items():
                if waited[o["eng"]].get(key, 0) >= val:
                    continue
                waited[o["eng"]][key] = val
                wl.append((key, val))
            o["waits"] = wl

    def emit(self, nc, stack):
        self.finalize()
        sems = {}
        for e in self.ENGS:
            sems[("tl", e)] = stack.enter_context(nc.semaphore("tl_" + e))
        for k in self.dma_count:
            sems[("dma", k)] = stack.enter_context(nc.semaphore("dma_" + str(k)))
        block = stack.enter_context(nc.Block())
        ops = self.ops
        dma_final = dict(self.dma_count)

        def run(eng_name, e, final=False):
            for o in ops:
                if o["eng"] != eng_name:
                    continue
                for key, val in o["waits"]:
                    e.wait_ge(sems[key], val)
                inst = o["fn"](e)
                if o["dma"] is not None:
                    inst.then_inc(sems[("dma", o["dma"])], 16)
                elif o["signal"]:
                    inst.then_inc(sems[("tl", eng_name)], 1)
            if final:
                for k, n in dma_final.items():
                    e.wait_ge(sems[("dma", k)], 16 * n)

        @block.tensor
        def _(e):
            run("pe", e)

        @block.scalar
        def _(e):
            run("act", e)

        @block.vector
        def _(e):
            run("dve", e)

        @block.gpsimd
        def _(e):
            run("pool", e)

        @block.sync
        def _(e):
            run("sp", e, final=True)


class Ctx:
    def __init__(self):
        self.nc = bass.Bass("TRN2", target_bir_lowering=False)
        self.sc = Sched()
        self.stack = ExitStack()
        self.bank_ctr = 0
        self.rot = {}

    def din(self, name, shape, dt=F32):
        return self.nc.dram_tensor(name, list(shape), dt, kind="ExternalInput").ap()

    def dout(self, name, shape, dt=F32):
        return self.nc.dram_tensor(name, list(shape), dt, kind="ExternalOutput").ap()

    def sb(self, name, shape, dt):
        return self.stack.enter_context(self.nc.sbuf_tensor("s_" + name, list(shape), dt))

    def ps(self, name, shape, dt=F32):
        return self.stack.enter_context(self.nc.psum_tensor("p_" + name, list(shape), dt))

    def bank(self):
        b = self.bank_ctr % 8
        self.bank_ctr += 1
        return b

    def slot(self, name, n):
        v = self.rot.get(name, 0)
        self.rot[name] = v + 1
        return v % n

    def finish(self):
        self.sc.emit(self.nc, self.stack)
        self.stack.close()
        return self.nc


def alloc_common(cx):
    t = {}
    t["x"] = cx.sb("x", [128, 8, HALF], F32)
    t["h"] = cx.sb("h", [128, 8, HALF], BF16)
    t["act"] = cx.sb("act", [128, NFC, HALF], BF16)
    t["wa"] = [cx.sb(f"wa{i}", [128, 8, 256], BF16) for i in range(3)]
    t["wb"] = [cx.sb(f"wb{i}", [128, 8, 256], BF16) for i in range(3)]
    t["wd"] = [cx.sb(f"wd{i}", [128, 4, 512], BF16) for i in range(3)]
    t["sq"] = [cx.sb(f"sq{i}", [128, 512], BF16) for i in range(4)]
    t["std"] = [cx.sb(f"std{i}", [128, 512], F32) for i in range(2)]
    t["rscr"] = cx.sb("rscr", [128, 512], F32)
    t["rstd"] = [cx.sb(f"rstd{i}", [128, 512], F32) for i in range(2)]
    t["tmpf"] = [cx.sb(f"tmpf{i}", [128, 512], F32) for i in range(4)]
    t["ones"] = cx.sb("ones", [128, 128], BF16)
    t["psum"] = cx.ps("psum", [128, 8, 512], F32)
    cx.sc.op("pool", lambda e: e.memset(t["ones"][:], 1.0), writes=[("ones",)])
    return t


def load_vec(cx, t, name, dram_ap, ncol):
    t[name] = cx.sb(name, [128, ncol], F32)
    cx.sc.op("sp", lambda e: e.dma_start(out=t[name][:], in_=dram_ap),
             writes=[(name,)], dma=name)


def rmsnorm_to_h(cx, t, gname, nchunk=8, src="x", dst="h", scale_d=D):
    sc = cx.sc
    ps = t["psum"]
    for b in range(NB):
        bs = slice(b * 512, (b + 1) * 512)
        bk = cx.bank()
        for c in range(nchunk):
            s = cx.slot("sq", 4)
            sc.op("act", lambda e, c=c, s=s, bs=bs: e.activation(
                out=t["sq"][s][:], in_=t[src][:, c, bs], func=AF.Square),
                reads=[(src, c, b)], writes=[("sq", s)])
            sc.op("pe", lambda e, c=c, s=s, bk=bk: e.matmul(
                ps[:, bk, :], lhsT=t["ones"][:], rhs=t["sq"][s][:],
                start=(c == 0), stop=(c == nchunk - 1)),
                reads=[("sq", s), ("ones",)], writes=[("ps", bk)])
        ss = cx.slot("std", 2)
        sc.op("act", lambda e, bk=bk, ss=ss: e.activation(
            out=t["std"][ss][:], in_=ps[:, bk, :], func=AF.Ln, bias=t["eps"][:, 0:1],
            scale=1.0 / scale_d),
            reads=[("ps", bk), ("eps",)], writes=[("std", ss)])
        sc.op("act", lambda e, b=b, ss=ss: e.activation(
            out=t["rstd"][b][:], in_=t["std"][ss][:], func=AF.Exp, scale=-0.5),
              reads=[("std", ss)], writes=[("rstd", b)])
        for c in range(nchunk):
            sc.op("dve", lambda e, c=c, b=b, bs=bs: e.scalar_tensor_tensor(
                out=t[dst][:, c, bs], in0=t[src][:, c, bs], scalar=t[gname][:, c:c + 1],
                in1=t["rstd"][b][:], op0=ALU.mult, op1=ALU.mult),
                reads=[(src, c, b), (gname,), ("rstd", b)], writes=[(dst, c, b)])


def ffn(cx, t, gname, wg, wu, wd):
    sc = cx.sc
    ps = t["psum"]
    rmsnorm_to_h(cx, t, gname)
    wgv = wg.rearrange("(c p) f -> p c f", p=128)
    wuv = wu.rearrange("(c p) f -> p c f", p=128)
    nslab = NFC // 2
    for s in range(nslab):
        sa = cx.slot("wa", 3)
        sb_ = cx.slot("wb", 3)
        sc.op("pool", lambda e, s=s, sa=sa: e.dma_start(
            out=t["wa"][sa][:], in_=wgv[:, :, 256 * s:256 * s + 256]),
            writes=[("wa", sa)], dma=f"wa{sa}")
        sc.op("pool", lambda e, s=s, sb_=sb_: e.dma_start(
            out=t["wb"][sb_][:], in_=wuv[:, :, 256 * s:256 * s + 256]),
            writes=[("wb", sb_)], dma=f"wb{sb_}")
        for f2 in range(2):
            fc = 2 * s + f2
            for b in range(NB):
                bs = slice(b * 512, (b + 1) * 512)
                bg = cx.bank()
                bu = cx.bank()
                for c in range(8):
                    sc.op("pe", lambda e, c=c, sa=sa, f2=f2, bg=bg, bs=bs: e.matmul(
                        ps[:, bg, :], lhsT=t["wa"][sa][:, c, 128 * f2:128 * f2 + 128],
                        rhs=t["h"][:, c, bs], start=(c == 0), stop=(c == 7)),
                        reads=[("wa", sa), ("h", c, b)], writes=[("ps", bg)])
                for c in range(8):
                    sc.op("pe", lambda e, c=c, sb_=sb_, f2=f2, bu=bu, bs=bs: e.matmul(
                        ps[:, bu, :], lhsT=t["wb"][sb_][:, c, 128 * f2:128 * f2 + 128],
                        rhs=t["h"][:, c, bs], start=(c == 0), stop=(c == 7)),
                        reads=[("wb", sb_), ("h", c, b)], writes=[("ps", bu)])
                ts = cx.slot("tmpf", 4)
                sc.op("act", lambda e, bg=bg, ts=ts: e.activation(
                    out=t["tmpf"][ts][:], in_=ps[:, bg, :], func=AF.Silu),
                    reads=[("ps", bg)], writes=[("tmpf", ts)])
                sc.op("dve", lambda e, bu=bu, ts=ts, fc=fc, bs=bs: e.tensor_tensor(
                    out=t["act"][:, fc, bs], in0=ps[:, bu, :], in1=t["tmpf"][ts][:],
                    op=ALU.mult),
                    reads=[("ps", bu), ("tmpf", ts)], writes=[("act", fc, b)])
    wdv = wd.rearrange("(s p) d -> p s d", p=128)
    for p_ in range(2):
        banks = [[cx.bank() for b in range(NB)] for jj in range(4)]
        nsl = (NFC + 3) // 4
        for s in range(nsl):
            nch = min(4, NFC - 4 * s)
            sd = cx.slot("wd", 3)
            sc.op("pool", lambda e, s=s, sd=sd, nch=nch, p_=p_: e.dma_start(
                out=t["wd"][sd][:, 0:nch, :],
                in_=wdv[:, 4 * s:4 * s + nch, 512 * p_:512 * p_ + 512]),
                writes=[("wd", sd)], dma=f"wd{sd}")
            for f4 in range(nch):
                fc = 4 * s + f4
                for jj in range(4):
                    for b in range(NB):
                        bs = slice(b * 512, (b + 1) * 512)
                        bk = banks[jj][b]
                        sc.op("pe", lambda e, sd=sd, f4=f4, jj=jj, bk=bk, fc=fc, bs=bs: e.matmul(
                            ps[:, bk, :], lhsT=t["wd"][sd][:, f4, 128 * jj:128 * jj + 128],
                            rhs=t["act"][:, fc, bs], start=(fc == 0), stop=(fc == NFC - 1)),
                            reads=[("wd", sd), ("act", fc, b)], writes=[("ps", bk)])
        for jj in range(4):
            j = 4 * p_ + jj
            for b in range(NB):
                bs = slice(b * 512, (b + 1) * 512)
                bk = banks[jj][b]
                sc.op("dve", lambda e, j=j, bk=bk, bs=bs: e.scalar_tensor_tensor(
                    out=t["x"][:, j, bs], in0=ps[:, bk, :], scalar=0.5,
                    in1=t["x"][:, j, bs], op0=ALU.mult, op1=ALU.add),
                    reads=[("ps", bk), ("x", j, b)], writes=[("x", j, b)])


def load_x(cx, t, x_dram, hf, eng="sp"):
    xv = x_dram.rearrange("(c p) n -> p c n", p=128)
    for b in range(NB):
        c0 = hf * HALF + b * 512
        cx.sc.op(eng, lambda e, b=b, c0=c0: e.dma_start(out=t["x"][:, :, b * 512:(b + 1) * 512],
                                                     in_=xv[:, :, c0:c0 + 512]),
                 writes=[("x", c, b) for c in range(8)], dma=f"xin{b}")


def store_x(cx, t, x_dram, hf):
    xv = x_dram.rearrange("(c p) n -> p c n", p=128)
    cx.sc.op("sp", lambda e: e.dma_start(out=xv[:, :, hf * HALF:(hf + 1) * HALF], in_=t["x"][:]),
             reads=[("x", c, b) for c in range(8) for b in range(NB)], dma="xout")


def build_A():
    cx = Ctx()
    sc = cx.sc
    x_in = cx.din("x_in", [D, T])
    g1 = cx.din("g1", [128, 8])
    g2 = cx.din("g2", [128, 8])
    epsd = cx.din("epsd", [128, 1])
    wg = cx.din("wg", [D, DFF])
    wu = cx.din("wu", [D, DFF])
    wd = cx.din("wd", [DFF, D])
    win = cx.din("win", [D, 2048])
    winp = cx.din("winp", [D, 1024])
    cosd = cx.din("cosd", [128, T])
    sind = cx.din("sind", [128, T])
    x_out = cx.dout("x_out", [D, T])
    q_out = cx.dout("q_out", [512, T], BF16)
    k_out = cx.dout("k_out", [512, T], BF16)
    v_out = cx.dout("v_out", [T, 512], BF16)
    u_out = cx.dout("u_out", [512, T])

    t = alloc_common(cx)
    load_vec(cx, t, "g1", g1, 8)
    load_vec(cx, t, "g2", g2, 8)
    load_vec(cx, t, "eps", epsd, 1)
    t["cos"] = cx.sb("cos", [128, HALF], F32)
    t["sin"] = cx.sb("sin", [128, HALF], F32)
    t["stg16"] = [cx.sb(f"stg16_{i}", [128, 512], BF16) for i in range(4)]
    t["stg32"] = [cx.sb(f"stg32_{i}", [128, 512], F32) for i in range(2)]
    ps = t["psum"]
    winv = win.rearrange("(c p) f -> p c f", p=128)
    winpv = winp.rearrange("(c p) f -> p c f", p=128)

    for hf in range(2):
        if hf == 0:
            load_x(cx, t, x_in, hf)
        ffn(cx, t, "g1", wg, wu, wd)
        store_x(cx, t, x_out, hf)
        rmsnorm_to_h(cx, t, "g2")
        if hf == 0:
            load_x(cx, t, x_in, 1, eng="act")
        sc.op("sp", lambda e, hf=hf: e.dma_start(out=t["cos"][:], in_=cosd[:, hf * HALF:(hf + 1) * HALF]),
              writes=[("cos",)], dma="cos")
        sc.op("sp", lambda e, hf=hf: e.dma_start(out=t["sin"][:], in_=sind[:, hf * HALF:(hf + 1) * HALF]),
              writes=[("sin",)], dma="sin")
        for s in range(4):
            sa = cx.slot("wa", 3)
            sb_ = cx.slot("wb", 3)
            sc.op("pool", lambda e, s=s, sa=sa: e.dma_start(
                out=t["wa"][sa][:], in_=winv[:, :, 256 * s:256 * s + 256]),
                writes=[("wa", sa)], dma=f"wa{sa}")
            sc.op("pool", lambda e, s=s, sb_=sb_: e.dma_start(
                out=t["wb"][sb_][:], in_=winpv[:, :, 256 * s:256 * s + 256]),
                writes=[("wb", sb_)], dma=f"wb{sb_}")
            for f2 in range(2):
                ch = 2 * s + f2
                dst = q_out if ch < 4 else k_out
                row0 = (ch % 4) * 128
                for b in range(NB):
                    bs = slice(b * 512, (b + 1) * 512)
                    b1 = cx.bank()
                    b2 = cx.bank()
                    for c in range(8):
                        sc.op("pe", lambda e, c=c, sa=sa, f2=f2, b1=b1, bs=bs: e.matmul(
                            ps[:, b1, :], lhsT=t["wa"][sa][:, c, 128 * f2:128 * f2 + 128],
                            rhs=t["h"][:, c, bs], start=(c == 0), stop=(c == 7)),
                            reads=[("wa", sa), ("h", c, b)], writes=[("ps", b1)])
                    for c in range(8):
                        sc.op("pe", lambda e, c=c, sb_=sb_, f2=f2, b2=b2, bs=bs: e.matmul(
                            ps[:, b2, :], lhsT=t["wb"][sb_][:, c, 128 * f2:128 * f2 + 128],
                            rhs=t["h"][:, c, bs], start=(c == 0), stop=(c == 7)),
                            reads=[("wb", sb_), ("h", c, b)], writes=[("ps", b2)])
                    s1 = cx.slot("tmpf", 4)
                    s2 = cx.slot("tmpf", 4)
                    so = cx.slot("stg16", 4)
                    sc.op("dve", lambda e, b1=b1, s1=s1, bs=bs: e.tensor_tensor(
                        out=t["tmpf"][s1][:], in0=ps[:, b1, :], in1=t["cos"][:, bs], op=ALU.mult),
                        reads=[("ps", b1), ("cos",)], writes=[("tmpf", s1)])
                    sc.op("dve", lambda e, b2=b2, s2=s2, bs=bs: e.tensor_tensor(
                        out=t["tmpf"][s2][:], in0=ps[:, b2, :], in1=t["sin"][:, bs], op=ALU.mult),
                        reads=[("ps", b2), ("sin",)], writes=[("tmpf", s2)])
                    sc.op("dve", lambda e, s1=s1, s2=s2, so=so: e.tensor_tensor(
                        out=t["stg16"][so][:], in0=t["tmpf"][s1][:], in1=t["tmpf"][s2][:], op=ALU.add),
                        reads=[("tmpf", s1), ("tmpf", s2)], writes=[("stg16", so)])
                    c0 = hf * HALF + b * 512
                    sc.op("sp", lambda e, dst=dst, row0=row0, c0=c0, so=so: e.dma_start(
                        out=dst[row0:row0 + 128, c0:c0 + 512], in_=t["stg16"][so][:]),
                        reads=[("stg16", so)], dma=f"stg16_{so}")
        for s in range(2):
            sa = cx.slot("wa", 3)
            sc.op("pool", lambda e, s=s, sa=sa: e.dma_start(
                out=t["wa"][sa][:], in_=winv[:, :, 1536 + 256 * s:1536 + 256 * s + 256]),
                writes=[("wa", sa)], dma=f"wa{sa}")
            for f2 in range(2):
                ch = 2 * s + f2
                for b in range(NB):
                    bs = slice(b * 512, (b + 1) * 512)
                    b1 = cx.bank()
                    for c in range(8):
                        sc.op("pe", lambda e, c=c, sa=sa, f2=f2, b1=b1, bs=bs: e.matmul(
                            ps[:, b1, :], lhsT=t["wa"][sa][:, c, 128 * f2:128 * f2 + 128],
                            rhs=t["h"][:, c, bs], start=(c == 0), stop=(c == 7)),
                            reads=[("wa", sa), ("h", c, b)], writes=[("ps", b1)])
                    so = cx.slot("stg32", 2)
                    sc.op("act", lambda e, b1=b1, so=so: e.activation(
                        out=t["stg32"][so][:], in_=ps[:, b1, :], func=AF.Copy),
                        reads=[("ps", b1)], writes=[("stg32", so)])
                    c0 = hf * HALF + b * 512
                    sc.op("sp", lambda e, ch=ch, c0=c0, so=so: e.dma_start(
                        out=u_out[ch * 128:ch * 128 + 128, c0:c0 + 512], in_=t["stg32"][so][:]),
                        reads=[("stg32", so)], dma=f"stg32_{so}")
        for s in range(2):
            sa = cx.slot("wa", 3)
            sc.op("pool", lambda e, s=s, sa=sa: e.dma_start(
                out=t["wa"][sa][:], in_=winv[:, :, 1024 + 256 * s:1024 + 256 * s + 256]),
                writes=[("wa", sa)], dma=f"wa{sa}")
            for tt in range(8):
                b = tt // 4
                b1 = cx.bank()
                for c in range(8):
                    sc.op("pe", lambda e, c=c, sa=sa, tt=tt, b1=b1: e.matmul(
                        ps[:, b1, 0:256], lhsT=t["h"][:, c, tt * 128:tt * 128 + 128],
                        rhs=t["wa"][sa][:, c, :], start=(c == 0), stop=(c == 7)),
                        reads=[("wa", sa), ("h", c, b)], writes=[("ps", b1)])
                so = cx.slot("stg16", 4)
                sc.op("act", lambda e, b1=b1, so=so: e.activation(
                    out=t["stg16"][so][:, 0:256], in_=ps[:, b1, 0:256], func=AF.Copy),
                    reads=[("ps", b1)], writes=[("stg16", so)])
                r0 = hf * HALF + tt * 128
                sc.op("sp", lambda e, r0=r0, s=s, so=so: e.dma_start(
                    out=v_out[r0:r0 + 128, 256 * s:256 * s + 256], in_=t["stg16"][so][:, 0:256]),
                    reads=[("stg16", so)], dma=f"stg16_{so}")
    return cx.finish()


def build_B():
    cx = Ctx()
    sc = cx.sc
    NQB = S // 512
    NKT = S // 128
    q_in = cx.din("q_in", [128, S], BF16)
    k_in = cx.din("k_in", [128, S // 2], BF16)
    v_in = cx.din("v_in", [128, NKT * 128], BF16)
    m_in = cx.din("m_in", [128, 4 * 512], BF16)
    o_out = cx.dout("o_out", [128, S])

    qT = cx.sb("qT", [128, S], BF16)
    kT = cx.sb("kT", [128, S // 2], BF16)
    pTs = [cx.sb(f"pTs{i}", [128, 512], BF16) for i in range(8)]
    vv = cx.sb("vv", [128, NKT * 128], BF16)
    mk = cx.sb("mk", [128, 4 * 512], BF16)
    mkv = mk[:].rearrange("p (j q) -> p j q", j=4)
    ones = cx.sb("ones", [128, 128], BF16)
    NP = 8
    pT = [cx.sb(f"pT{i}", [128, 2, 512], BF16) for i in range(NP)]
    rl = cx.sb("rl", [128, 512], F32)
    ostg = [cx.sb(f"ostg{i}", [128, 512], F32) for i in range(2)]
    ps = cx.ps("psum", [128, 8, 512], F32)

    sc.op("pool", lambda e: e.memset(ones[:], 1.0), writes=[("ones",)])
    sc.op("sp", lambda e: e.dma_start(out=mk[:], in_=m_in), writes=[("mk",)], dma="mk")
    NCH = 8
    cw = S // NCH
    for i in range(NCH):
        sc.op("sp", lambda e, i=i: e.dma_start(out=qT[:, i * cw:(i + 1) * cw], in_=q_in[:, i * cw:(i + 1) * cw]),
              writes=[("q", i)], dma=f"q{i}")
        sc.op("sp", lambda e, i=i: e.dma_start(out=kT[:, i * (cw // 2):(i + 1) * (cw // 2)],
                                               in_=k_in[:, i * (cw // 2):(i + 1) * (cw // 2)]),
              writes=[("k", i)], dma=f"k{i}")
        sc.op("sp", lambda e, i=i: e.dma_start(out=vv[:, i * cw:(i + 1) * cw], in_=v_in[:, i * cw:(i + 1) * cw]),
              writes=[("v", i)], dma=f"v{i}")

    groups = [(qb, g) for qb in range(NQB) for g in range(2 * (qb + 1))]
    LOOK = 2
    LLAG = 4
    ginfo = {}

    def emit_S(i):
        qb, g = groups[i]
        b0 = 2 * (i % 2)
        sl = i % NP
        ginfo[i] = (b0, sl)
        for t2 in range(2):
            sc.op("pe", lambda e, t2=t2: e.matmul(
                ps[0:128, b0 + t2, :], lhsT=kT[64 * t2:64 * t2 + 64, g * 128:(g + 1) * 128],
                rhs=qT[64 * t2:64 * t2 + 64, qb * 512:(qb + 1) * 512], start=True, stop=True),
                reads=[("k", g * 128 // (cw // 2)), ("q", qb * 512 // cw)], writes=[("ps", b0 + t2)])
        sc.op("act", lambda e: e.activation(out=pT[sl][:], in_=ps[:, b0:b0 + 2, :], func=AF.Exp, scale=0.125),
              reads=[("ps", b0), ("ps", b0 + 1)], writes=[("pT", sl)])
        j2 = g - 2 * qb
        if j2 >= 0:
            sc.op("dve", lambda e: e.tensor_tensor(out=pT[sl][:], in0=pT[sl][:],
                                                  in1=mkv[:, j2 * 2:j2 * 2 + 2, :], op=ALU.mult),
                  reads=[("pT", sl), ("mk",)], writes=[("pT", sl)])

    def emit_PV(i):
        qb, g = groups[i]
        b0, sl = ginfo[i]
        ob = 4 + (qb % 2)
        lb = 6 + (qb % 2)
        ng = 2 * (qb + 1)
        for t2 in range(2):
            kt = 2 * g + t2
            sc.op("pe", lambda e, kt=kt, t2=t2: e.matmul(
                ps[:, ob, :], lhsT=vv[:, kt * 128:(kt + 1) * 128], rhs=pT[sl][:, t2, :],
                start=(kt == 0), stop=(kt == 2 * ng - 1)),
                reads=[("pT", sl), ("v", kt * 128 // cw)], writes=[("ps", ob)])
        s2 = i % 8
        sc.op("dve", lambda e: e.tensor_tensor(out=pTs[s2][:], in0=pT[sl][:, 0, :], in1=pT[sl][:, 1, :], op=ALU.add),
              reads=[("pT", sl)], writes=[("pTs", s2)])

    def emit_L(i):
        qb, g = groups[i]
        ob = 4 + (qb % 2)
        lb = 6 + (qb % 2)
        ng = 2 * (qb + 1)
        s2 = i % 8
        sc.op("pe", lambda e: e.matmul(
            ps[:, lb, :], lhsT=ones[:], rhs=pTs[s2][:], start=(g == 0), stop=(g == ng - 1)),
            reads=[("pTs", s2), ("ones",)], writes=[("ps", lb)])
        if g == ng - 1:
            so = qb % 2
            sc.op("dve", lambda e: e.reciprocal(out=rl[:], in_=ps[:, lb, :]),
                  reads=[("ps", lb)], writes=[("rl",)])
            sc.op("dve", lambda e: e.tensor_tensor(out=ostg[so][:], in0=ps[:, ob, :], in1=rl[:], op=ALU.mult),
                  reads=[("ps", ob), ("rl",)], writes=[("ostg", so)])
            sc.op("sp", lambda e: e.dma_start(out=o_out[:, qb * 512:(qb + 1) * 512], in_=ostg[so][:]),
                  reads=[("ostg", so)], dma=f"ostg{so}")

    n = len(groups)
    for i in range(n + LOOK + LLAG):
        if i < n:
            emit_S(i)
        if 0 <= i - LOOK < n:
            emit_PV(i - LOOK)
        if 0 <= i - LOOK - LLAG < n:
            emit_L(i - LOOK - LLAG)
    return cx.finish()


def build_C(final=False):
    cx = Ctx()
    sc = cx.sc
    x_in = cx.din("x_in", [D, T])
    o_in = cx.din("o_in", [8 * 128, T])
    u_in = cx.din("u_in", [512, 16 + T])
    icnt = cx.din("icnt", [128, 4 * 16])
    lam4 = cx.din("lam4", [128, 4 * 64])
    lconst = cx.din("lconst", [128, 2])
    sgain = cx.din("sgain", [128, 1])
    pw = cx.din("pw", [4 * 128, 128])
    pscale = cx.din("pscale", [128, 4])
    wout = cx.din("wout", [D, D])
    g3 = cx.din("g3", [128, 8])
    gf = cx.din("gf", [128, 8])
    epsd = cx.din("epsd", [128, 1])
    wg = cx.din("wg", [D, DFF])
    wu = cx.din("wu", [D, DFF])
    wd = cx.din("wd", [DFF, D])
    x_out = cx.dout("x_out", [D, T])
    y_out = cx.dout("y_out", [D, T]) if final else None

    t = alloc_common(cx)
    ps = t["psum"]
    for nm, ap, n in (("g3", g3, 8), ("gf", gf, 8), ("eps", epsd, 1), ("lam4", lam4, 256),
                      ("lconst", lconst, 2), ("sgain", sgain, 1), ("pscale", pscale, 4),
                      ("icnt", icnt, 64)):
        load_vec(cx, t, nm, ap, n)
    t["lt"] = cx.sb("lt", [128, 128], F32)
    t["ls"] = cx.sb("ls", [128, 8], F32)
    sc.op("dve", lambda e: e.tensor_tensor(out=t["lt"][:, 0:64], in0=t["lam4"][:, 0:64],
                                           in1=t["lam4"][:, 64:128], op=ALU.mult),
          reads=[("lam4",)], writes=[("lt", 0)])
    sc.op("dve", lambda e: e.tensor_tensor(out=t["lt"][:, 64:128], in0=t["lam4"][:, 128:192],
                                           in1=t["lam4"][:, 192:256], op=ALU.mult),
          reads=[("lam4",)], writes=[("lt", 1)])
    sc.op("dve", lambda e: e.tensor_reduce(out=t["ls"][:, 0:1], in_=t["lt"][:, 0:64],
                                           axis=mybir.AxisListType.X, op=ALU.add),
          reads=[("lt", 0)], writes=[("ls", 0)])
    sc.op("dve", lambda e: e.tensor_reduce(out=t["ls"][:, 1:2], in_=t["lt"][:, 64:128],
                                           axis=mybir.AxisListType.X, op=ALU.add),
          reads=[("lt", 1)], writes=[("ls", 1)])
    sc.op("act", lambda e: e.activation(out=t["ls"][:, 2:4], in_=t["ls"][:, 0:2], func=AF.Exp),
          reads=[("ls", 0), ("ls", 1)], writes=[("ls", 2)])
    sc.op("dve", lambda e: e.tensor_tensor(out=t["ls"][:, 4:5], in0=t["ls"][:, 3:4], in1=t["ls"][:, 2:3],
                                           op=ALU.subtract),
          reads=[("ls", 2)], writes=[("ls", 4)])
    sc.op("dve", lambda e: e.tensor_tensor(out=t["ls"][:, 5:6], in0=t["ls"][:, 4:5], in1=t["lconst"][:, 0:1],
                                           op=ALU.subtract),
          reads=[("ls", 4), ("lconst",)], writes=[("neglam",)])
    sc.op("dve", lambda e: e.tensor_tensor(out=t["ls"][:, 6:7], in0=t["sgain"][:, 0:1], in1=t["lconst"][:, 1:2],
                                           op=ALU.mult),
          reads=[("sgain",), ("lconst",)], writes=[("sg",)])

    t["o1"] = [cx.sb(f"o1_{i}", [128, 512], F32) for i in range(3)]
    t["o2"] = [cx.sb(f"o2_{i}", [128, 512], F32) for i in range(3)]
    t["od"] = [cx.sb(f"od_{i}", [128, 512], F32) for i in range(3)]
    t["uu"] = cx.sb("uu", [128, 16 + HALF], F32)
    t["ua"] = cx.sb("ua", [128, 16 + HALF], F32)
    t["ub"] = cx.sb("ub", [128, 16 + HALF], F32)
    t["dif"] = cx.sb("dif", [128, HALF], BF16)
    t["pwt"] = cx.sb("pwt", [128, 4, 128], BF16)
    sc.op("pool", lambda e: e.dma_start(out=t["pwt"][:], in_=pw.rearrange("(g c) e -> c g e", c=128)),
          writes=[("pwt",)], dma="pwt")
    woutv = wout.rearrange("(c p) f -> p c f", p=128)
    WINS = (2, 4, 8, 16)

    for hf in range(2):
        load_x(cx, t, x_in, hf)
        items = [(hd, b) for hd in range(4) for b in range(NB)]
        st = {}

        def sub1(i):
            hd, b = items[i]
            c0 = hf * HALF + b * 512
            s1 = cx.slot("o1", 3)
            s2 = cx.slot("o2", 3)
            sd_ = cx.slot("od", 3)
            sc.op("sp", lambda e: e.dma_start(
                out=t["o1"][s1][:], in_=o_in[(2 * hd) * 128:(2 * hd) * 128 + 128, c0:c0 + 512]),
                writes=[("o1", s1)], dma=f"o1_{s1}")
            sc.op("sp", lambda e: e.dma_start(
                out=t["o2"][s2][:], in_=o_in[(2 * hd + 1) * 128:(2 * hd + 1) * 128 + 128, c0:c0 + 512]),
                writes=[("o2", s2)], dma=f"o2_{s2}")
            sc.op("dve", lambda e: e.scalar_tensor_tensor(
                out=t["od"][sd_][:], in0=t["o2"][s2][:], scalar=t["ls"][:, 5:6], in1=t["o1"][s1][:],
                op0=ALU.mult, op1=ALU.add),
                reads=[("o1", s1), ("o2", s2), ("neglam",)], writes=[("od", sd_)])
            sq = cx.slot("sq", 4)
            sc.op("act", lambda e: e.activation(
                out=t["sq"][sq][:], in_=t["od"][sd_][:], func=AF.Square),
                reads=[("od", sd_)], writes=[("sq", sq)])
            bk = cx.bank()
            sc.op("pe", lambda e: e.matmul(
                ps[:, bk, :], lhsT=t["ones"][:], rhs=t["sq"][sq][:], start=True, stop=True),
                reads=[("sq", sq), ("ones",)], writes=[("ps", bk)])
            st[i] = (sd_, bk)

        def sub2(i):
            hd, b = items[i]
            bs = slice(b * 512, (b + 1) * 512)
            sd_, bk = st[i]
            ss = cx.slot("std", 2)
            rs = cx.slot("rstd", 2)
            sc.op("act", lambda e: e.activation(
                out=t["std"][ss][:], in_=ps[:, bk, :], func=AF.Ln, bias=t["eps"][:, 0:1], scale=1.0 / 128),
                reads=[("ps", bk), ("eps",)], writes=[("std", ss)])
            sc.op("act", lambda e: e.activation(
                out=t["rstd"][rs][:], in_=t["std"][ss][:], func=AF.Exp, scale=-0.5),
                reads=[("std", ss)], writes=[("rstd", rs)])
            sc.op("dve", lambda e: e.scalar_tensor_tensor(
                out=t["h"][:, hd, bs], in0=t["od"][sd_][:], scalar=t["ls"][:, 6:7],
                in1=t["rstd"][rs][:], op0=ALU.mult, op1=ALU.mult),
                reads=[("od", sd_), ("sg",), ("rstd", rs)], writes=[("h", hd, b)])

        for i in range(len(items) + 1):
            if i < len(items):
                sub1(i)
            if i >= 1:
                sub2(i - 1)
        for g in range(4):
            w = WINS[g]
            c0 = hf * HALF
            sc.op("sp", lambda e, g=g, c0=c0: e.dma_start(
                out=t["uu"][:], in_=u_in[g * 128:g * 128 + 128, c0:c0 + 16 + HALF]),
                writes=[("uu",)], dma="uu")
            src = "uu"
            sh = 1
            flip = 0
            while sh < w:
                dst = "ua" if flip == 0 else "ub"
                sc.op("dve", lambda e, src=src, dst=dst, sh=sh: e.tensor_tensor(
                    out=t[dst][:, sh:16 + HALF], in0=t[src][:, sh:16 + HALF],
                    in1=t[src][:, 0:16 + HALF - sh], op=ALU.add),
                    reads=[(src,)], writes=[(dst,)])
                src = dst
                flip ^= 1
                sh *= 2
            oth = "ua" if src == "ub" else "ub"
            sc.op("dve", lambda e, src=src, w=w: e.scalar_tensor_tensor(
                out=t["dif"][:, 0:HALF], in0=t[src][:, 16:16 + HALF], scalar=1.0 / w,
                in1=t["uu"][:, 16:16 + HALF], op0=ALU.mult, op1=ALU.subtract),
                reads=[(src,), ("uu",)], writes=[("dif",)])
            if hf == 0:
                sc.op("dve", lambda e, src=src, oth=oth, g=g: e.tensor_tensor(
                    out=t[oth][:, 16:32], in0=t[src][:, 16:32],
                    in1=t["icnt"][:, g * 16:g * 16 + 16], op=ALU.mult),
                    reads=[(src,), ("icnt",)], writes=[(oth,)])
                sc.op("dve", lambda e, oth=oth: e.tensor_tensor(
                    out=t["dif"][:, 0:16], in0=t[oth][:, 16:32], in1=t["uu"][:, 16:32], op=ALU.subtract),
                    reads=[(oth,), ("uu",), ("dif",)], writes=[("dif",)])
            for b in range(NB):
                bs = slice(b * 512, (b + 1) * 512)
                bk = cx.bank()
                sc.op("pe", lambda e, g=g, bk=bk, bs=bs: e.matmul(
                    ps[:, bk, :], lhsT=t["pwt"][:, g, :], rhs=t["dif"][:, bs], start=True, stop=True),
                    reads=[("pwt",), ("dif",)], writes=[("ps", bk)])
                sc.op("dve", lambda e, g=g, bk=bk, bs=bs: e.tensor_scalar(
                    out=t["h"][:, 4 + g, bs], in0=ps[:, bk, :], scalar1=t["pscale"][:, g:g + 1],
                    scalar2=None, op0=ALU.mult),
                    reads=[("ps", bk), ("pscale",)], writes=[("h", 4 + g, b)])
        for s in range(4):
            sa = cx.slot("wa", 3)
            sc.op("pool", lambda e, s=s, sa=sa: e.dma_start(
                out=t["wa"][sa][:], in_=woutv[:, :, 256 * s:256 * s + 256]),
                writes=[("wa", sa)], dma=f"wa{sa}")
            for f2 in range(2):
                j = 2 * s + f2
                for b in range(NB):
                    bs = slice(b * 512, (b + 1) * 512)
                    bk = cx.bank()
                    for c in range(8):
                        sc.op("pe", lambda e, c=c, sa=sa, f2=f2, bk=bk, bs=bs: e.matmul(
                            ps[:, bk, :], lhsT=t["wa"][sa][:, c, 128 * f2:128 * f2 + 128],
                            rhs=t["h"][:, c, bs], start=(c == 0), stop=(c == 7)),
                            reads=[("wa", sa), ("h", c, b)], writes=[("ps", bk)])
                    sc.op("dve", lambda e, j=j, bk=bk, bs=bs: e.tensor_tensor(
                        out=t["x"][:, j, bs], in0=ps[:, bk, :], in1=t["x"][:, j, bs], op=ALU.add),
                        reads=[("ps", bk), ("x", j, b)], writes=[("x", j, b)])
        ffn(cx, t, "g3", wg, wu, wd)
        store_x(cx, t, x_out, hf)
        if final:
            rmsnorm_final(cx, t, y_out, hf)
    return cx.finish()


def rmsnorm_final(cx, t, y_out, hf):
    sc = cx.sc
    ps = t["psum"]
    yv = y_out.rearrange("(c p) n -> p c n", p=128)
    for b in range(NB):
        bs = slice(b * 512, (b + 1) * 512)
        bk = cx.bank()
        for c in range(8):
            s = cx.slot("sq", 4)
            sc.op("act", lambda e, c=c, s=s, bs=bs: e.activation(
                out=t["sq"][s][:], in_=t["x"][:, c, bs], func=AF.Square),
                reads=[("x", c, b)], writes=[("sq", s)])
            sc.op("pe", lambda e, c=c, s=s, bk=bk: e.matmul(
                ps[:, bk, :], lhsT=t["ones"][:], rhs=t["sq"][s][:], start=(c == 0), stop=(c == 7)),
                reads=[("sq", s), ("ones",)], writes=[("ps", bk)])
        ss = cx.slot("std", 2)
        sc.op("act", lambda e, bk=bk, ss=ss: e.activation(
            out=t["std"][ss][:], in_=ps[:, bk, :], func=AF.Ln, bias=t["eps"][:, 0:1], scale=1.0 / D),
            reads=[("ps", bk), ("eps",)], writes=[("std", ss)])
        sc.op("act", lambda e, b=b, ss=ss: e.activation(
            out=t["rstd"][b][:], in_=t["std"][ss][:], func=AF.Exp, scale=-0.5),
              reads=[("std", ss)], writes=[("rstd", b)])
        for c in range(8):
            so = cx.slot("tmpf", 4)
            sc.op("dve", lambda e, c=c, b=b, bs=bs, so=so: e.scalar_tensor_tensor(
                out=t["tmpf"][so][:], in0=t["x"][:, c, bs], scalar=t["gf"][:, c:c + 1],
                in1=t["rstd"][b][:], op0=ALU.mult, op1=ALU.mult),
                reads=[("x", c, b), ("gf",), ("rstd", b)], writes=[("tmpf", so)])
            c0 = hf * HALF + b * 512
            sc.op("sp", lambda e, c=c, c0=c0, so=so: e.dma_start(
                out=yv[:, c, c0:c0 + 512], in_=t["tmpf"][so][:]),
                reads=[("tmpf", so)], dma=f"tmpf{so}")


_PROGS = {}


def _prog(name):
    if name not in _PROGS:
        _PROGS[name] = {"A": build_A, "B": build_B, "C": build_C,
                        "CF": lambda: build_C(final=True)}[name]()
    return _PROGS[name]


def _run(name, in_maps):
    res = run_bass_kernel_spmd(_prog(name), in_maps, core_ids=list(range(NCORES)))
    return res.results


def _vec8(g):
    return np.ascontiguousarray(np.asarray(g, np.float32).reshape(8, 128).T)


def _rope_tables():
    d = 64
    inv = (10000.0 ** (-np.arange(0, d, 2, dtype=np.float32) / d)).astype(np.float32)
    ang = np.arange(S, dtype=np.float32)[:, None] * inv[None, :]
    ang = np.concatenate([ang, ang], axis=-1)
    cos = np.cos(ang).astype(np.float32).T
    sin = np.sin(ang).astype(np.float32).T
    sin[:32] *= -1.0
    cos2 = np.concatenate([cos, cos], axis=0)
    sin2 = np.concatenate([sin, sin], axis=0)
    return cos2, sin2


def kernel(x, ffn1_norm, ffn1_w_gate, ffn1_w_up, ffn1_w_down, mix_norm, w_in,
           lambda_q1, lambda_k1, lambda_q2, lambda_k2, subln_gain, pool_w, pool_scale,
           w_out, ffn2_norm, ffn2_w_gate, ffn2_w_up, ffn2_w_down, final_norm):
    f = lambda a: np.ascontiguousarray(np.asarray(a, dtype=np.float32))
    x = f(x)
    xT = [np.ascontiguousarray(x[0, c * T:(c + 1) * T, :].T) for c in range(NCORES)]
    cos2, sin2 = _rope_tables()
    epsd = np.full((128, 1), EPS, np.float32)
    kp = np.arange(128)[:, None, None] + 128 * np.arange(4)[None, :, None]
    qq = np.arange(512)[None, None, :]
    mask = (kp <= qq).astype(np.float32).reshape(128, 4 * 512).astype(ml_dtypes.bfloat16)
    perm = (np.arange(1024).reshape(16, 2, 32)[:, ::-1, :]).reshape(-1)
    WINS = (2, 4, 8, 16)
    y = None
    for l in range(DEPTH):
        wl = f(w_in[l])
        winp = np.ascontiguousarray(wl[:, :1024][:, perm])
        ins = []
        for c in range(NCORES):
            ins.append(dict(x_in=xT[c], g1=_vec8(ffn1_norm[l]), g2=_vec8(mix_norm[l]), epsd=epsd,
                            wg=f(ffn1_w_gate[l]), wu=f(ffn1_w_up[l]), wd=f(ffn1_w_down[l]),
                            win=wl, winp=winp,
                            cosd=np.ascontiguousarray(cos2[:, c * T:(c + 1) * T]),
                            sind=np.ascontiguousarray(sin2[:, c * T:(c + 1) * T])))
        ra = _run("A", ins)
        xT = [ra[c]["x_out"] for c in range(NCORES)]
        qT = np.concatenate([ra[c]["q_out"] for c in range(NCORES)], axis=1)
        kT = np.concatenate([ra[c]["k_out"] for c in range(NCORES)], axis=1)
        v = np.concatenate([ra[c]["v_out"] for c in range(NCORES)], axis=0)
        uT = np.concatenate([ra[c]["u_out"] for c in range(NCORES)], axis=1)
        ins = []
        for c in range(NCORES):
            h, m = c // 2, c % 2
            vh = v[:, h * 128:(h + 1) * 128].reshape(S // 128, 128, 128).transpose(1, 0, 2)
            qc = qT[c * 64:(c + 1) * 64]
            kc = kT[c * 64:(c + 1) * 64].reshape(64, S // 256, 2, 128).transpose(2, 0, 1, 3)
            ins.append(dict(q_in=np.ascontiguousarray(np.concatenate([qc, qc], axis=0)),
                            k_in=np.ascontiguousarray(kc).reshape(128, S // 2),
                            v_in=np.ascontiguousarray(vh).reshape(128, -1),
                            m_in=mask))
        rb = _run("B", ins)
        oT = np.concatenate([rb[c]["o_out"] for c in range(NCORES)], axis=0)
        lam_init = 0.8 - 0.6 * math.exp(-0.3 * l)
        lam4 = np.concatenate([f(lambda_q1[l]), f(lambda_k1[l]), f(lambda_q2[l]), f(lambda_k2[l])])
        lam4 = np.ascontiguousarray(np.broadcast_to(lam4[None, :], (128, 256)))
        lconst = np.ascontiguousarray(np.broadcast_to(
            np.array([lam_init, 1.0 - lam_init], np.float32)[None, :], (128, 2)))
        upad = np.concatenate([np.zeros((512, 16), np.float32), uT], axis=1)
        ins = []
        for c in range(NCORES):
            icnt = np.zeros((128, 64), np.float32)
            for g in range(4):
                pos = c * T + np.arange(16)
                icnt[:, g * 16:(g + 1) * 16] = (1.0 / np.minimum(pos + 1, WINS[g]))[None, :]
            ins.append(dict(x_in=xT[c], o_in=np.ascontiguousarray(oT[:, c * T:(c + 1) * T]),
                            u_in=np.ascontiguousarray(upad[:, c * T:c * T + 16 + T]),
                            icnt=icnt, lam4=lam4, lconst=lconst,
                            sgain=f(subln_gain[l]).reshape(128, 1),
                            pw=f(pool_w[l]).reshape(512, 128),
                            pscale=np.ascontiguousarray(f(pool_scale[l]).reshape(4, 128).T),
                            wout=f(w_out[l]), g3=_vec8(ffn2_norm[l]), gf=_vec8(final_norm), epsd=epsd,
                            wg=f(ffn2_w_gate[l]), wu=f(ffn2_w_up[l]), wd=f(ffn2_w_down[l])))
        rc = _run("CF" if l == DEPTH - 1 else "C", ins)
        xT = [rc[c]["x_out"] for c in range(NCORES)]
        if l == DEPTH - 1:
            y = [rc[c]["y_out"] for c in range(NCORES)]
    out = np.concatenate([yc.T for yc in y], axis=0)[None]
    return np.ascontiguousarray(out.astype(np.float32))
```

```python
import math
from contextlib import ExitStack

import numpy as np
import ml_dtypes

import concourse.bass as bass
import concourse.mybir as mybir
from concourse.bass_utils import run_bass_kernel_spmd

F32 = mybir.dt.float32
BF16 = mybir.dt.bfloat16
AF = mybir.ActivationFunctionType
ALU = mybir.AluOpType

NCORES = 8
D = 1024
S = 16384
DEPTH = 4
DFF = 2816
NFC = DFF // 128
T = S // NCORES
HALF = 1024
NB = 2
EPS = 1e-6
SAME_ENGINE_SYNC = True


class Sched:
    ENGS = ("pe", "act", "dve", "pool", "sp")

    def __init__(self):
        self.ops = []
        self.last_writer = {}
        self.readers = {}
        self.dma_count = {}

    def op(self, eng, fn, reads=(), writes=(), dma=None):
        idx = len(self.ops)
        deps = set()
        for r in reads:
            if r in self.last_writer:
                deps.add(self.last_writer[r])
        for w in writes:
            if w in self.last_writer:
                deps.add(self.last_writer[w])
            for rd in self.readers.get(w, ()):
                deps.add(rd)
        best = {}
        for d in deps:
            od = self.ops[d]
            k = ("dma", od["dma"]) if od["dma"] is not None else ("eng", od["eng"])
            if k not in best or d > best[k]:
                best[k] = d
        deps = set(best.values())
        o = dict(eng=eng, fn=fn, deps=deps, dma=dma, signal=False, sig_idx=None,
                 dma_val=None)
        if dma is not None:
            self.dma_count[dma] = self.dma_count.get(dma, 0) + 1
            o["dma_val"] = 16 * self.dma_count[dma]
        self.ops.append(o)
        for w in writes:
            self.last_writer[w] = idx
            self.readers[w] = []
        for r in reads:
            self.readers.setdefault(r, []).append(idx)
        return idx

    def finalize(self):
        ops = self.ops
        for o in ops:
            for d in o["deps"]:
                od = ops[d]
                if od["dma"] is None:
                    if od["eng"] != o["eng"] or (SAME_ENGINE_SYNC and od["eng"] != "pe"
                                                 and o["dma"] is None):
                        od["signal"] = True
                    elif o["dma"] is not None and od["eng"] == o["eng"]:
                        od["signal"] = True
        cnt = {e: 0 for e in self.ENGS}
        for o in ops:
            if o["signal"]:
                cnt[o["eng"]] += 1
                o["sig_idx"] = cnt[o["eng"]]
        waited = {e: {} for e in self.ENGS}
        for o in ops:
            w = {}
            for d in o["deps"]:
                od = ops[d]
                if od["dma"] is not None:
                    key = ("dma", od["dma"])
                    val = od["dma_val"]
                else:
                    if not od["signal"]:
                        continue
                    key = ("tl", od["eng"])
                    val = od["sig_idx"]
                w[key] = max(w.get(key, 0), val)
            wl = []
            for key, val in w.items():
                if waited[o["eng"]].get(key, 0) >= val:
                    continue
                waited[o["eng"]][key] = val
                wl.append((key, val))
            o["waits"] = wl

    def emit(self, nc, stack):
        self.finalize()
        sems = {}
        for e in self.ENGS:
            sems[("tl", e)] = stack.enter_context(nc.semaphore("tl_" + e))
        for k in self.dma_count:
            sems[("dma", k)] = stack.enter_context(nc.semaphore("dma_" + str(k)))
        block = stack.enter_context(nc.Block())
        ops = self.ops
        dma_final = dict(self.dma_count)

        def run(eng_name, e, final=False):
            for o in ops:
                if o["eng"] != eng_name:
                    continue
                for key, val in o["waits"]:
                    e.wait_ge(sems[key], val)
                inst = o["fn"](e)
                if o["dma"] is not None:
                    inst.then_inc(sems[("dma", o["dma"])], 16)
                elif o["signal"]:
                    inst.then_inc(sems[("tl", eng_name)], 1)
            if final:
                for k, n in dma_final.items():
                    e.wait_ge(sems[("dma", k)], 16 * n)

        @block.tensor
        def _(e):
            run("pe", e)

        @block.scalar
        def _(e):
            run("act", e)

        @block.vector
        def _(e):
            run("dve", e)

        @block.gpsimd
        def _(e):
            run("pool", e)

        @block.sync
        def _(e):
            run("sp", e, final=True)


class Ctx:
    def __init__(self):
        self.nc = bass.Bass("TRN2", target_bir_lowering=False)
        self.sc = Sched()
        self.stack = ExitStack()
        self.bank_ctr = 0
        self.hook_mode = False
        self.hook_ctr = 0
        self.rot = {}

    def din(self, name, shape, dt=F32):
        return self.nc.dram_tensor(name, list(shape), dt, kind="ExternalInput").ap()

    def dout(self, name, shape, dt=F32):
        return self.nc.dram_tensor(name, list(shape), dt, kind="ExternalOutput").ap()

    def sb(self, name, shape, dt):
        return self.stack.enter_context(self.nc.sbuf_tensor("s_" + name, list(shape), dt))

    def ps(self, name, shape, dt=F32):
        return self.stack.enter_context(self.nc.psum_tensor("p_" + name, list(shape), dt))

    def bank(self):
        if self.hook_mode:
            b = 4 + self.hook_ctr % 4
            self.hook_ctr += 1
            return b
        b = self.bank_ctr % 8
        self.bank_ctr += 1
        return b

    def slot(self, name, n):
        v = self.rot.get(name, 0)
        self.rot[name] = v + 1
        return v % n

    def finish(self):
        self.sc.emit(self.nc, self.stack)
        self.stack.close()
        return self.nc


def alloc_common(cx):
    t = {}
    t["x"] = cx.sb("x", [128, 8, HALF], F32)
    t["h"] = cx.sb("h", [128, 8, HALF], BF16)
    t["act"] = cx.sb("act", [128, NFC, HALF], BF16)
    t["wa"] = [cx.sb(f"wa{i}", [128, 8, 256], BF16) for i in range(3)]
    t["wb"] = [cx.sb(f"wb{i}", [128, 8, 256], BF16) for i in range(3)]
    t["wd"] = [cx.sb(f"wd{i}", [128, 4, 512], BF16) for i in range(3)]
    t["sq"] = [cx.sb(f"sq{i}", [128, 512], BF16) for i in range(4)]
    t["std"] = [cx.sb(f"std{i}", [128, 512], F32) for i in range(2)]
    t["rstd"] = [cx.sb(f"rstd{i}", [128, 512], F32) for i in range(2)]
    t["tmpf"] = [cx.sb(f"tmpf{i}", [128, 512], F32) for i in range(4)]
    t["ones"] = cx.sb("ones", [128, 128], BF16)
    t["psum"] = cx.ps("psum", [128, 8, 512], F32)
    cx.sc.op("pool", lambda e: e.memset(t["ones"][:], 1.0), writes=[("ones",)])
    return t


def load_vec(cx, t, name, dram_ap, ncol):
    t[name] = cx.sb(name, [128, ncol], F32)
    cx.sc.op("sp", lambda e: e.dma_start(out=t[name][:], in_=dram_ap),
             writes=[(name,)], dma=name)


def rmsnorm_to_h(cx, t, gname, nchunk=8, src="x", dst="h", scale_d=D):
    sc = cx.sc
    ps = t["psum"]
    for b in range(NB):
        bs = slice(b * 512, (b + 1) * 512)
        bk = cx.bank()
        for c in range(nchunk):
            s = cx.slot("sq", 4)
            sc.op("act", lambda e, c=c, s=s, bs=bs: e.activation(
                out=t["sq"][s][:], in_=t[src][:, c, bs], func=AF.Square),
                reads=[(src, c, b)], writes=[("sq", s)])
            sc.op("pe", lambda e, c=c, s=s, bk=bk: e.matmul(
                ps[:, bk, :], lhsT=t["ones"][:], rhs=t["sq"][s][:],
                start=(c == 0), stop=(c == nchunk - 1)),
                reads=[("sq", s), ("ones",)], writes=[("ps", bk)])
        ss = cx.slot("std", 2)
        sc.op("act", lambda e, bk=bk, ss=ss: e.activation(
            out=t["std"][ss][:], in_=ps[:, bk, :], func=AF.Ln, bias=t["eps"][:, 0:1],
            scale=1.0 / scale_d),
            reads=[("ps", bk), ("eps",)], writes=[("std", ss)])
        sc.op("act", lambda e, b=b, ss=ss: e.activation(
            out=t["rstd"][b][:], in_=t["std"][ss][:], func=AF.Exp, scale=-0.5),
              reads=[("std", ss)], writes=[("rstd", b)])
        for c in range(nchunk):
            sc.op("dve", lambda e, c=c, b=b, bs=bs: e.scalar_tensor_tensor(
                out=t[dst][:, c, bs], in0=t[src][:, c, bs], scalar=t[gname][:, c:c + 1],
                in1=t["rstd"][b][:], op0=ALU.mult, op1=ALU.mult),
                reads=[(src, c, b), (gname,), ("rstd", b)], writes=[(dst, c, b)])


def ffn(cx, t, gname, wg, wu, wd, hooks=None):
    sc = cx.sc
    ps = t["psum"]
    rmsnorm_to_h(cx, t, gname)
    wgv = wg.rearrange("(c p) f -> p c f", p=128)
    wuv = wu.rearrange("(c p) f -> p c f", p=128)
    nslab = NFC // 2
    for s in range(nslab):
        sa = cx.slot("wa", 3)
        sb_ = cx.slot("wb", 3)
        sc.op("pool", lambda e, s=s, sa=sa: e.dma_start(
            out=t["wa"][sa][:], in_=wgv[:, :, 256 * s:256 * s + 256]),
            writes=[("wa", sa)], dma=f"wa{sa}")
        sc.op("pool", lambda e, s=s, sb_=sb_: e.dma_start(
            out=t["wb"][sb_][:], in_=wuv[:, :, 256 * s:256 * s + 256]),
            writes=[("wb", sb_)], dma=f"wb{sb_}")
        for f2 in range(2):
            fc = 2 * s + f2
            for b in range(NB):
                bs = slice(b * 512, (b + 1) * 512)
                bg = cx.bank()
                bu = cx.bank()
                for c in range(8):
                    sc.op("pe", lambda e, c=c, sa=sa, f2=f2, bg=bg, bs=bs: e.matmul(
                        ps[:, bg, :], lhsT=t["wa"][sa][:, c, 128 * f2:128 * f2 + 128],
                        rhs=t["h"][:, c, bs], start=(c == 0), stop=(c == 7)),
                        reads=[("wa", sa), ("h", c, b)], writes=[("ps", bg)])
                for c in range(8):
                    sc.op("pe", lambda e, c=c, sb_=sb_, f2=f2, bu=bu, bs=bs: e.matmul(
                        ps[:, bu, :], lhsT=t["wb"][sb_][:, c, 128 * f2:128 * f2 + 128],
                        rhs=t["h"][:, c, bs], start=(c == 0), stop=(c == 7)),
                        reads=[("wb", sb_), ("h", c, b)], writes=[("ps", bu)])
                ts = cx.slot("tmpf", 4)
                sc.op("act", lambda e, bg=bg, ts=ts: e.activation(
                    out=t["tmpf"][ts][:], in_=ps[:, bg, :], func=AF.Silu),
                    reads=[("ps", bg)], writes=[("tmpf", ts)])
                sc.op("dve", lambda e, bu=bu, ts=ts, fc=fc, bs=bs: e.tensor_tensor(
                    out=t["act"][:, fc, bs], in0=ps[:, bu, :], in1=t["tmpf"][ts][:],
                    op=ALU.mult),
                    reads=[("ps", bu), ("tmpf", ts)], writes=[("act", fc, b)])
    wdv = wd.rearrange("(s p) d -> p s d", p=128)
    for p_ in range(4):
        if hooks is not None:
            banks = [[0, 1], [2, 3]]
            cx.hook_mode = True
        else:
            banks = [[cx.bank() for b in range(NB)] for jj in range(2)]
        nsl = (NFC + 3) // 4
        for s in range(nsl):
            nch = min(4, NFC - 4 * s)
            sd = cx.slot("wd", 3)
            sc.op("pool", lambda e, s=s, sd=sd, nch=nch, p_=p_: e.dma_start(
                out=t["wd"][sd][:, 0:nch, 0:256],
                in_=wdv[:, 4 * s:4 * s + nch, 256 * p_:256 * p_ + 256]),
                writes=[("wd", sd)], dma=f"wd{sd}")
            for f4 in range(nch):
                fc = 4 * s + f4
                for jj in range(2):
                    for b in range(NB):
                        bs = slice(b * 512, (b + 1) * 512)
                        bk = banks[jj][b]
                        sc.op("pe", lambda e, sd=sd, f4=f4, jj=jj, bk=bk, fc=fc, bs=bs: e.matmul(
                            ps[:, bk, :], lhsT=t["wd"][sd][:, f4, 128 * jj:128 * jj + 128],
                            rhs=t["act"][:, fc, bs], start=(fc == 0), stop=(fc == NFC - 1)),
                            reads=[("wd", sd), ("act", fc, b)], writes=[("ps", bk)])
            if hooks:
                hooks.pop(0)()
        for jj in range(2):
            j = 2 * p_ + jj
            for b in range(NB):
                bs = slice(b * 512, (b + 1) * 512)
                bk = banks[jj][b]
                sc.op("dve", lambda e, j=j, bk=bk, bs=bs: e.scalar_tensor_tensor(
                    out=t["x"][:, j, bs], in0=ps[:, bk, :], scalar=0.5,
                    in1=t["x"][:, j, bs], op0=ALU.mult, op1=ALU.add),
                    reads=[("ps", bk), ("x", j, b)], writes=[("x", j, b)])


def load_x(cx, t, x_dram, hf, eng="sp"):
    xv = x_dram.rearrange("(c p) n -> p c n", p=128)
    for b in range(NB):
        c0 = hf * HALF + b * 512
        cx.sc.op(eng, lambda e, b=b, c0=c0: e.dma_start(out=t["x"][:, :, b * 512:(b + 1) * 512],
                                                     in_=xv[:, :, c0:c0 + 512]),
                 writes=[("x", c, b) for c in range(8)], dma=f"xin{b}")


def store_x(cx, t, x_dram, hf):
    xv = x_dram.rearrange("(c p) n -> p c n", p=128)
    cx.sc.op("sp", lambda e: e.dma_start(out=xv[:, :, hf * HALF:(hf + 1) * HALF], in_=t["x"][:]),
             reads=[("x", c, b) for c in range(8) for b in range(NB)], dma="xout")


def build_A():
    cx = Ctx()
    sc = cx.sc
    x_in = cx.din("x_in", [D, T])
    g1 = cx.din("g1", [128, 8])
    g2 = cx.din("g2", [128, 8])
    epsd = cx.din("epsd", [128, 1])
    wg = cx.din("wg", [D, DFF])
    wu = cx.din("wu", [D, DFF])
    wd = cx.din("wd", [DFF, D])
    win = cx.din("win", [D, 2048])
    winp = cx.din("winp", [D, 1024])
    cosd = cx.din("cosd", [128, T])
    sind = cx.din("sind", [128, T])
    x_out = cx.dout("x_out", [D, T])
    q_out = cx.dout("q_out", [512, T], BF16)
    k_out = cx.dout("k_out", [512, T], BF16)
    v_out = cx.dout("v_out", [T, 512], BF16)
    u_out = cx.dout("u_out", [512, T])

    t = alloc_common(cx)
    load_vec(cx, t, "g1", g1, 8)
    load_vec(cx, t, "g2", g2, 8)
    load_vec(cx, t, "eps", epsd, 1)
    t["cos"] = cx.sb("cos", [128, HALF], F32)
    t["sin"] = cx.sb("sin", [128, HALF], F32)
    t["stg16"] = [cx.sb(f"stg16_{i}", [128, 512], BF16) for i in range(4)]
    t["stg32"] = [cx.sb(f"stg32_{i}", [128, 512], F32) for i in range(2)]
    ps = t["psum"]
    winv = win.rearrange("(c p) f -> p c f", p=128)
    winpv = winp.rearrange("(c p) f -> p c f", p=128)

    for hf in range(2):
        if hf == 0:
            load_x(cx, t, x_in, hf)
        ffn(cx, t, "g1", wg, wu, wd)
        store_x(cx, t, x_out, hf)
        rmsnorm_to_h(cx, t, "g2")
        if hf == 0:
            load_x(cx, t, x_in, 1, eng="act")
        sc.op("sp", lambda e, hf=hf: e.dma_start(out=t["cos"][:], in_=cosd[:, hf * HALF:(hf + 1) * HALF]),
              writes=[("cos",)], dma="cos")
        sc.op("sp", lambda e, hf=hf: e.dma_start(out=t["sin"][:], in_=sind[:, hf * HALF:(hf + 1) * HALF]),
              writes=[("sin",)], dma="sin")
        for s in range(4):
            sa = cx.slot("wa", 3)
            sb_ = cx.slot("wb", 3)
            sc.op("pool", lambda e, s=s, sa=sa: e.dma_start(
                out=t["wa"][sa][:], in_=winv[:, :, 256 * s:256 * s + 256]),
                writes=[("wa", sa)], dma=f"wa{sa}")
            sc.op("pool", lambda e, s=s, sb_=sb_: e.dma_start(
                out=t["wb"][sb_][:], in_=winpv[:, :, 256 * s:256 * s + 256]),
                writes=[("wb", sb_)], dma=f"wb{sb_}")
            for f2 in range(2):
                ch = 2 * s + f2
                dst = q_out if ch < 4 else k_out
                row0 = (ch % 4) * 128
                for b in range(NB):
                    bs = slice(b * 512, (b + 1) * 512)
                    b1 = cx.bank()
                    b2 = cx.bank()
                    for c in range(8):
                        sc.op("pe", lambda e, c=c, sa=sa, f2=f2, b1=b1, bs=bs: e.matmul(
                            ps[:, b1, :], lhsT=t["wa"][sa][:, c, 128 * f2:128 * f2 + 128],
                            rhs=t["h"][:, c, bs], start=(c == 0), stop=(c == 7)),
                            reads=[("wa", sa), ("h", c, b)], writes=[("ps", b1)])
                    for c in range(8):
                        sc.op("pe", lambda e, c=c, sb_=sb_, f2=f2, b2=b2, bs=bs: e.matmul(
                            ps[:, b2, :], lhsT=t["wb"][sb_][:, c, 128 * f2:128 * f2 + 128],
                            rhs=t["h"][:, c, bs], start=(c == 0), stop=(c == 7)),
                            reads=[("wb", sb_), ("h", c, b)], writes=[("ps", b2)])
                    s1 = cx.slot("tmpf", 4)
                    s2 = cx.slot("tmpf", 4)
                    so = cx.slot("stg16", 4)
                    sc.op("dve", lambda e, b1=b1, s1=s1, bs=bs: e.tensor_tensor(
                        out=t["tmpf"][s1][:], in0=ps[:, b1, :], in1=t["cos"][:, bs], op=ALU.mult),
                        reads=[("ps", b1), ("cos",)], writes=[("tmpf", s1)])
                    sc.op("dve", lambda e, b2=b2, s2=s2, bs=bs: e.tensor_tensor(
                        out=t["tmpf"][s2][:], in0=ps[:, b2, :], in1=t["sin"][:, bs], op=ALU.mult),
                        reads=[("ps", b2), ("sin",)], writes=[("tmpf", s2)])
                    sc.op("dve", lambda e, s1=s1, s2=s2, so=so: e.tensor_tensor(
                        out=t["stg16"][so][:], in0=t["tmpf"][s1][:], in1=t["tmpf"][s2][:], op=ALU.add),
                        reads=[("tmpf", s1), ("tmpf", s2)], writes=[("stg16", so)])
                    c0 = hf * HALF + b * 512
                    sc.op("sp", lambda e, dst=dst, row0=row0, c0=c0, so=so: e.dma_start(
                        out=dst[row0:row0 + 128, c0:c0 + 512], in_=t["stg16"][so][:]),
                        reads=[("stg16", so)], dma=f"stg16_{so}")
        for s in range(2):
            sa = cx.slot("wa", 3)
            sc.op("pool", lambda e, s=s, sa=sa: e.dma_start(
                out=t["wa"][sa][:], in_=winv[:, :, 1536 + 256 * s:1536 + 256 * s + 256]),
                writes=[("wa", sa)], dma=f"wa{sa}")
            for f2 in range(2):
                ch = 2 * s + f2
                for b in range(NB):
                    bs = slice(b * 512, (b + 1) * 512)
                    b1 = cx.bank()
                    for c in range(8):
                        sc.op("pe", lambda e, c=c, sa=sa, f2=f2, b1=b1, bs=bs: e.matmul(
                            ps[:, b1, :], lhsT=t["wa"][sa][:, c, 128 * f2:128 * f2 + 128],
                            rhs=t["h"][:, c, bs], start=(c == 0), stop=(c == 7)),
                            reads=[("wa", sa), ("h", c, b)], writes=[("ps", b1)])
                    so = cx.slot("stg32", 2)
                    sc.op("act", lambda e, b1=b1, so=so: e.activation(
                        out=t["stg32"][so][:], in_=ps[:, b1, :], func=AF.Copy),
                        reads=[("ps", b1)], writes=[("stg32", so)])
                    c0 = hf * HALF + b * 512
                    sc.op("sp", lambda e, ch=ch, c0=c0, so=so: e.dma_start(
                        out=u_out[ch * 128:ch * 128 + 128, c0:c0 + 512], in_=t["stg32"][so][:]),
                        reads=[("stg32", so)], dma=f"stg32_{so}")
        for s in range(2):
            sa = cx.slot("wa", 3)
            sc.op("pool", lambda e, s=s, sa=sa: e.dma_start(
                out=t["wa"][sa][:], in_=winv[:, :, 1024 + 256 * s:1024 + 256 * s + 256]),
                writes=[("wa", sa)], dma=f"wa{sa}")
            for tt in range(8):
                b = tt // 4
                b1 = cx.bank()
                for c in range(8):
                    sc.op("pe", lambda e, c=c, sa=sa, tt=tt, b1=b1: e.matmul(
                        ps[:, b1, 0:256], lhsT=t["h"][:, c, tt * 128:tt * 128 + 128],
                        rhs=t["wa"][sa][:, c, :], start=(c == 0), stop=(c == 7)),
                        reads=[("wa", sa), ("h", c, b)], writes=[("ps", b1)])
                so = cx.slot("stg16", 4)
                sc.op("act", lambda e, b1=b1, so=so: e.activation(
                    out=t["stg16"][so][:, 0:256], in_=ps[:, b1, 0:256], func=AF.Copy),
                    reads=[("ps", b1)], writes=[("stg16", so)])
                r0 = hf * HALF + tt * 128
                sc.op("sp", lambda e, r0=r0, s=s, so=so: e.dma_start(
                    out=v_out[r0:r0 + 128, 256 * s:256 * s + 256], in_=t["stg16"][so][:, 0:256]),
                    reads=[("stg16", so)], dma=f"stg16_{so}")
    return cx.finish()


def build_B():
    cx = Ctx()
    sc = cx.sc
    NQB = S // 512
    NKT = S // 128
    q_in = cx.din("q_in", [128, S], BF16)
    k_in = cx.din("k_in", [128, S // 2], BF16)
    v_in = cx.din("v_in", [128, NKT * 128], BF16)
    m_in = cx.din("m_in", [128, 4 * 512], BF16)
    o_out = cx.dout("o_out", [128, S])

    qT = cx.sb("qT", [128, S], BF16)
    kT = cx.sb("kT", [128, S // 2], BF16)
    pTs = [cx.sb(f"pTs{i}", [128, 512], BF16) for i in range(8)]
    vv = cx.sb("vv", [128, NKT * 128], BF16)
    mk = cx.sb("mk", [128, 4 * 512], BF16)
    mkv = mk[:].rearrange("p (j q) -> p j q", j=4)
    ones = cx.sb("ones", [128, 128], BF16)
    NP = 8
    pT = [cx.sb(f"pT{i}", [128, 2, 512], BF16) for i in range(NP)]
    rl = cx.sb("rl", [128, 512], F32)
    ostg = [cx.sb(f"ostg{i}", [128, 512], F32) for i in range(2)]
    ps = cx.ps("psum", [128, 8, 512], F32)

    sc.op("pool", lambda e: e.memset(ones[:], 1.0), writes=[("ones",)])
    sc.op("sp", lambda e: e.dma_start(out=mk[:], in_=m_in), writes=[("mk",)], dma="mk")
    NCH = 8
    cw = S // NCH
    for i in range(NCH):
        sc.op("sp", lambda e, i=i: e.dma_start(out=qT[:, i * cw:(i + 1) * cw], in_=q_in[:, i * cw:(i + 1) * cw]),
              writes=[("q", i)], dma=f"q{i}")
        sc.op("sp", lambda e, i=i: e.dma_start(out=kT[:, i * (cw // 2):(i + 1) * (cw // 2)],
                                               in_=k_in[:, i * (cw // 2):(i + 1) * (cw // 2)]),
              writes=[("k", i)], dma=f"k{i}")
        sc.op("sp", lambda e, i=i: e.dma_start(out=vv[:, i * cw:(i + 1) * cw], in_=v_in[:, i * cw:(i + 1) * cw]),
              writes=[("v", i)], dma=f"v{i}")

    groups = [(qb, g) for qb in range(NQB) for g in range(2 * (qb + 1))]
    LOOK = 2
    LLAG = 4
    ginfo = {}

    def emit_S(i):
        qb, g = groups[i]
        b0 = 2 * (i % 2)
        sl = i % NP
        ginfo[i] = (b0, sl)
        for t2 in range(2):
            sc.op("pe", lambda e, t2=t2: e.matmul(
                ps[0:128, b0 + t2, :], lhsT=kT[64 * t2:64 * t2 + 64, g * 128:(g + 1) * 128],
                rhs=qT[64 * t2:64 * t2 + 64, qb * 512:(qb + 1) * 512], start=True, stop=True),
                reads=[("k", g * 128 // (cw // 2)), ("q", qb * 512 // cw)], writes=[("ps", b0 + t2)])
        sc.op("act", lambda e: e.activation(out=pT[sl][:], in_=ps[:, b0:b0 + 2, :], func=AF.Exp, scale=0.125),
              reads=[("ps", b0), ("ps", b0 + 1)], writes=[("pT", sl)])
        j2 = g - 2 * qb
        if j2 >= 0:
            sc.op("dve", lambda e: e.tensor_tensor(out=pT[sl][:], in0=pT[sl][:],
                                                  in1=mkv[:, j2 * 2:j2 * 2 + 2, :], op=ALU.mult),
                  reads=[("pT", sl), ("mk",)], writes=[("pT", sl)])

    def emit_PV(i):
        qb, g = groups[i]
        b0, sl = ginfo[i]
        ob = 4 + (qb % 2)
        lb = 6 + (qb % 2)
        ng = 2 * (qb + 1)
        for t2 in range(2):
            kt = 2 * g + t2
            sc.op("pe", lambda e, kt=kt, t2=t2: e.matmul(
                ps[:, ob, :], lhsT=vv[:, kt * 128:(kt + 1) * 128], rhs=pT[sl][:, t2, :],
                start=(kt == 0), stop=(kt == 2 * ng - 1)),
                reads=[("pT", sl), ("v", kt * 128 // cw)], writes=[("ps", ob)])
        s2 = i % 8
        sc.op("dve", lambda e: e.tensor_tensor(out=pTs[s2][:], in0=pT[sl][:, 0, :], in1=pT[sl][:, 1, :], op=ALU.add),
              reads=[("pT", sl)], writes=[("pTs", s2)])

    def emit_L(i):
        qb, g = groups[i]
        ob = 4 + (qb % 2)
        lb = 6 + (qb % 2)
        ng = 2 * (qb + 1)
        s2 = i % 8
        sc.op("pe", lambda e: e.matmul(
            ps[:, lb, :], lhsT=ones[:], rhs=pTs[s2][:], start=(g == 0), stop=(g == ng - 1)),
            reads=[("pTs", s2), ("ones",)], writes=[("ps", lb)])
        if g == ng - 1:
            so = qb % 2
            sc.op("dve", lambda e: e.reciprocal(out=rl[:], in_=ps[:, lb, :]),
                  reads=[("ps", lb)], writes=[("rl",)])
            sc.op("dve", lambda e: e.tensor_tensor(out=ostg[so][:], in0=ps[:, ob, :], in1=rl[:], op=ALU.mult),
                  reads=[("ps", ob), ("rl",)], writes=[("ostg", so)])
            sc.op("sp", lambda e: e.dma_start(out=o_out[:, qb * 512:(qb + 1) * 512], in_=ostg[so][:]),
                  reads=[("ostg", so)], dma=f"ostg{so}")

    n = len(groups)
    for i in range(n + LOOK + LLAG):
        if i < n:
            emit_S(i)
        if 0 <= i - LOOK < n:
            emit_PV(i - LOOK)
        if 0 <= i - LOOK - LLAG < n:
            emit_L(i - LOOK - LLAG)
    return cx.finish()


def build_C(final=False):
    cx = Ctx()
    sc = cx.sc
    x_in = cx.din("x_in", [D, T])
    o_in = cx.din("o_in", [8 * 128, T])
    u_in = cx.din("u_in", [512, 16 + T])
    icnt = cx.din("icnt", [128, 4 * 16])
    lam4 = cx.din("lam4", [128, 4 * 64])
    lconst = cx.din("lconst", [128, 2])
    sgain = cx.din("sgain", [128, 1])
    pw = cx.din("pw", [4 * 128, 128])
    pscale = cx.din("pscale", [128, 4])
    wout = cx.din("wout", [D, D])
    g3 = cx.din("g3", [128, 8])
    gf = cx.din("gf", [128, 8])
    epsd = cx.din("epsd", [128, 1])
    wg = cx.din("wg", [D, DFF])
    wu = cx.din("wu", [D, DFF])
    wd = cx.din("wd", [DFF, D])
    x_out = cx.dout("x_out", [D, T])
    y_out = cx.dout("y_out", [D, T]) if final else None

    t = alloc_common(cx)
    ps = t["psum"]
    for nm, ap, n in (("g3", g3, 8), ("gf", gf, 8), ("eps", epsd, 1), ("lam4", lam4, 256),
                      ("lconst", lconst, 2), ("sgain", sgain, 1), ("pscale", pscale, 4),
                      ("icnt", icnt, 64)):
        load_vec(cx, t, nm, ap, n)
    t["lt"] = cx.sb("lt", [128, 128], F32)
    t["ls"] = cx.sb("ls", [128, 8], F32)
    sc.op("dve", lambda e: e.tensor_tensor(out=t["lt"][:, 0:64], in0=t["lam4"][:, 0:64],
                                           in1=t["lam4"][:, 64:128], op=ALU.mult),
          reads=[("lam4",)], writes=[("lt", 0)])
    sc.op("dve", lambda e: e.tensor_tensor(out=t["lt"][:, 64:128], in0=t["lam4"][:, 128:192],
                                           in1=t["lam4"][:, 192:256], op=ALU.mult),
          reads=[("lam4",)], writes=[("lt", 1)])
    sc.op("dve", lambda e: e.tensor_reduce(out=t["ls"][:, 0:1], in_=t["lt"][:, 0:64],
                                           axis=mybir.AxisListType.X, op=ALU.add),
          reads=[("lt", 0)], writes=[("ls", 0)])
    sc.op("dve", lambda e: e.tensor_reduce(out=t["ls"][:, 1:2], in_=t["lt"][:, 64:128],
                                           axis=mybir.AxisListType.X, op=ALU.add),
          reads=[("lt", 1)], writes=[("ls", 1)])
    sc.op("act", lambda e: e.activation(out=t["ls"][:, 2:4], in_=t["ls"][:, 0:2], func=AF.Exp),
          reads=[("ls", 0), ("ls", 1)], writes=[("ls", 2)])
    sc.op("dve", lambda e: e.tensor_tensor(out=t["ls"][:, 4:5], in0=t["ls"][:, 3:4], in1=t["ls"][:, 2:3],
                                           op=ALU.subtract),
          reads=[("ls", 2)], writes=[("ls", 4)])
    sc.op("dve", lambda e: e.tensor_tensor(out=t["ls"][:, 5:6], in0=t["ls"][:, 4:5], in1=t["lconst"][:, 0:1],
                                           op=ALU.subtract),
          reads=[("ls", 4), ("lconst",)], writes=[("neglam",)])
    sc.op("dve", lambda e: e.tensor_tensor(out=t["ls"][:, 6:7], in0=t["sgain"][:, 0:1], in1=t["lconst"][:, 1:2],
                                           op=ALU.mult),
          reads=[("sgain",), ("lconst",)], writes=[("sg",)])

    t["o1"] = [cx.sb(f"o1_{i}", [128, 512], F32) for i in range(3)]
    t["o2"] = [cx.sb(f"o2_{i}", [128, 512], F32) for i in range(3)]
    t["od"] = [cx.sb(f"od_{i}", [128, 512], F32) for i in range(3)]
    t["uu"] = cx.sb("uu", [128, 16 + HALF], F32)
    t["ua"] = cx.sb("ua", [128, 16 + HALF], F32)
    t["ub"] = cx.sb("ub", [128, 16 + HALF], F32)
    t["dif"] = cx.sb("dif", [128, HALF], BF16)
    t["pwt"] = cx.sb("pwt", [128, 4, 128], BF16)
    sc.op("pool", lambda e: e.dma_start(out=t["pwt"][:], in_=pw.rearrange("(g c) e -> c g e", c=128)),
          writes=[("pwt",)], dma="pwt")
    woutv = wout.rearrange("(c p) f -> p c f", p=128)
    WINS = (2, 4, 8, 16)

    t["mix"] = cx.sb("mix", [128, 8, HALF], BF16)

    def make_prologue(hf):
        items = [(hd, b) for hd in range(4) for b in range(NB)]
        st = {}

        def p1(i):
            hd, b = items[i]
            c0 = hf * HALF + b * 512
            s1 = cx.slot("o1", 3)
            s2 = cx.slot("o2", 3)
            sd_ = cx.slot("od", 3)
            sc.op("sp", lambda e: e.dma_start(
                out=t["o1"][s1][:], in_=o_in[(2 * hd) * 128:(2 * hd) * 128 + 128, c0:c0 + 512]),
                writes=[("o1", s1)], dma=f"o1_{s1}")
            sc.op("sp", lambda e: e.dma_start(
                out=t["o2"][s2][:], in_=o_in[(2 * hd + 1) * 128:(2 * hd + 1) * 128 + 128, c0:c0 + 512]),
                writes=[("o2", s2)], dma=f"o2_{s2}")
            sc.op("dve", lambda e: e.scalar_tensor_tensor(
                out=t["od"][sd_][:], in0=t["o2"][s2][:], scalar=t["ls"][:, 5:6], in1=t["o1"][s1][:],
                op0=ALU.mult, op1=ALU.add),
                reads=[("o1", s1), ("o2", s2), ("neglam",)], writes=[("od", sd_)])
            sq = cx.slot("sq", 4)
            sc.op("act", lambda e: e.activation(
                out=t["sq"][sq][:], in_=t["od"][sd_][:], func=AF.Square),
                reads=[("od", sd_)], writes=[("sq", sq)])
            st[i] = [sd_, sq, None]

        def p2(i):
            sd_, sq, _ = st[i]
            bk = cx.bank()
            sc.op("pe", lambda e: e.matmul(
                ps[:, bk, :], lhsT=t["ones"][:], rhs=t["sq"][sq][:], start=True, stop=True),
                reads=[("sq", sq), ("ones",)], writes=[("ps", bk)])
            st[i][2] = bk

        def p3(i):
            hd, b = items[i]
            bs = slice(b * 512, (b + 1) * 512)
            sd_, sq, bk = st[i]
            ss = cx.slot("std", 2)
            rs = cx.slot("rstd", 2)
            sc.op("act", lambda e: e.activation(
                out=t["std"][ss][:], in_=ps[:, bk, :], func=AF.Ln, bias=t["eps"][:, 0:1], scale=1.0 / 128),
                reads=[("ps", bk), ("eps",)], writes=[("std", ss)])
            sc.op("act", lambda e: e.activation(
                out=t["rstd"][rs][:], in_=t["std"][ss][:], func=AF.Exp, scale=-0.5),
                reads=[("std", ss)], writes=[("rstd", rs)])
            sc.op("dve", lambda e: e.scalar_tensor_tensor(
                out=t["mix"][:, hd, bs], in0=t["od"][sd_][:], scalar=t["ls"][:, 6:7],
                in1=t["rstd"][rs][:], op0=ALU.mult, op1=ALU.mult),
                reads=[("od", sd_), ("sg",), ("rstd", rs)], writes=[("mix", hd, b)])

        qst = {}

        def q1(g):
            w = WINS[g]
            c0 = hf * HALF
            sc.op("sp", lambda e: e.dma_start(
                out=t["uu"][:], in_=u_in[g * 128:g * 128 + 128, c0:c0 + 16 + HALF]),
                writes=[("uu",)], dma="uu")
            src_ = "uu"
            sh = 1
            flip = 0
            while sh < w:
                dst = "ua" if flip == 0 else "ub"
                sc.op("dve", lambda e, src_=src_, dst=dst, sh=sh: e.tensor_tensor(
                    out=t[dst][:, sh:16 + HALF], in0=t[src_][:, sh:16 + HALF],
                    in1=t[src_][:, 0:16 + HALF - sh], op=ALU.add),
                    reads=[(src_,)], writes=[(dst,)])
                src_ = dst
                flip ^= 1
                sh *= 2
            oth = "ua" if src_ == "ub" else "ub"
            sc.op("dve", lambda e: e.scalar_tensor_tensor(
                out=t["dif"][:, 0:HALF], in0=t[src_][:, 16:16 + HALF], scalar=1.0 / w,
                in1=t["uu"][:, 16:16 + HALF], op0=ALU.mult, op1=ALU.subtract),
                reads=[(src_,), ("uu",)], writes=[("dif",)])
            if hf == 0:
                sc.op("dve", lambda e: e.tensor_tensor(
                    out=t[oth][:, 16:32], in0=t[src_][:, 16:32],
                    in1=t["icnt"][:, g * 16:g * 16 + 16], op=ALU.mult),
                    reads=[(src_,), ("icnt",)], writes=[(oth,)])
                sc.op("dve", lambda e: e.tensor_tensor(
                    out=t["dif"][:, 0:16], in0=t[oth][:, 16:32], in1=t["uu"][:, 16:32], op=ALU.subtract),
                    reads=[(oth,), ("uu",), ("dif",)], writes=[("dif",)])

        def q2(g):
            bks = []
            for b in range(NB):
                bs = slice(b * 512, (b + 1) * 512)
                bk = cx.bank()
                bks.append(bk)
                sc.op("pe", lambda e, bk=bk, bs=bs: e.matmul(
                    ps[:, bk, :], lhsT=t["pwt"][:, g, :], rhs=t["dif"][:, bs], start=True, stop=True),
                    reads=[("pwt",), ("dif",)], writes=[("ps", bk)])
            qst[g] = bks

        def q3(g):
            for b in range(NB):
                bs = slice(b * 512, (b + 1) * 512)
                bk = qst[g][b]
                sc.op("dve", lambda e, bk=bk, bs=bs: e.tensor_scalar(
                    out=t["mix"][:, 4 + g, bs], in0=ps[:, bk, :], scalar1=t["pscale"][:, g:g + 1],
                    scalar2=None, op0=ALU.mult),
                    reads=[("ps", bk), ("pscale",)], writes=[("mix", 4 + g, b)])

        steps = []
        for k in range(len(items) + 2):
            def step(k=k):
                if k < len(items):
                    p1(k)
                if 0 <= k - 1 < len(items):
                    p2(k - 1)
                if 0 <= k - 2 < len(items):
                    p3(k - 2)
            steps.append(step)
        for k in range(4 + 2):
            def step(k=k):
                if 0 <= k - 1 < 4:
                    q2(k - 1)
                if k < 4:
                    q1(k)
                if 0 <= k - 2 < 4:
                    q3(k - 2)
            steps.append(step)
        return steps

    for stp in make_prologue(0):
        stp()
    for hf in range(2):
        load_x(cx, t, x_in, hf)
        for s in range(4):
            sa = cx.slot("wa", 3)
            sc.op("pool", lambda e, s=s, sa=sa: e.dma_start(
                out=t["wa"][sa][:], in_=woutv[:, :, 256 * s:256 * s + 256]),
                writes=[("wa", sa)], dma=f"wa{sa}")
            for f2 in range(2):
                j = 2 * s + f2
                for b in range(NB):
                    bs = slice(b * 512, (b + 1) * 512)
                    bk = cx.bank()
                    for c in range(8):
                        sc.op("pe", lambda e, c=c, sa=sa, f2=f2, bk=bk, bs=bs: e.matmul(
                            ps[:, bk, :], lhsT=t["wa"][sa][:, c, 128 * f2:128 * f2 + 128],
                            rhs=t["mix"][:, c, bs], start=(c == 0), stop=(c == 7)),
                            reads=[("wa", sa), ("mix", c, b)], writes=[("ps", bk)])
                    sc.op("dve", lambda e, j=j, bk=bk, bs=bs: e.tensor_tensor(
                        out=t["x"][:, j, bs], in0=ps[:, bk, :], in1=t["x"][:, j, bs], op=ALU.add),
                        reads=[("ps", bk), ("x", j, b)], writes=[("x", j, b)])
        hooks = make_prologue(1) if hf == 0 else None
        ffn(cx, t, "g3", wg, wu, wd, hooks=hooks)
        if hooks:
            for stp in hooks:
                stp()
        cx.hook_mode = False
        store_x(cx, t, x_out, hf)
        if final:
            rmsnorm_final(cx, t, y_out, hf)
    return cx.finish()


def rmsnorm_final(cx, t, y_out, hf):
    sc = cx.sc
    ps = t["psum"]
    yv = y_out.rearrange("(c p) n -> p c n", p=128)
    for b in range(NB):
        bs = slice(b * 512, (b + 1) * 512)
        bk = cx.bank()
        for c in range(8):
            s = cx.slot("sq", 4)
            sc.op("act", lambda e, c=c, s=s, bs=bs: e.activation(
                out=t["sq"][s][:], in_=t["x"][:, c, bs], func=AF.Square),
                reads=[("x", c, b)], writes=[("sq", s)])
            sc.op("pe", lambda e, c=c, s=s, bk=bk: e.matmul(
                ps[:, bk, :], lhsT=t["ones"][:], rhs=t["sq"][s][:], start=(c == 0), stop=(c == 7)),
                reads=[("sq", s), ("ones",)], writes=[("ps", bk)])
        ss = cx.slot("std", 2)
        sc.op("act", lambda e, bk=bk, ss=ss: e.activation(
            out=t["std"][ss][:], in_=ps[:, bk, :], func=AF.Ln, bias=t["eps"][:, 0:1], scale=1.0 / D),
            reads=[("ps", bk), ("eps",)], writes=[("std", ss)])
        sc.op("act", lambda e, b=b, ss=ss: e.activation(
            out=t["rstd"][b][:], in_=t["std"][ss][:], func=AF.Exp, scale=-0.5),
              reads=[("std", ss)], writes=[("rstd", b)])
        for c in range(8):
            so = cx.slot("tmpf", 4)
            sc.op("dve", lambda e, c=c, b=b, bs=bs, so=so: e.scalar_tensor_tensor(
                out=t["tmpf"][so][:], in0=t["x"][:, c, bs], scalar=t["gf"][:, c:c + 1],
                in1=t["rstd"][b][:], op0=ALU.mult, op1=ALU.mult),
                reads=[("x", c, b), ("gf",), ("rstd", b)], writes=[("tmpf", so)])
            c0 = hf * HALF + b * 512
            sc.op("sp", lambda e, c=c, c0=c0, so=so: e.dma_start(
                out=yv[:, c, c0:c0 + 512], in_=t["tmpf"][so][:]),
                reads=[("tmpf", so)], dma=f"tmpf{so}")


_PROGS = {}


def _prog(name):
    if name not in _PROGS:
        _PROGS[name] = {"A": build_A, "B": build_B, "C": build_C,
                        "CF": lambda: build_C(final=True)}[name]()
    return _PROGS[name]


def _run(name, in_maps):
    res = run_bass_kernel_spmd(_prog(name), in_maps, core_ids=list(range(NCORES)))
    return res.results


def _vec8(g):
    return np.ascontiguousarray(np.asarray(g, np.float32).reshape(8, 128).T)


def _rope_tables():
    d = 64
    inv = (10000.0 ** (-np.arange(0, d, 2, dtype=np.float32) / d)).astype(np.float32)
    ang = np.arange(S, dtype=np.float32)[:, None] * inv[None, :]
    ang = np.concatenate([ang, ang], axis=-1)
    cos = np.cos(ang).astype(np.float32).T
    sin = np.sin(ang).astype(np.float32).T
    sin[:32] *= -1.0
    cos2 = np.concatenate([cos, cos], axis=0)
    sin2 = np.concatenate([sin, sin], axis=0)
    return cos2, sin2


def kernel(x, ffn1_norm, ffn1_w_gate, ffn1_w_up, ffn1_w_down, mix_norm, w_in,
           lambda_q1, lambda_k1, lambda_q2, lambda_k2, subln_gain, pool_w, pool_scale,
           w_out, ffn2_norm, ffn2_w_gate, ffn2_w_up, ffn2_w_down, final_norm):
    f = lambda a: np.ascontiguousarray(np.asarray(a, dtype=np.float32))
    x = f(x)
    xT = [np.ascontiguousarray(x[0, c * T:(c + 1) * T, :].T) for c in range(NCORES)]
    cos2, sin2 = _rope_tables()
    epsd = np.full((128, 1), EPS, np.float32)
    kp = np.arange(128)[:, None, None] + 128 * np.arange(4)[None, :, None]
    qq = np.arange(512)[None, None, :]
    mask = (kp <= qq).astype(np.float32).reshape(128, 4 * 512).astype(ml_dtypes.bfloat16)
    perm = (np.arange(1024).reshape(16, 2, 32)[:, ::-1, :]).reshape(-1)
    WINS = (2, 4, 8, 16)
    y = None
    for l in range(DEPTH):
        wl = f(w_in[l])
        winp = np.ascontiguousarray(wl[:, :1024][:, perm])
        ins = []
        for c in range(NCORES):
            ins.append(dict(x_in=xT[c], g1=_vec8(ffn1_norm[l]), g2=_vec8(mix_norm[l]), epsd=epsd,
                            wg=f(ffn1_w_gate[l]), wu=f(ffn1_w_up[l]), wd=f(ffn1_w_down[l]),
                            win=wl, winp=winp,
                            cosd=np.ascontiguousarray(cos2[:, c * T:(c + 1) * T]),
                            sind=np.ascontiguousarray(sin2[:, c * T:(c + 1) * T])))
        ra = _run("A", ins)
        xT = [ra[c]["x_out"] for c in range(NCORES)]
        qT = np.concatenate([ra[c]["q_out"] for c in range(NCORES)], axis=1)
        kT = np.concatenate([ra[c]["k_out"] for c in range(NCORES)], axis=1)
        v = np.concatenate([ra[c]["v_out"] for c in range(NCORES)], axis=0)
        uT = np.concatenate([ra[c]["u_out"] for c in range(NCORES)], axis=1)
        ins = []
        for c in range(NCORES):
            h, m = c // 2, c % 2
            vh = v[:, h * 128:(h + 1) * 128].reshape(S // 128, 128, 128).transpose(1, 0, 2)
            qc = qT[c * 64:(c + 1) * 64]
            kc = kT[c * 64:(c + 1) * 64].reshape(64, S // 256, 2, 128).transpose(2, 0, 1, 3)
            ins.append(dict(q_in=np.ascontiguousarray(np.concatenate([qc, qc], axis=0)),
                            k_in=np.ascontiguousarray(kc).reshape(128, S // 2),
                            v_in=np.ascontiguousarray(vh).reshape(128, -1),
                            m_in=mask))
        rb = _run("B", ins)
        oT = np.concatenate([rb[c]["o_out"] for c in range(NCORES)], axis=0)
        lam_init = 0.8 - 0.6 * math.exp(-0.3 * l)
        lam4 = np.concatenate([f(lambda_q1[l]), f(lambda_k1[l]), f(lambda_q2[l]), f(lambda_k2[l])])
        lam4 = np.ascontiguousarray(np.broadcast_to(lam4[None, :], (128, 256)))
        lconst = np.ascontiguousarray(np.broadcast_to(
            np.array([lam_init, 1.0 - lam_init], np.float32)[None, :], (128, 2)))
        upad = np.concatenate([np.zeros((512, 16), np.float32), uT], axis=1)
        ins = []
        for c in range(NCORES):
            icnt = np.zeros((128, 64), np.float32)
            for g in range(4):
                pos = c * T + np.arange(16)
                icnt[:, g * 16:(g + 1) * 16] = (1.0 / np.minimum(pos + 1, WINS[g]))[None, :]
            ins.append(dict(x_in=xT[c], o_in=np.ascontiguousarray(oT[:, c * T:(c + 1) * T]),
                            u_in=np.ascontiguousarray(upad[:, c * T:c * T + 16 + T]),
                            icnt=icnt, lam4=lam4, lconst=lconst,
                            sgain=f(subln_gain[l]).reshape(128, 1),
                            pw=f(pool_w[l]).reshape(512, 128),
                            pscale=np.ascontiguousarray(f(pool_scale[l]).reshape(4, 128).T),
                            wout=f(w_out[l]), g3=_vec8(ffn2_norm[l]), gf=_vec8(final_norm), epsd=epsd,
                            wg=f(ffn2_w_gate[l]), wu=f(ffn2_w_up[l]), wd=f(ffn2_w_down[l])))
        rc = _run("CF" if l == DEPTH - 1 else "C", ins)
        xT = [rc[c]["x_out"] for c in range(NCORES)]
        if l == DEPTH - 1:
            y = [rc[c]["y_out"] for c in range(NCORES)]
    out = np.concatenate([yc.T for yc in y], axis=0)[None]
    return np.ascontiguousarray(out.astype(np.float32))
```
